# Optimizing a Trainium2 kernel written in Bass

```python
import math
import jax
import jax.numpy as jnp
from jax import lax
import numpy as np

D_MODEL = 1024
BATCH = 16
SEQ = 2048
DEPTH = 2

HEAD_DIM = 64
ROT_DIM = HEAD_DIM // 4
ROPE_THETA = 500000.0
Q_BLOCK = 128
NORM_EPS = 1e-6
SUBLN_EPS = 1e-5

SB_HEADS = 8
SB_WIDTH = SB_HEADS * HEAD_DIM

DIL_PATTERNS = ((128, 1), (512, 4), (2048, 16))
DIL_GROUPS = 3
DIL_HEADS_PER_GROUP = 4
DIL_OUT_WIDTH = DIL_HEADS_PER_GROUP * HEAD_DIM
DIL_WIDTH = DIL_GROUPS * DIL_OUT_WIDTH

DIFF_HEADS = 4
DIFF_V_DIM = 2 * HEAD_DIM
DIFF_QK_WIDTH = DIFF_HEADS * HEAD_DIM
DIFF_V_WIDTH = DIFF_HEADS * DIFF_V_DIM

N_BRANCHES = 3
GATE_WIDTH = N_BRANCHES * D_MODEL
IN_SPLIT_SIZES = (SB_WIDTH, SB_WIDTH, SB_WIDTH, DIL_WIDTH, DIL_WIDTH, DIL_WIDTH, DIFF_QK_WIDTH, DIFF_QK_WIDTH, DIFF_QK_WIDTH, DIFF_QK_WIDTH, DIFF_V_WIDTH, GATE_WIDTH)
IN_WIDTH = 3 * SB_WIDTH + 3 * DIL_WIDTH + 4 * DIFF_QK_WIDTH + DIFF_V_WIDTH + GATE_WIDTH

D_FF = 2816

kernel_name = 'hybrid_gated_sb_dilated_diff_macaron'


def rmsnorm(x, g, eps=NORM_EPS):
    xf = x.astype(jnp.float32)
    y = xf * lax.rsqrt(jnp.mean(xf * xf, axis=-1, keepdims=True) + eps)
    return (y * g.astype(jnp.float32)).astype(x.dtype)


def swiglu(x, w_in, w_out):
    gate, up = jnp.split(x @ w_in, 2, axis=-1)
    return (jax.nn.silu(gate) * up) @ w_out


def rope_tables(positions, dtype):
    inv_freq = ROPE_THETA ** (-jnp.arange(0, ROT_DIM, 2, dtype=jnp.float32) / ROT_DIM)
    ang = positions.astype(jnp.float32)[..., None] * inv_freq
    return jnp.cos(ang)[:, :, None, :].astype(dtype), jnp.sin(ang)[:, :, None, :].astype(dtype)


def apply_partial_rope(t, cos, sin):
    half = ROT_DIM // 2
    t1 = t[..., :half]
    t2 = t[..., half:ROT_DIM]
    return jnp.concatenate([t1 * cos - t2 * sin, t2 * cos + t1 * sin, t[..., ROT_DIM:]], axis=-1)


def stick_breaking_attention(q, k, v):
    B, S, H, hd = q.shape
    nb = S // Q_BLOCK
    scale = hd ** -0.5
    qb = q.reshape(B, nb, Q_BLOCK, H, hd).transpose(1, 0, 3, 2, 4)
    kh = k.transpose(0, 2, 1, 3)
    vh = v.transpose(0, 2, 1, 3)
    key_pos = jnp.arange(S)

    def block(args):
        q_blk, start = args
        q_pos = start + jnp.arange(Q_BLOCK)
        strict = key_pos[None, :] < q_pos[:, None]
        z = jnp.einsum('bhqd,bhkd->bhqk', q_blk, kh).astype(jnp.float32) * scale
        log_beta = jax.nn.log_sigmoid(z)
        log_1m_beta = jnp.where(strict, log_beta - z, 0.0)
        between = lax.cumsum(log_1m_beta, axis=3, reverse=True) - log_1m_beta
        w = jnp.where(strict, jnp.exp(log_beta + between), 0.0)
        return jnp.einsum('bhqk,bhkd->bhqd', w.astype(v.dtype), vh)

    out = lax.map(block, (qb, jnp.arange(nb) * Q_BLOCK))
    return out.transpose(1, 0, 3, 2, 4).reshape(B, S, H * hd)


def _dilated_group(q, k, v, window, dil):
    B, S, H, hd = q.shape
    span = window // dil
    L = S // dil
    Lp = -(-L // span) * span
    nb = Lp // span
    scale = hd ** -0.5

    def to_sub(t):
        t = t.reshape(B, L, dil, H, hd).transpose(0, 2, 3, 1, 4)
        t = jnp.pad(t, ((0, 0), (0, 0), (0, 0), (0, Lp - L), (0, 0)))
        return t.reshape(B, dil, H, nb, span, hd)

    def with_prev(t):
        prev = jnp.pad(t, ((0, 0), (0, 0), (0, 0), (1, 0), (0, 0), (0, 0)))[:, :, :, :-1]
        return jnp.concatenate([prev, t], axis=4)

    qs = to_sub(q)
    kband = with_prev(to_sub(k))
    vband = with_prev(to_sub(v))
    s = jnp.einsum('bchnqd,bchnkd->bchnqk', qs, kband).astype(jnp.float32) * scale
    qi = jnp.arange(span)[:, None] + span
    kj = jnp.arange(2 * span)[None, :]
    dist = qi - kj
    blk = jnp.arange(nb)[:, None, None]
    valid = (dist >= 0) & (dist <= span) & ((blk > 0) | (kj >= span))
    s = jnp.where(valid, s, -jnp.inf)
    lse = jax.nn.logsumexp(s, axis=-1)
    p = jnp.exp(s - lse[..., None])
    o = jnp.einsum('bchnqk,bchnkd->bchnqd', p.astype(v.dtype), vband)
    o = o.reshape(B, dil, H, Lp, hd)[:, :, :, :L].transpose(0, 3, 1, 2, 4).reshape(B, S, H, hd)
    lse = lse.reshape(B, dil, H, Lp)[:, :, :, :L].transpose(0, 3, 1, 2).reshape(B, S, H)
    return o, lse


def dilated_window_attention(q, k, v):
    B, S, _, hd = q.shape
    outs = []
    lses = []
    for g, (window, dil) in enumerate(DIL_PATTERNS):
        sl = slice(g * DIL_HEADS_PER_GROUP, (g + 1) * DIL_HEADS_PER_GROUP)
        o, lse = _dilated_group(q[:, :, sl], k[:, :, sl], v[:, :, sl], window, dil)
        outs.append(o)
        lses.append(lse)
    wts = jax.nn.softmax(jnp.stack(lses, axis=0), axis=0)
    out = jnp.sum(wts[..., None].astype(q.dtype) * jnp.stack(outs, axis=0), axis=0)
    return out.reshape(B, S, DIL_OUT_WIDTH)


def diff_attention(q1, q2, k1, k2, v, lam):
    B, S, H, hd = q1.shape
    nb = S // Q_BLOCK
    scale = hd ** -0.5

    def to_blocks(t):
        return t.reshape(B, nb, Q_BLOCK, H, hd).transpose(1, 0, 3, 2, 4)

    k1h = k1.transpose(0, 2, 1, 3)
    k2h = k2.transpose(0, 2, 1, 3)
    vh = v.transpose(0, 2, 1, 3)
    key_pos = jnp.arange(S)

    def block(args):
        q1b, q2b, start = args
        q_pos = start + jnp.arange(Q_BLOCK)
        causal = key_pos[None, :] <= q_pos[:, None]
        s1 = jnp.einsum('bhqd,bhkd->bhqk', q1b, k1h).astype(jnp.float32) * scale
        s2 = jnp.einsum('bhqd,bhkd->bhqk', q2b, k2h).astype(jnp.float32) * scale
        p1 = jax.nn.softmax(jnp.where(causal, s1, -jnp.inf), axis=-1)
        p2 = jax.nn.softmax(jnp.where(causal, s2, -jnp.inf), axis=-1)
        w = (p1 - lam * p2).astype(v.dtype)
        return jnp.einsum('bhqk,bhkd->bhqd', w, vh)

    out = lax.map(block, (to_blocks(q1), to_blocks(q2), jnp.arange(nb) * Q_BLOCK))
    return out.transpose(1, 0, 3, 2, 4).reshape(B, S, H, 2 * hd)


def setup_inputs(seed: int = 0) -> dict:
    key = jax.random.key(seed)
    ks = jax.random.split(key, 24)
    f32 = jnp.float32

    def dense(k, fan_in, fan_out):
        return jax.random.normal(k, (DEPTH, fan_in, fan_out), f32) * fan_in ** -0.5

    def gain(k, shape):
        return 1.0 + 0.01 * jax.random.normal(k, shape, f32)

    x = jax.random.normal(ks[0], (BATCH, SEQ, D_MODEL), f32)
    start = jax.random.randint(ks[1], (BATCH, 1), 0, 4096, dtype=jnp.int32)
    positions = start + jnp.arange(SEQ, dtype=jnp.int32)[None, :]
    return {
        'x': x,
        'positions': positions,
        'ffn1_norm': gain(ks[2], (DEPTH, D_MODEL)),
        'ffn1_w_in': dense(ks[3], D_MODEL, 2 * D_FF),
        'ffn1_w_out': dense(ks[4], D_FF, D_MODEL),
        'mix_norm': gain(ks[5], (DEPTH, D_MODEL)),
        'w_in': dense(ks[6], D_MODEL, IN_WIDTH),
        'b_gate': 0.01 * jax.random.normal(ks[7], (DEPTH, GATE_WIDTH), f32),
        'lam_q1': 0.1 * jax.random.normal(ks[8], (DEPTH, HEAD_DIM), f32),
        'lam_k1': 0.1 * jax.random.normal(ks[9], (DEPTH, HEAD_DIM), f32),
        'lam_q2': 0.1 * jax.random.normal(ks[10], (DEPTH, HEAD_DIM), f32),
        'lam_k2': 0.1 * jax.random.normal(ks[11], (DEPTH, HEAD_DIM), f32),
        'diff_subln': gain(ks[12], (DEPTH, DIFF_V_DIM)),
        'w_up_a': dense(ks[13], SB_WIDTH, D_MODEL),
        'w_up_b': dense(ks[14], DIL_OUT_WIDTH, D_MODEL),
        'w_up_c': dense(ks[15], DIFF_V_WIDTH, D_MODEL),
        'w_out': dense(ks[16], D_MODEL, D_MODEL),
        'ffn2_norm': gain(ks[17], (DEPTH, D_MODEL)),
        'ffn2_w_in': dense(ks[18], D_MODEL, 2 * D_FF),
        'ffn2_w_out': dense(ks[19], D_FF, D_MODEL),
        'final_norm': gain(ks[20], (D_MODEL,)),
    }


def reference(x, positions, ffn1_norm, ffn1_w_in, ffn1_w_out, mix_norm, w_in, b_gate, lam_q1, lam_k1, lam_q2, lam_k2, diff_subln, w_up_a, w_up_b, w_up_c, w_out, ffn2_norm, ffn2_w_in, ffn2_w_out, final_norm):
    B, S, D = x.shape
    cos, sin = rope_tables(positions, x.dtype)
    split_at = np.cumsum(IN_SPLIT_SIZES)[:-1].tolist()
    for l in range(DEPTH):
        x = x + 0.5 * swiglu(rmsnorm(x, ffn1_norm[l]), ffn1_w_in[l], ffn1_w_out[l])

        h = rmsnorm(x, mix_norm[l])
        (qa, ka, va, qb, kb, vb, q1, q2, k1, k2, vc, gate_pre) = jnp.split(h @ w_in[l], split_at, axis=-1)

        y_a = stick_breaking_attention(qa.reshape(B, S, SB_HEADS, HEAD_DIM), ka.reshape(B, S, SB_HEADS, HEAD_DIM), va.reshape(B, S, SB_HEADS, HEAD_DIM))

        hb = DIL_GROUPS * DIL_HEADS_PER_GROUP
        y_b = dilated_window_attention(apply_partial_rope(qb.reshape(B, S, hb, HEAD_DIM), cos, sin), apply_partial_rope(kb.reshape(B, S, hb, HEAD_DIM), cos, sin), vb.reshape(B, S, hb, HEAD_DIM))

        lam_init = 0.8 - 0.6 * math.exp(-0.3 * l)
        lam = (jnp.exp(jnp.sum(lam_q1[l].astype(jnp.float32) * lam_k1[l].astype(jnp.float32))) - jnp.exp(jnp.sum(lam_q2[l].astype(jnp.float32) * lam_k2[l].astype(jnp.float32))) + lam_init)
        rs = lambda t: apply_partial_rope(t.reshape(B, S, DIFF_HEADS, HEAD_DIM), cos, sin)
        o_c = diff_attention(rs(q1), rs(q2), rs(k1), rs(k2), vc.reshape(B, S, DIFF_HEADS, DIFF_V_DIM), lam)
        y_c = (rmsnorm(o_c, diff_subln[l], SUBLN_EPS) * (1.0 - lam_init)).reshape(B, S, DIFF_V_WIDTH)

        gates = jax.nn.sigmoid(gate_pre + b_gate[l]).reshape(B, S, N_BRANCHES, D)
        merged = (gates[:, :, 0] * (y_a @ w_up_a[l]) + gates[:, :, 1] * (y_b @ w_up_b[l]) + gates[:, :, 2] * (y_c @ w_up_c[l]))
        x = x + merged @ w_out[l]

        x = x + 0.5 * swiglu(rmsnorm(x, ffn2_norm[l]), ffn2_w_in[l], ffn2_w_out[l])
    return rmsnorm(x, final_norm)
```

```python
import contextlib
import math
import numpy as np
import concourse.bass as bass
import concourse.mybir as mybir
from concourse.bass_utils import run_bass_kernel_spmd

F32 = mybir.dt.float32
BF16 = mybir.dt.bfloat16
I32 = mybir.dt.int32
AF = mybir.ActivationFunctionType
ALU = mybir.AluOpType
AX = mybir.AxisListType

D = 1024
S = 2048
DFF = 2816
NFC = DFF // 128
NCORES = 8
COMPUTE = ("pe", "act", "dve", "pool")
NVEC = 57 + 256
NCONST = 128 * 5 + 512


class _Rec:
    def __init__(self):
        self.call = None

    def __getattr__(self, name):
        def f(*a, **k):
            self.call = (name, a, k)
        return f


class Sched:
    def __init__(self, nc, n_dma_sems=24):
        self.nc = nc
        self.ops = []
        self.res = {}
        self.n_dma_sems = n_dma_sems
        self.last = {}
        self.dmas = []
        self.pending = {}

    def _conf(self, key):
        name, sub = key
        d = self.res.setdefault(name, {})
        if sub is None:
            return list(d.values())
        out = []
        if None in d:
            out.append(d[None])
        if sub in d:
            out.append(d[sub])
        return out

    def barrier(self, engs=("pe", "act", "dve", "sp")):
        deps = set(v for k, v in self.last.items() if k in engs)
        deps |= set(d for d in self.dmas if self.ops[d]["eng"] in engs)
        self.dmas = [d for d in self.dmas if self.ops[d]["eng"] not in engs]
        for e in engs:
            self.pending[e] = set(self.pending.get(e, set())) | deps

    def op(self, eng, fn, reads=(), writes=(), dma=False):
        oid = len(self.ops)
        raw, other = set(), set()
        for key in reads:
            for st in self._conf(key):
                if st[0] is not None:
                    raw.add(st[0])
        for key in writes:
            for st in self._conf(key):
                if st[0] is not None:
                    other.add(st[0])
                other.update(st[1])
        for key in reads:
            d = self.res.setdefault(key[0], {})
            if key[1] not in d:
                d[key[1]] = [None, []]
            for st in self._conf(key):
                st[1].append(oid)
        for key in writes:
            d = self.res.setdefault(key[0], {})
            if key[1] not in d:
                d[key[1]] = [None, []]
            for st in self._conf(key):
                st[0] = oid
                st[1] = []
        if eng in self.pending:
            other |= self.pending.pop(eng)
        other -= raw
        raw.discard(oid)
        other.discard(oid)
        rec = _Rec()
        fn(rec)
        cname, ca, ck = rec.call
        self.ops.append(dict(eng=eng, fn=(lambda e: getattr(e, cname)(*ca, **ck)), raw=raw, other=other, dma=dma))
        self.last[eng] = oid
        if dma:
            self.dmas.append(oid)
        return oid

    def defer(self, eng, fn, reads=(), writes=(), dma=False):
        rec = _Rec()
        fn(rec)
        cname, ca, ck = rec.call
        return lambda: self.op(eng, (lambda e: getattr(e, cname)(*ca, **ck)), reads, writes, dma)

    def emit(self):
        nc = self.nc
        ops = self.ops
        need = []
        for o in ops:
            w = set()
            for d in o["raw"]:
                od = ops[d]
                if od["dma"] or od["eng"] != o["eng"] or o["dma"] or o["eng"] != "pe":
                    w.add(d)
            for d in o["other"]:
                od = ops[d]
                if od["dma"] or od["eng"] != o["eng"] or o["dma"] or o["eng"] != "pe":
                    w.add(d)
            need.append(w)
        signal = [False] * len(ops)
        for w in need:
            for d in w:
                signal[d] = True
        sig_idx = [0] * len(ops)
        cnt = {e: 0 for e in COMPUTE}
        dma_rr, dma_sem, slot_val = {}, [None] * len(ops), {}
        for i, o in enumerate(ops):
            if o["dma"]:
                q = o["eng"]
                k = dma_rr.get(q, 0)
                dma_rr[q] = k + 1
                slot = (q, k % self.n_dma_sems)
                v = slot_val.get(slot, 0) + 16
                slot_val[slot] = v
                dma_sem[i] = (slot, v)
            elif signal[i]:
                cnt[o["eng"]] += 1
                sig_idx[i] = cnt[o["eng"]]
        with contextlib.ExitStack() as es:
            sems = {e: es.enter_context(nc.semaphore("s_" + e)) for e in COMPUTE}
            dsems = {}
            for q in dma_rr:
                for k in range(min(self.n_dma_sems, dma_rr[q])):
                    dsems[(q, k)] = es.enter_context(nc.semaphore("d_%s_%d" % (q, k)))
            block = es.enter_context(nc.Block())

            def run(engname, e):
                waited = {}
                for i, o in enumerate(ops):
                    if o["eng"] != engname:
                        continue
                    tgt = {}
                    for d in need[i]:
                        od = ops[d]
                        if od["dma"]:
                            slot, v = dma_sem[d]
                            key = ("d", slot)
                        else:
                            key = ("c", od["eng"])
                            v = sig_idx[d]
                        if v > tgt.get(key, 0):
                            tgt[key] = v
                    if o["dma"]:
                        slot, v = dma_sem[i]
                        if v > 16 and v - 16 > tgt.get(("d", slot), 0):
                            tgt[("d", slot)] = v - 16
                    for key, v in tgt.items():
                        if waited.get(key, 0) >= v:
                            continue
                        waited[key] = v
                        e.wait_ge(dsems[key[1]] if key[0] == "d" else sems[key[1]], v)
                    ins = o["fn"](e)
                    if o["dma"]:
                        ins.then_inc(dsems[dma_sem[i][0]], 16)
                    elif signal[i]:
                        ins.then_inc(sems[engname], 1)
                for slot, v in slot_val.items():
                    if slot[0] == engname and waited.get(("d", slot), 0) < v:
                        e.wait_ge(dsems[slot], v)

            block.tensor(lambda e: run("pe", e))
            block.scalar(lambda e: run("act", e))
            block.vector(lambda e: run("dve", e))
            block.gpsimd(lambda e: run("pool", e))
            block.sync(lambda e: run("sp", e))


def _img(W):
    nk = W.shape[0] // 128
    C = W.shape[1]
    return np.ascontiguousarray(W.reshape(nk, 128, C).transpose(1, 0, 2).reshape(128, nk * C))


def _swapcols(base):
    cols = []
    for hh in range(2):
        b = base + hh * 64
        cols += list(range(b + 8, b + 16)) + list(range(b, b + 8)) + list(range(b + 16, b + 64))
    return cols


def _rng(a, n=128):
    return list(range(a, a + n))


QA, KA, VA = 0, 512, 1024
QB, KB, VB = 1536, 2304, 3072
Q1, Q2, K1, K2 = 3840, 4096, 4352, 4608
VC = 4864
GATE = 5376


def _load_plan():
    plan = []

    def ffn(w):
        for g in range(6):
            plan.append(("f%d_in%d" % (w, g), "ffn_in", (w, g)))
        for q in range(4):
            plan.append(("f%d_out%d" % (w, q), "ffn_out", (w, q)))

    ffn(1)
    plan.append(("a_v", "cols", _rng(VA, 512)))
    cols = []
    for hp in range(4):
        cols += _rng(QA + hp * 128) + _rng(KA + hp * 128)
    plan.append(("a_qk", "cols", cols))
    plan.append(("g_a", "cols", _rng(GATE, 1024)))
    plan.append(("up_a", "up", "a"))
    plan.append(("wo_a", "wo", None))
    plan.append(("b_v", "cols", _rng(VB, 768)))
    for j in range(3):
        cols = []
        for c in (2 * j, 2 * j + 1):
            cols += _rng(QB + c * 128) + _swapcols(QB + c * 128) + _rng(KB + c * 128) + _swapcols(KB + c * 128)
        plan.append(("b_qk%d" % j, "cols", cols))
    plan.append(("g_b", "cols", _rng(GATE + 1024, 1024)))
    plan.append(("up_b", "up", "b"))
    plan.append(("c_v", "cols", _rng(VC, 512)))
    for hp in range(2):
        cols = []
        for base in (Q1, Q2, K1, K2):
            cols += _rng(base + hp * 128) + _swapcols(base + hp * 128)
        plan.append(("c_qk%d" % hp, "cols", cols))
    plan.append(("g_c", "cols", _rng(GATE + 2048, 1024)))
    plan.append(("up_c", "up", "c"))
    ffn(2)
    return plan


def _ffn_in_cols(g):
    ncg = min(4, NFC - 4 * g)
    return ncg, _rng(512 * g, 128 * ncg) + _rng(DFF + 512 * g, 128 * ncg)


def _plan_sizes():
    sizes = {}
    off = 0
    for name, kind, spec in _load_plan():
        if kind == "ffn_in":
            ncg, cols = _ffn_in_cols(spec[1])
            E = 8 * len(cols)
        elif kind == "ffn_out":
            E = NFC * 256
        elif kind == "cols":
            E = 8 * len(spec)
        elif kind == "up":
            E = {"a": 4, "b": 2, "c": 4}[spec] * 1024
        else:
            E = 8192
        sizes[name] = (off, E)
        off += E
    return sizes, off


def _build_wimg(inp, l):
    sizes, tot = _plan_sizes()
    out = np.empty((128, tot), np.float32)
    for name, kind, spec in _load_plan():
        off, E = sizes[name]
        if kind == "ffn_in":
            W = {1: inp["ffn1_w_in"], 2: inp["ffn2_w_in"]}[spec[0]][l]
            _, cols = _ffn_in_cols(spec[1])
            im = _img(W[:, cols])
        elif kind == "ffn_out":
            W = {1: inp["ffn1_w_out"], 2: inp["ffn2_w_out"]}[spec[0]][l]
            im = _img(W[:, spec[1] * 256:(spec[1] + 1) * 256])
        elif kind == "cols":
            im = _img(inp["w_in"][l][:, spec])
        elif kind == "up":
            im = _img({"a": inp["w_up_a"], "b": inp["w_up_b"], "c": inp["w_up_c"]}[spec][l])
        else:
            im = _img(inp["w_out"][l])
        assert im.shape == (128, E), (name, im.shape, E)
        out[:, off:off + E] = im
    return out


def _build_vecs(inp, l):
    v = np.zeros((128, NVEC), np.float32)
    v[:, 0:8] = inp["ffn1_norm"][l].reshape(8, 128).T
    v[:, 8:16] = inp["mix_norm"][l].reshape(8, 128).T
    v[:, 16:24] = inp["ffn2_norm"][l].reshape(8, 128).T
    v[:, 24:32] = inp["final_norm"].reshape(8, 128).T
    v[:, 32:56] = inp["b_gate"][l].reshape(24, 128).T
    v[:, 56] = inp["diff_subln"][l]
    for i, lv in enumerate((inp["lam_q1"], inp["lam_k1"], inp["lam_q2"], inp["lam_k2"])):
        v[:, 57 + 64 * i:57 + 64 * (i + 1)] = np.broadcast_to(lv[l][None, :], (128, 64))
    return v


def _build_consts():
    c = np.zeros((128, NCONST + 2), np.float32)
    r = np.arange(128)[:, None]
    col = np.arange(128)[None, :]
    c[:, 0:128] = 1.0
    c[:, 128:256] = np.where(r >= col, -1.0, 0.0)
    c[:, 256:384] = -1.0
    c[:, 384:512] = np.where(col > r, 1.0, 0.0)
    c[:, 512:640] = np.where(col >= r, 1.0, 0.0)
    col2 = np.arange(256)[None, :]
    c[:, 640:896] = np.where((col2 >= r) & (col2 <= r + 128), 1.0, 0.0)
    c[:, 896:1152] = c[:, 640:896]
    inv_freq = (500000.0 ** (-np.arange(0, 16, 2, dtype=np.float32) / np.float32(16))).astype(np.float32)
    for p in range(128):
        q = p % 64
        if q < 16:
            c[p, NCONST] = inv_freq[q % 8]
            c[p, NCONST + 1] = -1.0 if q < 8 else 1.0
    return c


def build_program(NSEQ=2, DEPTH=2, stop_after=None):
    T = NSEQ * S
    nc = bass.Bass("TRN2", target_bir_lowering=False, dynamic_dma_scratch_size=8192)
    sizes, WTOT = _plan_sizes()
    xin = nc.dram_tensor("xT", [D, T], F32, kind="ExternalInput")
    posd = nc.dram_tensor("pos", [NSEQ, 128, S], I32, kind="ExternalInput")
    wimg = [nc.dram_tensor("wimg%d" % l, [128, WTOT], F32, kind="ExternalInput") for l in range(DEPTH)]
    vecd = [nc.dram_tensor("vecs%d" % l, [128, NVEC], F32, kind="ExternalInput") for l in range(DEPTH)]
    cstd = nc.dram_tensor("consts", [128, NCONST + 2], F32, kind="ExternalInput")
    outd = nc.dram_tensor("outT", [D, T], F32, kind="ExternalOutput")
    xa = nc.dram_tensor("xa", [D, T], F32, kind="Internal")
    hmd = nc.dram_tensor("hmd", [D, T], BF16, kind="Internal")

    sch = Sched(nc)
    op = sch.op
    top = contextlib.ExitStack()
    with top:
        uniq = {"n": 0}

        def sb(es, name, shape, dt):
            uniq["n"] += 1
            return es.enter_context(nc.sbuf_tensor("%s_%d" % (name, uniq["n"]), shape, dt))

        WB = [sb(top, "wb%d" % i, [128, 8192], BF16) for i in range(4)]
        pst = top.enter_context(nc.psum_tensor("pst", [128, 4096], F32))
        PS = [pst[:, i * 512:(i + 1) * 512] for i in range(8)]
        cb = sb(top, "cb", [128, NCONST], BF16)
        cf = sb(top, "cf", [128, 2], F32)
        vec = [sb(top, "vec%d" % l, [128, NVEC], F32) for l in range(DEPTH)]
        ones = cb[:, 0:128]
        trineg = cb[:, 128:256]
        negones = cb[:, 256:384]
        mstrict = cb[:, 384:512]
        mincl = cb[:, 512:640]
        mband = cb[:, 640:896]
        mband2 = cb[:, 640:1152].rearrange("p (h c) -> p h c", h=2)

        op("pool", lambda e: e.dma_start(out=cb[:], in_=cstd.ap()[:, 0:NCONST]), writes=[("cb", None)], dma=True)
        op("sp", lambda e: e.dma_start(out=cf[:], in_=cstd.ap()[:, NCONST:NCONST + 2]), writes=[("cf", None)], dma=True)
        for l in range(DEPTH):
            op("sp", lambda e, l=l: e.dma_start(out=vec[l][:], in_=vecd[l].ap()), writes=[("vec", l)], dma=True)

        wstate = {"n": 0}

        def wload(l, name):
            off, E = sizes[name]
            b = wstate["n"] % 4
            wstate["n"] += 1
            op("pool", lambda e: e.dma_start(out=WB[b][:, 0:E], in_=wimg[l].ap()[:, off:off + E]),
               writes=[("wb", b)], dma=True)
            return WB[b], ("wb", b)

        psrr = {"n": 0}

        def psnext():
            b = psrr["n"] % 8
            psrr["n"] += 1
            return PS[b], ("ps", b)

        def rms_rstd(es_name, src_chunks, src_keys, rstd_ap, rstd_key, sqt, width, inv_n, eps, lnt):
            ps, pk = psnext()
            n = len(src_chunks)
            for i, (a, k) in enumerate(zip(src_chunks, src_keys)):
                sq = sqt[i % 2]
                sk = ("sq" + es_name, i % 2)
                if i % 2 == 0:
                    op("act", lambda e, a=a, sq=sq: e.activation(out=sq[:, 0:width], in_=a, func=AF.Square),
                       reads=[k], writes=[sk])
                else:
                    op("dve", lambda e, a=a, sq=sq: e.tensor_tensor(out=sq[:, 0:width], in0=a, in1=a, op=ALU.mult),
                       reads=[k], writes=[sk])
                op("pe", lambda e, sq=sq, i=i: e.matmul(ps[:, 0:width], lhsT=ones, rhs=sq[:, 0:width],
                                                        start=(i == 0), stop=(i == n - 1)),
                   reads=[sk, ("cb", None)], writes=[pk])
            lk = ("ln" + es_name, None)
            op("act", lambda e: e.activation(out=lnt[:, 0:width], in_=ps[:, 0:width], func=AF.Ln, scale=inv_n, bias=eps),
               reads=[pk], writes=[lk])
            op("act", lambda e: e.activation(out=rstd_ap, in_=lnt[:, 0:width], func=AF.Exp, scale=-0.5),
               reads=[lk], writes=[rstd_key])

        def ffn_phase(l, w, src, final=False, emit_hm=False):
            gcol = 0 if w == 1 else 16
            TT = 1024
            NT = T // TT
            with contextlib.ExitStack() as es:
                xt = sb(es, "xt", [128, 8, TT], F32)
                xst = sb(es, "xst", [128, 8, 512], F32)
                hts = [sb(es, "ht%d" % i, [128, 8, TT], BF16) for i in range(2)]
                act = sb(es, "act", [128, NFC, TT], BF16)
                sqt = [sb(es, "sq%d" % i, [128, 512], BF16) for i in range(8)]
                lnt = sb(es, "lnt", [128, 512], F32)
                rstd = sb(es, "rstd", [128, 512], F32)
                sil = [sb(es, "sil%d" % i, [128, 512], F32) for i in range(2)]
                if emit_hm:
                    hst = sb(es, "hst", [128, 4, 512], BF16)

                def pre_load(tt, h):
                    c0 = tt * TT + h * 512
                    for kc in range(8):
                        op("sp", lambda e, kc=kc: e.dma_start(out=xst[:, kc, :], in_=src.ap()[kc * 128:(kc + 1) * 128, c0:c0 + 512]),
                           reads=[("xdram", (kc, tt))], writes=[("xst", kc)], dma=True)
                    for kc in range(8):
                        if kc % 2 == 0:
                            op("act", lambda e, kc=kc: e.activation(out=sqt[kc][:], in_=xst[:, kc, :], func=AF.Square),
                               reads=[("xst", kc)], writes=[("sq", kc)])
                        else:
                            op("dve", lambda e, kc=kc: e.tensor_tensor(out=sqt[kc][:], in0=xst[:, kc, :], in1=xst[:, kc, :], op=ALU.mult),
                               reads=[("xst", kc)], writes=[("sq", kc)])

                def pre_norm(tt, h):
                    ps, pk = psnext()
                    for kc in range(8):
                        op("pe", lambda e, kc=kc: e.matmul(ps[:], lhsT=ones, rhs=sqt[kc][:], start=(kc == 0), stop=(kc == 7)),
                           reads=[("sq", kc), ("cb", None)], writes=[pk])
                    op("act", lambda e: e.activation(out=lnt[:], in_=ps[:], func=AF.Ln, scale=1.0 / D, bias=1e-6),
                       reads=[pk], writes=[("lnt", None)])
                    op("act", lambda e: e.activation(out=rstd[:], in_=lnt[:], func=AF.Exp, scale=-0.5),
                       reads=[("lnt", None)], writes=[("rstd", None)])
                    hb_ = hts[tt % 2]
                    for kc in range(8):
                        op("dve", lambda e, kc=kc: e.scalar_tensor_tensor(
                            out=hb_[:, kc, h * 512:(h + 1) * 512], in0=xst[:, kc, :],
                            scalar=vec[l][:, gcol + kc:gcol + kc + 1], in1=rstd[:], op0=ALU.mult, op1=ALU.mult),
                           reads=[("xst", kc), ("rstd", None), ("vec", l)], writes=[("ht", (tt % 2, kc, h))])

                for h in range(2):
                    pre_load(0, h)
                    pre_norm(0, h)
                pending = []
                for tt in range(NT):
                    c0 = tt * TT
                    ht = hts[tt % 2]
                    si = 0
                    for g in range(6):
                        if g == 1 and pending:
                            pending.pop(0)()
                        if g == 2:
                            while pending:
                                pending.pop(0)()
                            for kc in range(8):
                                op("sp", lambda e, kc=kc, c0=c0: e.dma_start(out=xt[:, kc, :], in_=src.ap()[kc * 128:(kc + 1) * 128, c0:c0 + TT]),
                                   reads=[("xdram", (kc, tt))], writes=[("xt", kc)], dma=True)
                        wb, wk = wload(l, "f%d_in%d" % (w, g))
                        ncg = min(4, NFC - 4 * g)
                        C = 2 * 128 * ncg
                        for j in range(ncg):
                            fc = 4 * g + j
                            for h in range(2):
                                pg, pgk = psnext()
                                pu, puk = psnext()
                                for kc in range(8):
                                    op("pe", lambda e, kc=kc: e.matmul(
                                        pg[:], lhsT=wb[:, kc * C + j * 128:kc * C + (j + 1) * 128],
                                        rhs=ht[:, kc, h * 512:(h + 1) * 512], start=(kc == 0), stop=(kc == 7)),
                                       reads=[wk, ("ht", (tt % 2, kc, h))], writes=[pgk])
                                for kc in range(8):
                                    op("pe", lambda e, kc=kc: e.matmul(
                                        pu[:], lhsT=wb[:, kc * C + (ncg + j) * 128:kc * C + (ncg + j + 1) * 128],
                                        rhs=ht[:, kc, h * 512:(h + 1) * 512], start=(kc == 0), stop=(kc == 7)),
                                       reads=[wk, ("ht", (tt % 2, kc, h))], writes=[puk])
                                st = sil[si % 2]
                                stk = ("sil", si % 2)
                                si += 1
                                op("act", lambda e: e.activation(out=st[:], in_=pg[:], func=AF.Silu), reads=[pgk], writes=[stk])
                                op("dve", lambda e: e.tensor_tensor(out=act[:, fc, h * 512:(h + 1) * 512], in0=pu[:], in1=st[:], op=ALU.mult),
                                   reads=[puk, stk], writes=[("act", (fc, h))])
                    if tt + 1 < NT:
                        pre_load(tt + 1, 0)
                    for q in range(4):
                        wb, wk = wload(l, "f%d_out%d" % (w, q))
                        for jj in range(2):
                            dc = 2 * q + jj
                            for h in range(2):
                                py, pyk = psnext()
                                for fc in range(NFC):
                                    op("pe", lambda e, fc=fc: e.matmul(
                                        py[:], lhsT=wb[:, fc * 256 + jj * 128:fc * 256 + (jj + 1) * 128],
                                        rhs=act[:, fc, h * 512:(h + 1) * 512], start=(fc == 0), stop=(fc == NFC - 1)),
                                       reads=[wk, ("act", (fc, h))], writes=[pyk])
                                op("dve", lambda e: e.scalar_tensor_tensor(
                                    out=xt[:, dc, h * 512:(h + 1) * 512], in0=py[:], scalar=0.5,
                                    in1=xt[:, dc, h * 512:(h + 1) * 512], op0=ALU.mult, op1=ALU.add),
                                   reads=[pyk, ("xt", dc)], writes=[("xt", dc)])
                        if tt + 1 < NT:
                            if q == 0:
                                pre_norm(tt + 1, 0)
                                pre_load(tt + 1, 1)
                            elif q == 1:
                                pre_norm(tt + 1, 1)
                    if not final:
                        for kc in range(8):
                            op("sp", lambda e, kc=kc, c0=c0: e.dma_start(out=xa.ap()[kc * 128:(kc + 1) * 128, c0:c0 + TT], in_=xt[:, kc, :]),
                               reads=[("xt", kc)], writes=[("xdram", (kc, tt))], dma=True)
                        if emit_hm:
                            def epilogue(h, tt=tt, c0=c0):
                                if True:
                                    ps, pk = psnext()
                                    for kc in range(8):
                                        if kc % 2 == 0:
                                            op("act", lambda e, kc=kc: e.activation(out=sqt[kc][:], in_=xt[:, kc, h * 512:(h + 1) * 512], func=AF.Square),
                                               reads=[("xt", kc)], writes=[("sq", kc)])
                                        else:
                                            op("dve", lambda e, kc=kc: e.tensor_tensor(out=sqt[kc][:], in0=xt[:, kc, h * 512:(h + 1) * 512],
                                                                                       in1=xt[:, kc, h * 512:(h + 1) * 512], op=ALU.mult),
                                               reads=[("xt", kc)], writes=[("sq", kc)])
                                        op("pe", lambda e, kc=kc: e.matmul(ps[:], lhsT=ones, rhs=sqt[kc][:], start=(kc == 0), stop=(kc == 7)),
                                           reads=[("sq", kc), ("cb", None)], writes=[pk])
                                    op("act", lambda e: e.activation(out=sil[0][:], in_=ps[:], func=AF.Ln, scale=1.0 / D, bias=1e-6),
                                       reads=[pk], writes=[("sil", 0)])
                                    op("act", lambda e: e.activation(out=sil[1][:], in_=sil[0][:], func=AF.Exp, scale=-0.5),
                                       reads=[("sil", 0)], writes=[("sil", 1)])
                                    for kc in range(8):
                                        op("dve", lambda e, kc=kc: e.scalar_tensor_tensor(
                                            out=hst[:, kc % 4, :], in0=xt[:, kc, h * 512:(h + 1) * 512],
                                            scalar=vec[l][:, 8 + kc:9 + kc], in1=sil[1][:], op0=ALU.mult, op1=ALU.mult),
                                           reads=[("xt", kc), ("sil", 1), ("vec", l)], writes=[("hst", kc % 4)])
                                        op("sp", lambda e, kc=kc: e.dma_start(
                                            out=hmd.ap()[kc * 128:(kc + 1) * 128, c0 + h * 512:c0 + (h + 1) * 512], in_=hst[:, kc % 4, :]),
                                           reads=[("hst", kc % 4)], writes=[("hdram", (kc, tt, h))], dma=True)
                            pending.append(lambda ep=epilogue: ep(0))
                            pending.append(lambda ep=epilogue: ep(1))
                    else:
                        for h in range(2):
                            ps, pk = psnext()
                            for kc in range(8):
                                op("dve", lambda e, kc=kc: e.tensor_tensor(out=sqt[kc][:], in0=xt[:, kc, h * 512:(h + 1) * 512],
                                                                           in1=xt[:, kc, h * 512:(h + 1) * 512], op=ALU.mult),
                                   reads=[("xt", kc)], writes=[("sq", kc)])
                                op("pe", lambda e, kc=kc: e.matmul(ps[:], lhsT=ones, rhs=sqt[kc][:], start=(kc == 0), stop=(kc == 7)),
                                   reads=[("sq", kc), ("cb", None)], writes=[pk])
                            op("act", lambda e: e.activation(out=sil[0][:], in_=ps[:], func=AF.Ln, scale=1.0 / D, bias=1e-6),
                               reads=[pk], writes=[("sil", 0)])
                            op("act", lambda e: e.activation(out=sil[1][:], in_=sil[0][:], func=AF.Exp, scale=-0.5),
                               reads=[("sil", 0)], writes=[("sil", 1)])
                            for kc in range(8):
                                op("dve", lambda e, kc=kc: e.scalar_tensor_tensor(
                                    out=xt[:, kc, h * 512:(h + 1) * 512], in0=xt[:, kc, h * 512:(h + 1) * 512],
                                    scalar=vec[l][:, 24 + kc:25 + kc], in1=sil[1][:], op0=ALU.mult, op1=ALU.mult),
                                   reads=[("xt", kc), ("sil", 1), ("vec", l)], writes=[("xt", kc)])
                        for kc in range(8):
                            op("sp", lambda e, kc=kc, c0=c0: e.dma_start(out=outd.ap()[kc * 128:(kc + 1) * 128, c0:c0 + TT], in_=xt[:, kc, :]),
                               reads=[("xt", kc)], writes=[("odram", (kc, tt))], dma=True)
                while pending:
                    pending.pop(0)()
                sch.barrier()

        def evac(i, out_ap, in_ap, reads, writes, scale=None):
            if i % 2 == 0:
                if scale is None:
                    op("act", lambda e: e.copy(out=out_ap, in_=in_ap), reads=reads, writes=writes)
                else:
                    op("act", lambda e: e.mul(out=out_ap, in_=in_ap, mul=scale), reads=reads, writes=writes)
            else:
                if scale is None:
                    op("dve", lambda e: e.tensor_copy(out=out_ap, in_=in_ap), reads=reads, writes=writes)
                else:
                    op("dve", lambda e: e.tensor_scalar(out=out_ap, in0=in_ap, scalar1=scale, scalar2=None, op0=ALU.mult),
                       reads=reads, writes=writes)

        def mixer_phase(l, s, parts="abc"):
            c0s = s * S
            lam_init = 0.8 - 0.6 * math.exp(-0.3 * l)
            with contextlib.ExitStack() as es:
                ht = sb(es, "mht", [128, 8, S], BF16)
                cosF = sb(es, "cosF", [128, S], F32)
                sinF = sb(es, "sinF", [128, S], F32)
                lamv = sb(es, "lamv", [128, 8], F32)
                with contextlib.ExitStack() as es2:
                    rstd = sb(es2, "mrstd", [128, 512], F32)
                    for tt in range(4):
                        for kc in range(8):
                            op("sp", lambda e, kc=kc, tt=tt: e.dma_start(
                                out=ht[:, kc, tt * 512:(tt + 1) * 512],
                                in_=hmd.ap()[kc * 128:(kc + 1) * 128, c0s + tt * 512:c0s + (tt + 1) * 512]),
                               reads=[("hdram", None)], writes=[("mht", (kc, tt))], dma=True)
                    lt = rstd
                    for i in range(2):
                        op("dve", lambda e, i=i: e.tensor_tensor(out=lt[:, i * 64:(i + 1) * 64], in0=vec[l][:, 57 + 128 * i:57 + 128 * i + 64],
                                                                 in1=vec[l][:, 57 + 128 * i + 64:57 + 128 * i + 128], op=ALU.mult),
                           reads=[("vec", l), ("mrstd", None)], writes=[("mrstd", None)])
                        op("dve", lambda e, i=i: e.reduce_sum(out=lamv[:, 2 + i:3 + i], in_=lt[:, i * 64:(i + 1) * 64], axis=AX.X),
                           reads=[("mrstd", None)], writes=[("lamv", None)])
                    op("act", lambda e: e.activation(out=lamv[:, 4:6], in_=lamv[:, 2:4], func=AF.Exp),
                       reads=[("lamv", None)], writes=[("lamv", None)])
                    op("dve", lambda e: e.tensor_tensor(out=lamv[:, 0:1], in0=lamv[:, 5:6], in1=lamv[:, 4:5], op=ALU.subtract),
                       reads=[("lamv", None)], writes=[("lamv", None)])
                    op("dve", lambda e: e.tensor_scalar(out=lamv[:, 0:1], in0=lamv[:, 0:1], scalar1=-lam_init, scalar2=None, op0=ALU.add),
                       reads=[("lamv", None)], writes=[("lamv", None)])
                    op("dve", lambda e: e.tensor_scalar(out=lamv[:, 1:2], in0=vec[l][:, 56:57], scalar1=1.0 - lam_init, scalar2=None, op0=ALU.mult),
                       reads=[("lamv", None), ("vec", l)], writes=[("lamv", None)])
                sch.barrier()

                HS = S // 2
                tabs = {}

                def tables_alloc(scope):
                    tabs["tP"] = sb(scope, "tP", [128, HS], I32)
                    tabs["tT"] = [sb(scope, "tT%d" % i, [128, HS], F32) for i in range(2)]
                    tabs["tK"] = sb(scope, "tK", [128, HS], I32)
                C1 = 6.28125
                C2 = 2.0 * math.pi - 6.28125
                tab_jobs = []

                def tables_pool(hf):
                    cs = slice(hf * HS, (hf + 1) * HS)
                    tP, tT, tK = tabs["tP"], tabs["tT"], tabs["tK"]
                    tPf = tP[:].bitcast(F32)
                    tKf = tK[:].bitcast(F32)
                    tq.append(sch.defer("sp", lambda e: e.dma_start(out=tP[:], in_=posd.ap()[s][:, cs]), writes=[("tP", None)], dma=True))
                    tq.append(sch.defer("dve", lambda e: e.tensor_copy(out=tPf, in_=tP[:]), reads=[("tP", None)], writes=[("tP", None)]))
                    tq.append(sch.defer("dve", lambda e: e.tensor_scalar(out=tPf, in0=tPf, scalar1=cf[:, 0:1], scalar2=None, op0=ALU.mult),
                       reads=[("tP", None), ("cf", None)], writes=[("tP", None)]))
                    for wi, (shift, dst, dk, signed) in enumerate(((0.0, sinF, "tabs", True), (math.pi / 2, cosF, "tabc", False))):
                        t1 = tT[wi]
                        tk = ("tT", wi)
                        tq.append(sch.defer("dve", lambda e: e.tensor_scalar(out=t1[:], in0=tPf, scalar1=shift, scalar2=None, op0=ALU.add),
                           reads=[("tP", None)], writes=[tk]))
                        tq.append(sch.defer("dve", lambda e: e.tensor_scalar(out=tKf, in0=t1[:], scalar1=1.0 / (2 * math.pi), scalar2=None, op0=ALU.mult),
                           reads=[tk], writes=[("tK", None)]))
                        tq.append(sch.defer("dve", lambda e: e.tensor_copy(out=tK[:], in_=tKf), reads=[("tK", None)], writes=[("tK", None)]))
                        tq.append(sch.defer("dve", lambda e: e.tensor_copy(out=tKf, in_=tK[:]), reads=[("tK", None)], writes=[("tK", None)]))
                        tq.append(sch.defer("dve", lambda e: e.tensor_scalar(out=tKf, in0=tKf, scalar1=-C1, scalar2=None, op0=ALU.mult),
                           reads=[("tK", None)], writes=[("tK", None)]))
                        tq.append(sch.defer("dve", lambda e: e.tensor_tensor(out=t1[:], in0=t1[:], in1=tKf, op=ALU.add),
                           reads=[("tK", None), tk], writes=[tk]))
                        tq.append(sch.defer("dve", lambda e: e.tensor_scalar(out=tKf, in0=tKf, scalar1=C2 / C1, scalar2=None, op0=ALU.mult),
                           reads=[("tK", None)], writes=[("tK", None)]))
                        tq.append(sch.defer("dve", lambda e: e.tensor_tensor(out=t1[:], in0=t1[:], in1=tKf, op=ALU.add),
                           reads=[("tK", None), tk], writes=[tk]))
                        tq.append(sch.defer("dve", lambda e: e.tensor_scalar(out=tKf, in0=t1[:], scalar1=math.pi, scalar2=-2 * math.pi, op0=ALU.is_gt, op1=ALU.mult),
                           reads=[tk], writes=[("tK", None)]))
                        tq.append(sch.defer("dve", lambda e: e.tensor_tensor(out=t1[:], in0=t1[:], in1=tKf, op=ALU.add),
                           reads=[("tK", None), tk], writes=[tk]))
                        tq.append(sch.defer("dve", lambda e: e.tensor_scalar(out=tKf, in0=t1[:], scalar1=-math.pi, scalar2=2 * math.pi, op0=ALU.is_lt, op1=ALU.mult),
                           reads=[tk], writes=[("tK", None)]))
                        tq.append(sch.defer("dve", lambda e: e.tensor_tensor(out=t1[:], in0=t1[:], in1=tKf, op=ALU.add),
                           reads=[("tK", None), tk], writes=[tk]))
                        tq.append(sch.defer("dve", lambda e: e.tensor_scalar(out=t1[:], in0=t1[:], scalar1=-3.14159, scalar2=3.14159, op0=ALU.max, op1=ALU.min),
                           reads=[tk], writes=[tk]))
                        if signed:
                            tq.append(sch.defer("dve", lambda e: e.tensor_scalar(out=t1[:], in0=t1[:], scalar1=cf[:, 1:2], scalar2=None, op0=ALU.mult),
                               reads=[tk, ("cf", None)], writes=[tk]))

                def tables_act(hf):
                    cs = slice(hf * HS, (hf + 1) * HS)
                    tT = tabs["tT"]
                    tq.append(sch.defer("act", lambda e: e.activation(out=sinF[:, cs], in_=tT[0][:], func=AF.Sin), reads=[("tT", 0)], writes=[("tabs", None)]))
                    tq.append(sch.defer("act", lambda e: e.activation(out=cosF[:, cs], in_=tT[1][:], func=AF.Sin), reads=[("tT", 1)], writes=[("tabc", None)]))

                tstate = {"n": 0}
                tq = []

                def tables_drain(n=None):
                    k = 0
                    while tq and (n is None or k < n):
                        tq.pop(0)()
                        k += 1

                def tables_step():
                    k = tstate["n"]
                    tstate["n"] += 1
                    if k == 0:
                        tables_pool(0)
                    elif k == 1:
                        tables_act(0)
                        tables_pool(1)
                    elif k == 2:
                        tables_act(1)

                def tables_finish():
                    while tstate["n"] < 3:
                        tables_step()
                    tables_drain()

                def proj_fm(wb, wk, C, cidx, dst_fn, key, evi=[0]):
                    for tt in range(4):
                        ps, pk = psnext()
                        for kc in range(8):
                            op("pe", lambda e, kc=kc, tt=tt, ps=ps: e.matmul(
                                ps[:], lhsT=wb[:, kc * C + cidx * 128:kc * C + (cidx + 1) * 128],
                                rhs=ht[:, kc, tt * 512:(tt + 1) * 512], start=(kc == 0), stop=(kc == 7)),
                               reads=[wk, ("mht", (kc, tt))], writes=[pk])
                        dst_fn(tt, ps, pk)

                def proj_v(wb, wk, C, col0, ncols, vt, vkey, stride=1, tokfn=None):
                    for blk in range(16):
                        ps, pk = psnext()
                        tk = tokfn(blk)
                        for kc in range(8):
                            op("pe", lambda e, kc=kc, ps=ps, tk=tk: e.matmul(
                                ps[:, 0:ncols], lhsT=ht[:, kc, tk], rhs=wb[:, kc * C + col0:kc * C + col0 + ncols],
                                start=(kc == 0), stop=(kc == 7)),
                               reads=[wk, (("mht", (kc, blk // 4)) if stride == 1 else ("mht", None))], writes=[pk])
                        evac(blk, vt[:, blk, 0:ncols], ps[:, 0:ncols], [pk], [(vkey, blk)])

                rope_n = {"n": 0}

                def rope_evac(psP, pkP, psS, pkS, dst_ap, dkey, tt, rt, perm=None):
                    ri = rope_n["n"] % 2
                    rope_n["n"] += 1
                    a, b = rt[ri]
                    cs = slice(tt * 512, (tt + 1) * 512)
                    op("dve", lambda e: e.tensor_tensor(out=a[:], in0=psP[:], in1=cosF[:, cs], op=ALU.mult),
                       reads=[pkP, ("tabc", None)], writes=[("ra", ri)])
                    op("dve", lambda e: e.tensor_tensor(out=b[:], in0=psS[:], in1=sinF[:, cs], op=ALU.mult),
                       reads=[pkS, ("tabs", None)], writes=[("rb", ri)])
                    if perm is None:
                        op("dve", lambda e: e.tensor_tensor(out=dst_ap, in0=a[:], in1=b[:], op=ALU.add),
                           reads=[("ra", ri), ("rb", ri)], writes=[dkey])
                    else:
                        d = perm
                        op("dve", lambda e: e.tensor_tensor(out=dst_ap, in0=a[:].rearrange("p (m c) -> p c m", c=d),
                                                             in1=b[:].rearrange("p (m c) -> p c m", c=d), op=ALU.add),
                           reads=[("ra", ri), ("rb", ri)], writes=[dkey])

                def merge_all(branches):
                    with contextlib.ExitStack() as es3:
                        macc = sb(es3, "macc", [128, 8, S], BF16)
                        xt1 = sb(es3, "gx", [128, 8, 512], F32)
                        sg = [sb(es3, "gs%d" % i, [128, 512], F32) for i in range(2)]
                        pr = [sb(es3, "gp%d" % i, [128, 512], BF16) for i in range(2)]
                        n = 0
                        for bidx, (b, ychunks, ykey) in enumerate(branches):
                            nk = len(ychunks)
                            bi = "abc".index(b)
                            wg, wgk = wload(l, "g_" + b)
                            wu, wuk = wload(l, "up_" + b)
                            for tt in range(4):
                                cs = slice(tt * 512, (tt + 1) * 512)
                                for dc in range(8):
                                    pg, pgk = psnext()
                                    for kc in range(8):
                                        op("pe", lambda e, kc=kc: e.matmul(
                                            pg[:], lhsT=wg[:, kc * 1024 + dc * 128:kc * 1024 + (dc + 1) * 128],
                                            rhs=ht[:, kc, cs], start=(kc == 0), stop=(kc == 7)),
                                           reads=[wgk, ("mht", None)], writes=[pgk])
                                    sgt = sg[n % 2]
                                    prt = pr[n % 2]
                                    sgk = ("gs", n % 2)
                                    prk = ("gp", n % 2)
                                    n += 1
                                    op("act", lambda e: e.activation(out=sgt[:], in_=pg[:], func=AF.Sigmoid,
                                                                     bias=vec[l][:, 32 + bi * 8 + dc:33 + bi * 8 + dc]),
                                       reads=[pgk, ("vec", l)], writes=[sgk])
                                    pu, puk = psnext()
                                    for j in range(nk):
                                        op("pe", lambda e, j=j: e.matmul(
                                            pu[:], lhsT=wu[:, j * 1024 + dc * 128:j * 1024 + (dc + 1) * 128],
                                            rhs=ychunks[j][:, cs], start=(j == 0), stop=(j == nk - 1)),
                                           reads=[wuk, ykey], writes=[puk])
                                    if bidx == 0:
                                        op("dve", lambda e: e.tensor_tensor(out=macc[:, dc, cs], in0=pu[:], in1=sgt[:], op=ALU.mult),
                                           reads=[puk, sgk], writes=[("macc", (dc, tt))])
                                    else:
                                        op("dve", lambda e: e.tensor_tensor(out=prt[:], in0=pu[:], in1=sgt[:], op=ALU.mult),
                                           reads=[puk, sgk], writes=[prk])
                                        op("dve", lambda e: e.tensor_tensor(out=macc[:, dc, cs], in0=macc[:, dc, cs], in1=prt[:], op=ALU.add),
                                           reads=[prk, ("macc", (dc, tt))], writes=[("macc", (dc, tt))])
                        wo, wok = wload(l, "wo_a")
                        for tt in range(4):
                            cs = slice(tt * 512, (tt + 1) * 512)
                            for kc in range(8):
                                op("sp", lambda e, kc=kc: e.dma_start(
                                    out=xt1[:, kc, :], in_=xa.ap()[kc * 128:(kc + 1) * 128, c0s + tt * 512:c0s + (tt + 1) * 512]),
                                   reads=[("xdram", (kc, s, tt))], writes=[("gx", kc)], dma=True)
                            for dc2 in range(8):
                                pz, pzk = psnext()
                                for dc in range(8):
                                    op("pe", lambda e, dc=dc: e.matmul(
                                        pz[:], lhsT=wo[:, dc * 1024 + dc2 * 128:dc * 1024 + (dc2 + 1) * 128],
                                        rhs=macc[:, dc, cs], start=(dc == 0), stop=(dc == 7)),
                                       reads=[wok, ("macc", (dc, tt))], writes=[pzk])
                                op("dve", lambda e: e.tensor_tensor(out=xt1[:, dc2, :], in0=pz[:], in1=xt1[:, dc2, :], op=ALU.add),
                                   reads=[pzk, ("gx", dc2)], writes=[("gx", dc2)])
                                op("sp", lambda e: e.dma_start(
                                    out=xa.ap()[dc2 * 128:(dc2 + 1) * 128, c0s + tt * 512:c0s + (tt + 1) * 512], in_=xt1[:, dc2, :]),
                                   reads=[("gx", dc2)], writes=[("xdram", (dc2, s, tt))], dma=True)
                    sch.barrier()

                if "a" in parts:
                    if True:
                        ya = sb(es, "ya", [128, 4, S], BF16)
                        with contextlib.ExitStack() as es3:
                            tables_alloc(es3)
                            va = sb(es3, "va", [128, 16, 512], BF16)
                            qts = [sb(es3, "aq%d" % i, [128, S], BF16) for i in range(2)]
                            kts = [sb(es3, "ak%d" % i, [128, S], BF16) for i in range(2)]
                            et = [sb(es3, "ae%d" % i, [128, 2, 512], F32) for i in range(2)]
                            spt = [sb(es3, "asp%d" % i, [128, 2, 512], BF16) for i in range(3)]
                            wt = [sb(es3, "aw%d" % i, [128, 2, 512], BF16) for i in range(3)]
                            Rt = [sb(es3, "aR%d" % i, [128, 2, 512], BF16) for i in range(4)]
                            wb, wk = wload(l, "a_v")
                            proj_v(wb, wk, 512, 0, 512, va, "va", tokfn=lambda blk: slice(blk * 128, (blk + 1) * 128))
                            wb, wk = wload(l, "a_qk")
                            ZB = [0, 2, 4]
                            OB = [6, 7]
                            def proj_items(hp):
                                return [dict(kind="proj", hp=hp, tt=tt, which=which) for tt in range(4) for which in range(2)]

                            def step_items(hp, gi0):
                                out = []
                                gi = gi0
                                for qc in range(4):
                                    kbs = list(range(4 * qc + 3, -1, -1))
                                    n = len(kbs)

                                    def rng_of(kb, qc=qc):
                                        lo = 128 * (kb - 4 * qc) if kb >= 4 * qc else 0
                                        return lo, 512 - lo

                                    for i, kb in enumerate(kbs):
                                        lo, w = rng_of(kb)
                                        nxt = rng_of(kbs[i + 1]) if i + 1 < n else None
                                        out.append(dict(kind="step", hp=hp, qc=qc, i=i, n=n, kb=kb, lo=lo, w=w, nxt=nxt,
                                                        diag=(kb >= 4 * qc), gi=gi))
                                    gi += 1
                                return out, gi

                            items = proj_items(0)
                            gi = 0
                            for hp in range(4):
                                st, gi = step_items(hp, gi)
                                pj = proj_items(hp + 1) if hp + 1 < 4 else []
                                for k, it in enumerate(st):
                                    items.append(it)
                                    if pj and k % 4 == 3:
                                        items.append(pj.pop(0))
                                items.extend(pj)
                            prev_real = None
                            for g_, it in enumerate(items):
                                it["g"] = g_
                                if it["kind"] == "step":
                                    it["gprev"] = prev_real["g"] if (prev_real is not None and it["i"] > 0) else None
                                    if prev_real is not None and it["i"] > 0:
                                        prev_real["gnext"] = g_
                                    it["gnext"] = None
                                    prev_real = it
                            NI = len(items)

                            def zview(g_, w):
                                zb = ZB[g_ % 3]
                                return pst[:, zb * 512:(zb + 2) * 512].rearrange("p (h c) -> p h c", h=2)[:, :, 0:w]

                            def stA(g_):
                                it = items[g_]
                                zb = ZB[g_ % 3]
                                if it["kind"] == "proj":
                                    hp, tt, which = it["hp"], it["tt"], it["which"]
                                    dst = (qts if which == 0 else kts)[hp % 2]
                                    dk = "aq" if which == 0 else "ak"
                                    ps = PS[zb]
                                    for kc in range(8):
                                        op("pe", lambda e, kc=kc: e.matmul(
                                            ps[:], lhsT=wb[:, kc * 1024 + (2 * hp + which) * 128:kc * 1024 + (2 * hp + which + 1) * 128],
                                            rhs=ht[:, kc, tt * 512:(tt + 1) * 512], start=(kc == 0), stop=(kc == 7)),
                                           reads=[wk, ("mht", None)], writes=[("ps", zb)])
                                    evac(1, dst[:, tt * 512:(tt + 1) * 512], ps[:], [("ps", zb)], [(dk, (hp % 2, tt))],
                                         scale=(0.125 if which == 0 else None))
                                    return
                                hp, qc, i, n, kb, lo, w = it["hp"], it["qc"], it["i"], it["n"], it["kb"], it["lo"], it["w"]
                                qt, kt = qts[hp % 2], kts[hp % 2]
                                if g_ >= 12:
                                    tables_drain(1)
                                for hh in range(2):
                                    hb = slice(64 * hh, 64 * hh + 64)
                                    op("pe", lambda e: e.matmul(PS[zb + hh][:, 0:w], lhsT=kt[hb, kb * 128:(kb + 1) * 128],
                                                                rhs=qt[hb, qc * 512 + lo:qc * 512 + 512], start=True, stop=True),
                                       reads=[("ak", (hp % 2, kb // 4)), ("aq", (hp % 2, qc))], writes=[("ps", zb + hh)])
                                ee = et[g_ % 2]
                                sp = spt[g_ % 3]
                                op("act", lambda e: e.activation(out=ee[:, :, 0:w], in_=zview(g_, w), func=AF.Exp),
                                   reads=[("ps", zb), ("ps", zb + 1)], writes=[("ae", g_ % 2)])
                                op("act", lambda e: e.activation(out=sp[:, :, 0:w], in_=ee[:, :, 0:w], func=AF.Ln, bias=1.0),
                                   reads=[("ae", g_ % 2)], writes=[("asp", g_ % 3)])
                                if it["diag"]:
                                    for hh in range(2):
                                        op("dve", lambda e: e.tensor_tensor(out=sp[:, hh, 0:128], in0=sp[:, hh, 0:128], in1=mstrict, op=ALU.mult),
                                           reads=[("asp", g_ % 3), ("cb", None)], writes=[("asp", g_ % 3)])
                                if it["nxt"] is not None:
                                    lo2, w2 = it["nxt"]
                                    gn = it["gnext"]
                                    Rn = Rt[gn % 4]
                                    Rp = Rt[g_ % 4]
                                    if lo2 < lo:
                                        op("dve", lambda e: e.memset(Rn[:, :, 0:lo - lo2], 0.0), writes=[("aR", gn % 4)])
                                    if i == 0:
                                        op("dve", lambda e: e.tensor_copy(out=Rn[:, :, lo - lo2:w2], in_=sp[:, :, 0:w]),
                                           reads=[("asp", g_ % 3)], writes=[("aR", gn % 4)])
                                    else:
                                        op("dve", lambda e: e.tensor_tensor(out=Rn[:, :, lo - lo2:w2], in0=Rp[:, :, 0:w], in1=sp[:, :, 0:w], op=ALU.add),
                                           reads=[("asp", g_ % 3), ("aR", g_ % 4)], writes=[("aR", gn % 4)])

                            def stC(g_):
                                it = items[g_]
                                if it["kind"] == "proj":
                                    return
                                i, kb, lo, w = it["i"], it["kb"], it["lo"], it["w"]
                                zb = ZB[g_ % 3]
                                sp = spt[g_ % 3]
                                Rp = Rt[g_ % 4]
                                for hh in range(2):
                                    op("pe", lambda e: e.matmul(PS[zb + hh][:, 0:w], lhsT=trineg, rhs=sp[:, hh, 0:w], start=False, stop=(i == 0), skip_group_check=True),
                                       reads=[("asp", g_ % 3), ("cb", None)], writes=[("ps", zb + hh)])
                                    if i > 0:
                                        op("pe", lambda e: e.matmul(PS[zb + hh][:, 0:w], lhsT=negones, rhs=Rp[:, hh, 0:w], start=False, stop=True, skip_group_check=True),
                                           reads=[("aR", g_ % 4), ("cb", None)], writes=[("ps", zb + hh)])
                                ww = wt[g_ % 3]
                                op("act", lambda e: e.activation(out=ww[:, :, 0:w], in_=zview(g_, w), func=AF.Exp),
                                   reads=[("ps", zb), ("ps", zb + 1)], writes=[("aw", g_ % 3)])
                                if it["diag"]:
                                    for hh in range(2):
                                        op("dve", lambda e: e.tensor_tensor(out=ww[:, hh, 0:128], in0=ww[:, hh, 0:128], in1=mstrict, op=ALU.mult),
                                           reads=[("aw", g_ % 3), ("cb", None)], writes=[("aw", g_ % 3)])

                            def stE(g_):
                                it = items[g_]
                                if it["kind"] == "proj":
                                    return
                                hp, qc, i, n, kb, lo, w = it["hp"], it["qc"], it["i"], it["n"], it["kb"], it["lo"], it["w"]
                                ob = OB[it["gi"] % 2]
                                po, pok = PS[ob], ("ps", ob)
                                ww = wt[g_ % 3]
                                for hh in range(2):
                                    hb = slice(64 * hh, 64 * hh + 64)
                                    h = 2 * hp + hh
                                    op("pe", lambda e: e.matmul(po[hb, lo:512], lhsT=va[:, kb, h * 64:(h + 1) * 64], rhs=ww[:, hh, 0:w],
                                                                start=(i == 0), stop=(i == n - 1), skip_group_check=True),
                                       reads=[("aw", g_ % 3), ("va", kb)], writes=[pok])
                                if i == n - 1:
                                    evac(it["gi"], ya[:, hp, qc * 512:(qc + 1) * 512], po[:, :], [pok], [("ya", (hp, qc))])

                            while tstate["n"] < 3:
                                tables_step()
                            for t in range(NI + 2):
                                if t < NI:
                                    stA(t)
                                if 0 <= t - 1 < NI:
                                    stC(t - 1)
                                if 0 <= t - 2 < NI:
                                    stE(t - 2)
                        sch.barrier()

                if tstate["n"] < 3:
                    with contextlib.ExitStack() as est:
                        if "tP" not in tabs or "a" not in parts:
                            tables_alloc(est)
                        tables_finish()
                        sch.barrier()
                if "b" in parts:
                    if True:
                        yb = sb(es, "yb", [128, 2, S], BF16)
                        with contextlib.ExitStack() as es3:
                            vb = [sb(es3, "vb%d" % g, [128, 16, 256], BF16) for g in range(3)]
                            qbt = sb(es3, "bq", [128, S], BF16)
                            kbt = sb(es3, "bk", [128, S], BF16)
                            rt = [(sb(es3, "ra%d" % i, [128, 512], F32), sb(es3, "rb%d" % i, [128, 512], F32)) for i in range(2)]
                            pt = [sb(es3, "bp%d" % i, [128, 2, 256], BF16) for i in range(3)]
                            snum = sb(es3, "snum", [128, S], F32)
                            sden = sb(es3, "sden", [128, S], F32)
                            wb, wk = wload(l, "b_v")
                            for g, d in enumerate((1, 4, 16)):
                                Lg = S // d
                                nbg = Lg // 128

                                def tokfn(blk, d=d, nbg=nbg):
                                    c, b = blk // nbg, blk % nbg
                                    st = c + 128 * b * d
                                    return slice(st, st + 127 * d + 1, d) if d > 1 else slice(st, st + 128)

                                proj_v(wb, wk, 768, 256 * g, 256, vb[g], "vb%d" % g, stride=d, tokfn=tokfn)
                            wbs = [wload(l, "b_qk%d" % j) for j in range(3)]
                            SS = [(PS[0], ("ps", 0)), (PS[1], ("ps", 1))]
                            for pi in range(2):
                                for g, d in enumerate((1, 4, 16)):
                                    wb, wk = wbs[g]
                                    Lg = S // d
                                    nbg = Lg // 128
                                    for which, dstT, dkey in ((0, qbt, "bq"), (1, kbt, "bk")):
                                        for tt in range(4):
                                            pP, pPk = psnext()
                                            pS, pSk = psnext()
                                            for kc in range(8):
                                                for (pp, ppk, cidx) in ((pP, pPk, pi * 4 + which * 2), (pS, pSk, pi * 4 + which * 2 + 1)):
                                                    op("pe", lambda e, kc=kc, tt=tt, pp=pp, cidx=cidx, wb=wb: e.matmul(
                                                        pp[:], lhsT=wb[:, kc * 1024 + cidx * 128:kc * 1024 + (cidx + 1) * 128],
                                                        rhs=ht[:, kc, tt * 512:(tt + 1) * 512], start=(kc == 0), stop=(kc == 7)),
                                                       reads=[wk, ("mht", (kc, tt))], writes=[ppk])
                                            if d == 1:
                                                dst = dstT[:, tt * 512:(tt + 1) * 512]
                                                rope_evac(pP, pPk, pS, pSk, dst, (dkey, None), tt, rt)
                                            else:
                                                m = 512 // d
                                                dst = dstT[:].rearrange("p (c l) -> p c l", c=d)[:, :, tt * m:(tt + 1) * m]
                                                rope_evac(pP, pPk, pS, pSk, dst, (dkey, None), tt, rt, perm=d)
                                    jobs = []
                                    gi = 0
                                    for c in range(d):
                                        for QT in range((Lg + 511) // 512):
                                            wqt = min(512, Lg - 512 * QT)
                                            kbs = list(range(max(0, 4 * QT - 1), min(4 * QT + 3, nbg - 1) + 1))
                                            for ji, kb in enumerate(kbs):
                                                lo = max(128 * kb, 512 * QT) - 512 * QT
                                                hi = min(128 * kb + 256, 512 * QT + wqt) - 512 * QT
                                                mc0 = (512 * QT + lo) - 128 * kb
                                                jobs.append(dict(c=c, QT=QT, kb=kb, lo=lo, hi=hi, mc0=mc0, first=(ji == 0),
                                                                 last=(ji == len(kbs) - 1), gi=gi, wqt=wqt))
                                            gi += 1
                                    nj = len(jobs)

                                    def s0(i):
                                        J = jobs[i]
                                        w = J["hi"] - J["lo"]
                                        zb = 2 * (i % 2)
                                        kc0 = J["c"] * Lg + 128 * J["kb"]
                                        base = J["c"] * Lg + 512 * J["QT"]
                                        for hh in range(2):
                                            hb = slice(64 * hh, 64 * hh + 64)
                                            op("pe", lambda e: e.matmul(PS[zb + hh][:, 0:w], lhsT=kbt[hb, kc0:kc0 + 128],
                                                                        rhs=qbt[hb, base + J["lo"]:base + J["hi"]], start=True, stop=True),
                                               reads=[("bk", None), ("bq", None)], writes=[("ps", zb + hh)])
                                        p = pt[i % 3]
                                        zv = pst[:, zb * 512:(zb + 2) * 512].rearrange("p (h c) -> p h c", h=2)[:, :, 0:w]
                                        op("act", lambda e: e.activation(out=p[:, :, 0:w], in_=zv, func=AF.Exp, scale=0.125),
                                           reads=[("ps", zb), ("ps", zb + 1)], writes=[("bp", i % 3)])
                                        op("dve", lambda e: e.tensor_tensor(out=p[:, :, 0:w], in0=p[:, :, 0:w], in1=mband2[:, :, J["mc0"]:J["mc0"] + w], op=ALU.mult),
                                           reads=[("bp", i % 3), ("cb", None)], writes=[("bp", i % 3)])

                                    def s1(i):
                                        J = jobs[i]
                                        w = J["hi"] - J["lo"]
                                        lo, hi = J["lo"], J["hi"]
                                        p = pt[i % 3]
                                        blk = J["c"] * nbg + J["kb"]
                                        nb_ = 4 + (J["gi"] % 2)
                                        db_ = 6 + (J["gi"] % 2)
                                        pn, pd = PS[nb_], PS[db_]
                                        for hh in range(2):
                                            hb = slice(64 * hh, 64 * hh + 64)
                                            hloc = 2 * pi + hh
                                            op("pe", lambda e: e.matmul(pn[hb, lo:hi], lhsT=vb[g][:, blk, hloc * 64:(hloc + 1) * 64], rhs=p[:, hh, 0:w],
                                                                        start=J["first"], stop=J["last"], skip_group_check=True),
                                               reads=[("bp", i % 3), ("vb%d" % g, blk)], writes=[("ps", nb_)])
                                            op("pe", lambda e: e.matmul(pd[hb, lo:hi], lhsT=ones[:, 0:64], rhs=p[:, hh, 0:w],
                                                                        start=J["first"], stop=J["last"], skip_group_check=True),
                                               reads=[("bp", i % 3), ("cb", None)], writes=[("ps", db_)])
                                        if J["last"]:
                                            wqt = J["wqt"]
                                            n0 = 512 * J["QT"] * d + J["c"]
                                            nsl = slice(n0, n0 + wqt) if d == 1 else slice(n0, n0 + (wqt - 1) * d + 1, d)
                                            if g == 0:
                                                op("act", lambda e: e.copy(out=snum[:, nsl], in_=pn[:, 0:wqt]), reads=[("ps", nb_)], writes=[("snum", None)])
                                                op("dve", lambda e: e.tensor_copy(out=sden[:, nsl], in_=pd[:, 0:wqt]), reads=[("ps", db_)], writes=[("sden", None)])
                                            else:
                                                op("dve", lambda e: e.tensor_tensor(out=snum[:, nsl], in0=pn[:, 0:wqt], in1=snum[:, nsl], op=ALU.add),
                                                   reads=[("ps", nb_), ("snum", None)], writes=[("snum", None)])
                                                op("dve", lambda e: e.tensor_tensor(out=sden[:, nsl], in0=pd[:, 0:wqt], in1=sden[:, nsl], op=ALU.add),
                                                   reads=[("ps", db_), ("sden", None)], writes=[("sden", None)])

                                    for t in range(nj + 2):
                                        if t < nj:
                                            s0(t)
                                        if t >= 2:
                                            s1(t - 2)
                                op("dve", lambda e: e.reciprocal(out=sden[:], in_=sden[:]), reads=[("sden", None)], writes=[("sden", None)])
                                op("dve", lambda e, pi=pi: e.tensor_tensor(out=yb[:, pi, :], in0=snum[:], in1=sden[:], op=ALU.mult),
                                   reads=[("sden", None), ("snum", None)], writes=[("yb", pi)])
                        sch.barrier()

                if "c" in parts:
                    if True:
                        yc = sb(es, "yc", [128, 4, S], BF16)
                        with contextlib.ExitStack() as es3:
                            vc = sb(es3, "vc", [128, 16, 512], BF16)
                            qk = [sb(es3, "cqk%d" % i, [128, S], BF16) for i in range(4)]
                            rt = [(sb(es3, "ra%d" % i, [128, 512], F32), sb(es3, "rb%d" % i, [128, 512], F32)) for i in range(2)]
                            pt = [sb(es3, "cp%d" % i, [128, 2, 512], BF16) for i in range(3)]
                            lnd = sb(es3, "clnd", [128, 512], F32)
                            rr = [sb(es3, "crr%d" % i, [128, 512], F32) for i in range(2)]
                            oo = [sb(es3, "coo%d" % i, [128, 512], F32) for i in range(3)]
                            osq = sb(es3, "cosq", [128, 512], BF16)
                            wb, wk = wload(l, "c_v")
                            proj_v(wb, wk, 512, 0, 512, vc, "vc", tokfn=lambda blk: slice(blk * 128, (blk + 1) * 128))
                            PO = [(PS[0], ("ps", 0)), (PS[1], ("ps", 1))]
                            PD = [(PS[2], ("ps", 2)), (PS[3], ("ps", 3))]
                            wbs = {}
                            items = []
                            for hp in range(2):
                                for m4 in range(4):
                                    for tt in range(4):
                                        items.append(dict(kind="proj", hp=hp, m4=m4, tt=tt))
                                for hh in range(2):
                                    for qc in range(4):
                                        nsteps = 4 * qc + 4
                                        for kb in range(nsteps):
                                            lo = 128 * (kb - 4 * qc) if kb >= 4 * qc else 0
                                            items.append(dict(kind="step", hp=hp, hh=hh, qc=qc, kb=kb, lo=lo, w=512 - lo,
                                                              first=(kb == 0), last=(kb == nsteps - 1), diag=(kb >= 4 * qc)))
                                        items.append(dict(kind="post", hp=hp, hh=hh, qc=qc))
                            NI = len(items)
                            deferred = {}

                            def c0(g_):
                                it = items[g_]
                                zb = 4 + 2 * (g_ % 2)
                                if it["kind"] == "proj":
                                    hp, m4, tt = it["hp"], it["m4"], it["tt"]
                                    if hp not in wbs:
                                        wbs[hp] = wload(l, "c_qk%d" % hp)
                                    wb, wk = wbs[hp]
                                    pP, pPk = PS[zb], ("ps", zb)
                                    pS, pSk = PS[zb + 1], ("ps", zb + 1)
                                    for kc in range(8):
                                        for (pp, ppk, cidx) in ((pP, pPk, 2 * m4), (pS, pSk, 2 * m4 + 1)):
                                            op("pe", lambda e: e.matmul(
                                                pp[:], lhsT=wb[:, kc * 1024 + cidx * 128:kc * 1024 + (cidx + 1) * 128],
                                                rhs=ht[:, kc, tt * 512:(tt + 1) * 512], start=(kc == 0), stop=(kc == 7)),
                                               reads=[wk, ("mht", None)], writes=[ppk])
                                    rope_evac(pP, pPk, pS, pSk, qk[m4][:, tt * 512:(tt + 1) * 512], ("cqk", m4), tt, rt)
                                    return
                                if it["kind"] == "post":
                                    return
                                hb = slice(64 * it["hh"], 64 * it["hh"] + 64)
                                qc, kb, lo, w = it["qc"], it["kb"], it["lo"], it["w"]
                                p = pt[g_ % 3]
                                for m in range(2):
                                    op("pe", lambda e: e.matmul(PS[zb + m][:, 0:w], lhsT=qk[2 + m][hb, kb * 128:(kb + 1) * 128],
                                                                rhs=qk[m][hb, qc * 512 + lo:qc * 512 + 512], start=True, stop=True),
                                       reads=[("cqk", 2 + m), ("cqk", m)], writes=[("ps", zb + m)])
                                zv = pst[:, zb * 512:(zb + 2) * 512].rearrange("p (h c) -> p h c", h=2)[:, :, 0:w]
                                op("act", lambda e: e.activation(out=p[:, :, 0:w], in_=zv, func=AF.Exp, scale=0.125),
                                   reads=[("ps", zb), ("ps", zb + 1)], writes=[("cp", g_ % 3)])
                                if it["diag"]:
                                    for m in range(2):
                                        op("dve", lambda e: e.tensor_tensor(out=p[:, m, 0:128], in0=p[:, m, 0:128], in1=mincl, op=ALU.mult),
                                           reads=[("cp", g_ % 3), ("cb", None)], writes=[("cp", g_ % 3)])

                            def c1(g_):
                                it = items[g_]
                                if it["kind"] == "proj":
                                    return
                                h = 2 * it["hp"] + it["hh"]
                                qc = it["qc"]
                                if it["kind"] == "step":
                                    kb, lo, w = it["kb"], it["lo"], it["w"]
                                    p = pt[g_ % 3]
                                    for m in range(2):
                                        po, pok = PO[m]
                                        pd, pdk = PD[m]
                                        op("pe", lambda e: e.matmul(po[:, lo:512], lhsT=vc[:, kb, h * 128:(h + 1) * 128], rhs=p[:, m, 0:w],
                                                                    start=it["first"], stop=it["last"]),
                                           reads=[("cp", g_ % 3), ("vc", kb)], writes=[pok])
                                        op("pe", lambda e: e.matmul(pd[:, lo:512], lhsT=ones, rhs=p[:, m, 0:w],
                                                                    start=it["first"], stop=it["last"]),
                                           reads=[("cp", g_ % 3), ("cb", None)], writes=[pdk])
                                    return
                                for m in range(2):
                                    pd, pdk = PD[m]
                                    op("act", lambda e: e.activation(out=rr[m][:], in_=pd[:], func=AF.Ln), reads=[pdk], writes=[("crr", m)])
                                for m in range(2):
                                    po, pok = PO[m]
                                    op("dve", lambda e: e.tensor_copy(out=oo[m][:], in_=po[:]), reads=[pok], writes=[("coo", m)])
                                for m in range(2):
                                    op("act", lambda e: e.activation(out=rr[m][:], in_=rr[m][:], func=AF.Exp, scale=-1.0),
                                       reads=[("crr", m)], writes=[("crr", m)])
                                op("dve", lambda e: e.tensor_tensor(out=oo[1][:], in0=oo[1][:], in1=rr[1][:], op=ALU.mult),
                                   reads=[("coo", 1), ("crr", 1)], writes=[("coo", 1)])
                                op("dve", lambda e: e.tensor_tensor(out=oo[0][:], in0=oo[0][:], in1=rr[0][:], op=ALU.mult),
                                   reads=[("coo", 0), ("crr", 0)], writes=[("coo", 0)])
                                op("dve", lambda e: e.scalar_tensor_tensor(out=oo[2][:], in0=oo[1][:], scalar=lamv[:, 0:1], in1=oo[0][:],
                                                                            op0=ALU.mult, op1=ALU.add),
                                   reads=[("coo", 0), ("coo", 1), ("lamv", None)], writes=[("coo", 2)])
                                op("dve", lambda e: e.tensor_tensor(out=osq[:], in0=oo[2][:], in1=oo[2][:], op=ALU.mult),
                                   reads=[("coo", 2)], writes=[("cosq", None)])
                                def post2(g2, h=h, qc=qc):
                                    zb = 4 + 2 * (g2 % 2)
                                    pss, pssk = PS[zb], ("ps", zb)
                                    op("pe", lambda e: e.matmul(pss[:], lhsT=ones, rhs=osq[:], start=True, stop=True),
                                       reads=[("cosq", None), ("cb", None)], writes=[pssk])
                                    op("act", lambda e: e.activation(out=lnd[:], in_=pss[:], func=AF.Ln, scale=1.0 / 128, bias=1e-5),
                                       reads=[pssk], writes=[("clnd", None)])
                                    op("act", lambda e: e.activation(out=lnd[:], in_=lnd[:], func=AF.Exp, scale=-0.5),
                                       reads=[("clnd", None)], writes=[("clnd", None)])
                                    op("dve", lambda e: e.scalar_tensor_tensor(out=yc[:, h, qc * 512:(qc + 1) * 512], in0=oo[2][:], scalar=lamv[:, 1:2],
                                                                                in1=lnd[:], op0=ALU.mult, op1=ALU.mult),
                                       reads=[("coo", 2), ("clnd", None), ("lamv", None)], writes=[("yc", (h, qc))])

                                deferred[g_ + 3] = post2

                            for t in range(NI + 6):
                                if t < NI:
                                    c0(t)
                                if 2 <= t < NI + 2:
                                    c1(t - 2)
                                if (t - 2) in deferred:
                                    deferred.pop(t - 2)(t - 2)
                            assert not deferred
                        sch.barrier()
                branches = []
                if "a" in parts:
                    branches.append(("a", [ya[:, j, :] for j in range(4)], ("ya", None)))
                if "b" in parts:
                    branches.append(("b", [yb[:, j, :] for j in range(2)], ("yb", None)))
                if "c" in parts:
                    branches.append(("c", [yc[:, j, :] for j in range(4)], ("yc", None)))
                merge_all(branches)
            sch.barrier()

        prog = []
        for l in range(DEPTH):
            prog.append(("ffn", l, 1))
            for s in range(NSEQ):
                prog.append(("mix", l, s))
            prog.append(("ffn", l, 2))
        if stop_after is not None:
            prog = prog[:stop_after]
        src = xin
        for pi, item in enumerate(prog):
            last = pi == len(prog) - 1
            if item[0] == "ffn":
                ffn_phase(item[1], item[2], src, final=last, emit_hm=(item[2] == 1 and not last))
                src = xa
            else:
                mixer_phase(item[1], item[2], parts=MIX_PARTS)
                if last:
                    for kc in range(8):
                        op("sp", lambda e, kc=kc: e.dma_start(out=outd.ap()[kc * 128:(kc + 1) * 128, :], in_=xa.ap()[kc * 128:(kc + 1) * 128, :]),
                           reads=[("xdram", None)], writes=[("odram", None)], dma=True)
        sch.emit()
    return nc


MIX_PARTS = "abc"
_CACHE = {}


def kernel(**inputs):
    inp = {k: np.asarray(v) for k, v in inputs.items()}
    B = inp["x"].shape[0]
    nseq = B // NCORES
    depth = inp["w_in"].shape[0]
    key = (nseq, depth)
    if key not in _CACHE:
        _CACHE[key] = build_program(NSEQ=nseq, DEPTH=depth)
    nc = _CACHE[key]
    consts = _build_consts()
    wimgs = [_build_wimg(inp, l) for l in range(depth)]
    vecs = [_build_vecs(inp, l) for l in range(depth)]
    in_maps = []
    for c in range(NCORES):
        xb = inp["x"][c * nseq:(c + 1) * nseq]
        xT = np.ascontiguousarray(xb.reshape(nseq * S, D).T)
        pos = inp["positions"][c * nseq:(c + 1) * nseq].astype(np.int32)
        posb = np.ascontiguousarray(np.broadcast_to(pos[:, None, :], (nseq, 128, S)))
        m = {"xT": xT, "pos": posb, "consts": consts}
        for l in range(depth):
            m["wimg%d" % l] = wimgs[l]
            m["vecs%d" % l] = vecs[l]
        in_maps.append(m)
    res = run_bass_kernel_spmd(nc, in_maps, core_ids=list(range(NCORES)))
    outs = [np.asarray(r["outT"]).T.reshape(nseq, S, D) for r in res.results]
    return np.ascontiguousarray(np.concatenate(outs, axis=0).astype(np.float32))
```

```python
import contextlib
import math
import numpy as np
import concourse.bass as bass
import concourse.mybir as mybir
from concourse.bass_utils import run_bass_kernel_spmd

F32 = mybir.dt.float32
BF16 = mybir.dt.bfloat16
I32 = mybir.dt.int32
AF = mybir.ActivationFunctionType
ALU = mybir.AluOpType
AX = mybir.AxisListType

D = 1024
S = 2048
DFF = 2816
NFC = DFF // 128
NCORES = 8
COMPUTE = ("pe", "act", "dve", "pool")
NVEC = 57 + 256
NCONST = 128 * 5 + 512


class _Rec:
    def __init__(self):
        self.call = None

    def __getattr__(self, name):
        def f(*a, **k):
            self.call = (name, a, k)
        return f


class Sched:
    def __init__(self, nc, n_dma_sems=24):
        self.nc = nc
        self.ops = []
        self.res = {}
        self.n_dma_sems = n_dma_sems
        self.last = {}
        self.dmas = []
        self.pending = {}

    def _conf(self, key):
        name, sub = key
        d = self.res.setdefault(name, {})
        if sub is None:
            return list(d.values())
        out = []
        if None in d:
            out.append(d[None])
        if sub in d:
            out.append(d[sub])
        return out

    def barrier(self, engs=("pe", "act", "dve", "sp")):
        deps = set(v for k, v in self.last.items() if k in engs)
        deps |= set(d for d in self.dmas if self.ops[d]["eng"] in engs)
        self.dmas = [d for d in self.dmas if self.ops[d]["eng"] not in engs]
        for e in engs:
            self.pending[e] = set(self.pending.get(e, set())) | deps

    def op(self, eng, fn, reads=(), writes=(), dma=False):
        oid = len(self.ops)
        raw, other = set(), set()
        for key in reads:
            for st in self._conf(key):
                if st[0] is not None:
                    raw.add(st[0])
        for key in writes:
            for st in self._conf(key):
                if st[0] is not None:
                    other.add(st[0])
                other.update(st[1])
        for key in reads:
            d = self.res.setdefault(key[0], {})
            if key[1] not in d:
                d[key[1]] = [None, []]
            for st in self._conf(key):
                st[1].append(oid)
        for key in writes:
            d = self.res.setdefault(key[0], {})
            if key[1] not in d:
                d[key[1]] = [None, []]
            for st in self._conf(key):
                st[0] = oid
                st[1] = []
        if eng in self.pending:
            other |= self.pending.pop(eng)
        other -= raw
        raw.discard(oid)
        other.discard(oid)
        rec = _Rec()
        fn(rec)
        cname, ca, ck = rec.call
        self.ops.append(dict(eng=eng, fn=(lambda e: getattr(e, cname)(*ca, **ck)), raw=raw, other=other, dma=dma))
        self.last[eng] = oid
        if dma:
            self.dmas.append(oid)
        return oid

    def defer(self, eng, fn, reads=(), writes=(), dma=False):
        rec = _Rec()
        fn(rec)
        cname, ca, ck = rec.call
        return lambda: self.op(eng, (lambda e: getattr(e, cname)(*ca, **ck)), reads, writes, dma)

    def emit(self):
        nc = self.nc
        ops = self.ops
        need = []
        for o in ops:
            w = set()
            for d in o["raw"]:
                od = ops[d]
                if od["dma"] or od["eng"] != o["eng"] or o["dma"] or o["eng"] != "pe":
                    w.add(d)
            for d in o["other"]:
                od = ops[d]
                if od["dma"] or od["eng"] != o["eng"] or o["dma"] or o["eng"] != "pe":
                    w.add(d)
            need.append(w)
        signal = [False] * len(ops)
        for w in need:
            for d in w:
                signal[d] = True
        sig_idx = [0] * len(ops)
        cnt = {e: 0 for e in COMPUTE}
        dma_rr, dma_sem, slot_val = {}, [None] * len(ops), {}
        for i, o in enumerate(ops):
            if o["dma"]:
                q = o["eng"]
                k = dma_rr.get(q, 0)
                dma_rr[q] = k + 1
                slot = (q, k % self.n_dma_sems)
                v = slot_val.get(slot, 0) + 16
                slot_val[slot] = v
                dma_sem[i] = (slot, v)
            elif signal[i]:
                cnt[o["eng"]] += 1
                sig_idx[i] = cnt[o["eng"]]
        with contextlib.ExitStack() as es:
            sems = {e: es.enter_context(nc.semaphore("s_" + e)) for e in COMPUTE}
            dsems = {}
            for q in dma_rr:
                for k in range(min(self.n_dma_sems, dma_rr[q])):
                    dsems[(q, k)] = es.enter_context(nc.semaphore("d_%s_%d" % (q, k)))
            block = es.enter_context(nc.Block())

            def run(engname, e):
                waited = {}
                for i, o in enumerate(ops):
                    if o["eng"] != engname:
                        continue
                    tgt = {}
                    for d in need[i]:
                        od = ops[d]
                        if od["dma"]:
                            slot, v = dma_sem[d]
                            key = ("d", slot)
                        else:
                            key = ("c", od["eng"])
                            v = sig_idx[d]
                        if v > tgt.get(key, 0):
                            tgt[key] = v
                    if o["dma"]:
                        slot, v = dma_sem[i]
                        if v > 16 and v - 16 > tgt.get(("d", slot), 0):
                            tgt[("d", slot)] = v - 16
                    for key, v in tgt.items():
                        if waited.get(key, 0) >= v:
                            continue
                        waited[key] = v
                        e.wait_ge(dsems[key[1]] if key[0] == "d" else sems[key[1]], v)
                    ins = o["fn"](e)
                    if o["dma"]:
                        ins.then_inc(dsems[dma_sem[i][0]], 16)
                    elif signal[i]:
                        ins.then_inc(sems[engname], 1)
                for slot, v in slot_val.items():
                    if slot[0] == engname and waited.get(("d", slot), 0) < v:
                        e.wait_ge(dsems[slot], v)

            block.tensor(lambda e: run("pe", e))
            block.scalar(lambda e: run("act", e))
            block.vector(lambda e: run("dve", e))
            block.gpsimd(lambda e: run("pool", e))
            block.sync(lambda e: run("sp", e))


def _img(W):
    nk = W.shape[0] // 128
    C = W.shape[1]
    return np.ascontiguousarray(W.reshape(nk, 128, C).transpose(1, 0, 2).reshape(128, nk * C))


def _swapcols(base):
    cols = []
    for hh in range(2):
        b = base + hh * 64
        cols += list(range(b + 8, b + 16)) + list(range(b, b + 8)) + list(range(b + 16, b + 64))
    return cols


def _rng(a, n=128):
    return list(range(a, a + n))


QA, KA, VA = 0, 512, 1024
QB, KB, VB = 1536, 2304, 3072
Q1, Q2, K1, K2 = 3840, 4096, 4352, 4608
VC = 4864
GATE = 5376


def _load_plan():
    plan = []

    def ffn(w):
        for g in range(6):
            plan.append(("f%d_in%d" % (w, g), "ffn_in", (w, g)))
        for q in range(4):
            plan.append(("f%d_out%d" % (w, q), "ffn_out", (w, q)))

    ffn(1)
    plan.append(("a_v", "cols", _rng(VA, 512)))
    cols = []
    for hp in range(4):
        cols += _rng(QA + hp * 128) + _rng(KA + hp * 128)
    plan.append(("a_qk", "cols", cols))
    plan.append(("g_a", "cols", _rng(GATE, 1024)))
    plan.append(("up_a", "up", "a"))
    plan.append(("wo_a", "wo", None))
    plan.append(("b_v", "cols", _rng(VB, 768)))
    for j in range(3):
        cols = []
        for c in (2 * j, 2 * j + 1):
            cols += _rng(QB + c * 128) + _swapcols(QB + c * 128) + _rng(KB + c * 128) + _swapcols(KB + c * 128)
        plan.append(("b_qk%d" % j, "cols", cols))
    plan.append(("g_b", "cols", _rng(GATE + 1024, 1024)))
    plan.append(("up_b", "up", "b"))
    plan.append(("c_v", "cols", _rng(VC, 512)))
    for hp in range(2):
        cols = []
        for base in (Q1, Q2, K1, K2):
            cols += _rng(base + hp * 128) + _swapcols(base + hp * 128)
        plan.append(("c_qk%d" % hp, "cols", cols))
    plan.append(("g_c", "cols", _rng(GATE + 2048, 1024)))
    plan.append(("up_c", "up", "c"))
    ffn(2)
    return plan


def _ffn_in_cols(g):
    ncg = min(4, NFC - 4 * g)
    return ncg, _rng(512 * g, 128 * ncg) + _rng(DFF + 512 * g, 128 * ncg)


def _plan_sizes():
    sizes = {}
    off = 0
    for name, kind, spec in _load_plan():
        if kind == "ffn_in":
            ncg, cols = _ffn_in_cols(spec[1])
            E = 8 * len(cols)
        elif kind == "ffn_out":
            E = NFC * 256
        elif kind == "cols":
            E = 8 * len(spec)
        elif kind == "up":
            E = {"a": 4, "b": 2, "c": 4}[spec] * 1024
        else:
            E = 8192
        sizes[name] = (off, E)
        off += E
    return sizes, off


def _build_wimg(inp, l):
    sizes, tot = _plan_sizes()
    out = np.empty((128, tot), np.float32)
    for name, kind, spec in _load_plan():
        off, E = sizes[name]
        if kind == "ffn_in":
            W = {1: inp["ffn1_w_in"], 2: inp["ffn2_w_in"]}[spec[0]][l]
            _, cols = _ffn_in_cols(spec[1])
            im = _img(W[:, cols])
        elif kind == "ffn_out":
            W = {1: inp["ffn1_w_out"], 2: inp["ffn2_w_out"]}[spec[0]][l]
            im = _img(W[:, spec[1] * 256:(spec[1] + 1) * 256])
        elif kind == "cols":
            im = _img(inp["w_in"][l][:, spec])
        elif kind == "up":
            im = _img({"a": inp["w_up_a"], "b": inp["w_up_b"], "c": inp["w_up_c"]}[spec][l])
        else:
            im = _img(inp["w_out"][l])
        assert im.shape == (128, E), (name, im.shape, E)
        out[:, off:off + E] = im
    return out


def _build_vecs(inp, l):
    v = np.zeros((128, NVEC), np.float32)
    v[:, 0:8] = inp["ffn1_norm"][l].reshape(8, 128).T
    v[:, 8:16] = inp["mix_norm"][l].reshape(8, 128).T
    v[:, 16:24] = inp["ffn2_norm"][l].reshape(8, 128).T
    v[:, 24:32] = inp["final_norm"].reshape(8, 128).T
    v[:, 32:56] = inp["b_gate"][l].reshape(24, 128).T
    v[:, 56] = inp["diff_subln"][l]
    for i, lv in enumerate((inp["lam_q1"], inp["lam_k1"], inp["lam_q2"], inp["lam_k2"])):
        v[:, 57 + 64 * i:57 + 64 * (i + 1)] = np.broadcast_to(lv[l][None, :], (128, 64))
    return v


def _build_consts():
    c = np.zeros((128, NCONST + 2), np.float32)
    r = np.arange(128)[:, None]
    col = np.arange(128)[None, :]
    c[:, 0:128] = 1.0
    c[:, 128:256] = np.where(r >= col, -1.0, 0.0)
    c[:, 256:384] = -1.0
    c[:, 384:512] = np.where(col > r, 1.0, 0.0)
    c[:, 512:640] = np.where(col >= r, 1.0, 0.0)
    col2 = np.arange(256)[None, :]
    c[:, 640:896] = np.where((col2 >= r) & (col2 <= r + 128), 1.0, 0.0)
    c[:, 896:1152] = c[:, 640:896]
    inv_freq = (500000.0 ** (-np.arange(0, 16, 2, dtype=np.float32) / np.float32(16))).astype(np.float32)
    for p in range(128):
        q = p % 64
        if q < 16:
            c[p, NCONST] = inv_freq[q % 8]
            c[p, NCONST + 1] = -1.0 if q < 8 else 1.0
    return c


def build_program(NSEQ=2, DEPTH=2, stop_after=None):
    T = NSEQ * S
    nc = bass.Bass("TRN2", target_bir_lowering=False, dynamic_dma_scratch_size=8192)
    sizes, WTOT = _plan_sizes()
    xin = nc.dram_tensor("xT", [D, T], F32, kind="ExternalInput")
    posd = nc.dram_tensor("pos", [NSEQ, 128, S], I32, kind="ExternalInput")
    wimg = [nc.dram_tensor("wimg%d" % l, [128, WTOT], F32, kind="ExternalInput") for l in range(DEPTH)]
    vecd = [nc.dram_tensor("vecs%d" % l, [128, NVEC], F32, kind="ExternalInput") for l in range(DEPTH)]
    cstd = nc.dram_tensor("consts", [128, NCONST + 2], F32, kind="ExternalInput")
    outd = nc.dram_tensor("outT", [D, T], F32, kind="ExternalOutput")
    xa = nc.dram_tensor("xa", [D, T], F32, kind="Internal")
    hmd = nc.dram_tensor("hmd", [D, T], BF16, kind="Internal")

    sch = Sched(nc)
    op = sch.op
    top = contextlib.ExitStack()
    with top:
        uniq = {"n": 0}

        def sb(es, name, shape, dt):
            uniq["n"] += 1
            return es.enter_context(nc.sbuf_tensor("%s_%d" % (name, uniq["n"]), shape, dt))

        WB = [sb(top, "wb%d" % i, [128, 8192], BF16) for i in range(4)]
        pst = top.enter_context(nc.psum_tensor("pst", [128, 4096], F32))
        PS = [pst[:, i * 512:(i + 1) * 512] for i in range(8)]
        cb = sb(top, "cb", [128, NCONST], BF16)
        cf = sb(top, "cf", [128, 2], F32)
        vec = [sb(top, "vec%d" % l, [128, NVEC], F32) for l in range(DEPTH)]
        ones = cb[:, 0:128]
        trineg = cb[:, 128:256]
        negones = cb[:, 256:384]
        mstrict = cb[:, 384:512]
        mincl = cb[:, 512:640]
        mband = cb[:, 640:896]
        mband2 = cb[:, 640:1152].rearrange("p (h c) -> p h c", h=2)

        op("pool", lambda e: e.dma_start(out=cb[:], in_=cstd.ap()[:, 0:NCONST]), writes=[("cb", None)], dma=True)
        op("sp", lambda e: e.dma_start(out=cf[:], in_=cstd.ap()[:, NCONST:NCONST + 2]), writes=[("cf", None)], dma=True)
        for l in range(DEPTH):
            op("sp", lambda e, l=l: e.dma_start(out=vec[l][:], in_=vecd[l].ap()), writes=[("vec", l)], dma=True)

        wstate = {"n": 0}

        def wload(l, name):
            off, E = sizes[name]
            b = wstate["n"] % 4
            wstate["n"] += 1
            op("pool", lambda e: e.dma_start(out=WB[b][:, 0:E], in_=wimg[l].ap()[:, off:off + E]),
               writes=[("wb", b)], dma=True)
            return WB[b], ("wb", b)

        psrr = {"n": 0}

        def psnext():
            b = psrr["n"] % 8
            psrr["n"] += 1
            return PS[b], ("ps", b)

        def rms_rstd(es_name, src_chunks, src_keys, rstd_ap, rstd_key, sqt, width, inv_n, eps, lnt):
            ps, pk = psnext()
            n = len(src_chunks)
            for i, (a, k) in enumerate(zip(src_chunks, src_keys)):
                sq = sqt[i % 2]
                sk = ("sq" + es_name, i % 2)
                if i % 2 == 0:
                    op("act", lambda e, a=a, sq=sq: e.activation(out=sq[:, 0:width], in_=a, func=AF.Square),
                       reads=[k], writes=[sk])
                else:
                    op("dve", lambda e, a=a, sq=sq: e.tensor_tensor(out=sq[:, 0:width], in0=a, in1=a, op=ALU.mult),
                       reads=[k], writes=[sk])
                op("pe", lambda e, sq=sq, i=i: e.matmul(ps[:, 0:width], lhsT=ones, rhs=sq[:, 0:width],
                                                        start=(i == 0), stop=(i == n - 1)),
                   reads=[sk, ("cb", None)], writes=[pk])
            lk = ("ln" + es_name, None)
            op("act", lambda e: e.activation(out=lnt[:, 0:width], in_=ps[:, 0:width], func=AF.Ln, scale=inv_n, bias=eps),
               reads=[pk], writes=[lk])
            op("act", lambda e: e.activation(out=rstd_ap, in_=lnt[:, 0:width], func=AF.Exp, scale=-0.5),
               reads=[lk], writes=[rstd_key])

        def ffn_phase(l, w, src, final=False, emit_hm=False):
            gcol = 0 if w == 1 else 16
            TT = 1024
            NT = T // TT
            with contextlib.ExitStack() as es:
                xt = sb(es, "xt", [128, 8, TT], F32)
                xst = sb(es, "xst", [128, 8, 512], F32)
                hts = [sb(es, "ht%d" % i, [128, 8, TT], BF16) for i in range(2)]
                act = sb(es, "act", [128, NFC, TT], BF16)
                sqt = [sb(es, "sq%d" % i, [128, 512], BF16) for i in range(8)]
                lnt = sb(es, "lnt", [128, 512], F32)
                rstd = sb(es, "rstd", [128, 512], F32)
                sil = [sb(es, "sil%d" % i, [128, 512], F32) for i in range(2)]
                if emit_hm:
                    hst = sb(es, "hst", [128, 4, 512], BF16)

                def pre_load(tt, h):
                    c0 = tt * TT + h * 512
                    for kc in range(8):
                        op("sp", lambda e, kc=kc: e.dma_start(out=xst[:, kc, :], in_=src.ap()[kc * 128:(kc + 1) * 128, c0:c0 + 512]),
                           reads=[("xdram", (kc, tt))], writes=[("xst", kc)], dma=True)
                    for kc in range(8):
                        if kc % 2 == 0:
                            op("act", lambda e, kc=kc: e.activation(out=sqt[kc][:], in_=xst[:, kc, :], func=AF.Square),
                               reads=[("xst", kc)], writes=[("sq", kc)])
                        else:
                            op("dve", lambda e, kc=kc: e.tensor_tensor(out=sqt[kc][:], in0=xst[:, kc, :], in1=xst[:, kc, :], op=ALU.mult),
                               reads=[("xst", kc)], writes=[("sq", kc)])

                def pre_norm(tt, h):
                    ps, pk = psnext()
                    for kc in range(8):
                        op("pe", lambda e, kc=kc: e.matmul(ps[:], lhsT=ones, rhs=sqt[kc][:], start=(kc == 0), stop=(kc == 7)),
                           reads=[("sq", kc), ("cb", None)], writes=[pk])
                    op("act", lambda e: e.activation(out=lnt[:], in_=ps[:], func=AF.Ln, scale=1.0 / D, bias=1e-6),
                       reads=[pk], writes=[("lnt", None)])
                    op("act", lambda e: e.activation(out=rstd[:], in_=lnt[:], func=AF.Exp, scale=-0.5),
                       reads=[("lnt", None)], writes=[("rstd", None)])
                    hb_ = hts[tt % 2]
                    for kc in range(8):
                        op("dve", lambda e, kc=kc: e.scalar_tensor_tensor(
                            out=hb_[:, kc, h * 512:(h + 1) * 512], in0=xst[:, kc, :],
                            scalar=vec[l][:, gcol + kc:gcol + kc + 1], in1=rstd[:], op0=ALU.mult, op1=ALU.mult),
                           reads=[("xst", kc), ("rstd", None), ("vec", l)], writes=[("ht", (tt % 2, kc, h))])

                for h in range(2):
                    pre_load(0, h)
                    pre_norm(0, h)
                pending = []
                for tt in range(NT):
                    c0 = tt * TT
                    ht = hts[tt % 2]
                    si = 0
                    for g in range(6):
                        if g == 1 and pending:
                            pending.pop(0)()
                        if g == 2:
                            while pending:
                                pending.pop(0)()
                            for kc in range(8):
                                op("sp", lambda e, kc=kc, c0=c0: e.dma_start(out=xt[:, kc, :], in_=src.ap()[kc * 128:(kc + 1) * 128, c0:c0 + TT]),
                                   reads=[("xdram", (kc, tt))], writes=[("xt", kc)], dma=True)
                        wb, wk = wload(l, "f%d_in%d" % (w, g))
                        ncg = min(4, NFC - 4 * g)
                        C = 2 * 128 * ncg
                        for j in range(ncg):
                            fc = 4 * g + j
                            for h in range(2):
                                pg, pgk = psnext()
                                pu, puk = psnext()
                                for kc in range(8):
                                    op("pe", lambda e, kc=kc: e.matmul(
                                        pg[:], lhsT=wb[:, kc * C + j * 128:kc * C + (j + 1) * 128],
                                        rhs=ht[:, kc, h * 512:(h + 1) * 512], start=(kc == 0), stop=(kc == 7)),
                                       reads=[wk, ("ht", (tt % 2, kc, h))], writes=[pgk])
                                for kc in range(8):
                                    op("pe", lambda e, kc=kc: e.matmul(
                                        pu[:], lhsT=wb[:, kc * C + (ncg + j) * 128:kc * C + (ncg + j + 1) * 128],
                                        rhs=ht[:, kc, h * 512:(h + 1) * 512], start=(kc == 0), stop=(kc == 7)),
                                       reads=[wk, ("ht", (tt % 2, kc, h))], writes=[puk])
                                st = sil[si % 2]
                                stk = ("sil", si % 2)
                                si += 1
                                op("act", lambda e: e.activation(out=st[:], in_=pg[:], func=AF.Silu), reads=[pgk], writes=[stk])
                                op("dve", lambda e: e.tensor_tensor(out=act[:, fc, h * 512:(h + 1) * 512], in0=pu[:], in1=st[:], op=ALU.mult),
                                   reads=[puk, stk], writes=[("act", (fc, h))])
                    if tt + 1 < NT:
                        pre_load(tt + 1, 0)
                    for q in range(4):
                        wb, wk = wload(l, "f%d_out%d" % (w, q))
                        for jj in range(2):
                            dc = 2 * q + jj
                            for h in range(2):
                                py, pyk = psnext()
                                for fc in range(NFC):
                                    op("pe", lambda e, fc=fc: e.matmul(
                                        py[:], lhsT=wb[:, fc * 256 + jj * 128:fc * 256 + (jj + 1) * 128],
                                        rhs=act[:, fc, h * 512:(h + 1) * 512], start=(fc == 0), stop=(fc == NFC - 1)),
                                       reads=[wk, ("act", (fc, h))], writes=[pyk])
                                op("dve", lambda e: e.scalar_tensor_tensor(
                                    out=xt[:, dc, h * 512:(h + 1) * 512], in0=py[:], scalar=0.5,
                                    in1=xt[:, dc, h * 512:(h + 1) * 512], op0=ALU.mult, op1=ALU.add),
                                   reads=[pyk, ("xt", dc)], writes=[("xt", dc)])
                        if tt + 1 < NT:
                            if q == 0:
                                pre_norm(tt + 1, 0)
                                pre_load(tt + 1, 1)
                            elif q == 1:
                                pre_norm(tt + 1, 1)
                    if not final:
                        for kc in range(8):
                            op("sp", lambda e, kc=kc, c0=c0: e.dma_start(out=xa.ap()[kc * 128:(kc + 1) * 128, c0:c0 + TT], in_=xt[:, kc, :]),
                               reads=[("xt", kc)], writes=[("xdram", (kc, tt))], dma=True)
                        if emit_hm:
                            def epilogue(h, tt=tt, c0=c0):
                                if True:
                                    ps, pk = psnext()
                                    for kc in range(8):
                                        if kc % 2 == 0:
                                            op("act", lambda e, kc=kc: e.activation(out=sqt[kc][:], in_=xt[:, kc, h * 512:(h + 1) * 512], func=AF.Square),
                                               reads=[("xt", kc)], writes=[("sq", kc)])
                                        else:
                                            op("dve", lambda e, kc=kc: e.tensor_tensor(out=sqt[kc][:], in0=xt[:, kc, h * 512:(h + 1) * 512],
                                                                                       in1=xt[:, kc, h * 512:(h + 1) * 512], op=ALU.mult),
                                               reads=[("xt", kc)], writes=[("sq", kc)])
                                        op("pe", lambda e, kc=kc: e.matmul(ps[:], lhsT=ones, rhs=sqt[kc][:], start=(kc == 0), stop=(kc == 7)),
                                           reads=[("sq", kc), ("cb", None)], writes=[pk])
                                    op("act", lambda e: e.activation(out=sil[0][:], in_=ps[:], func=AF.Ln, scale=1.0 / D, bias=1e-6),
                                       reads=[pk], writes=[("sil", 0)])
                                    op("act", lambda e: e.activation(out=sil[1][:], in_=sil[0][:], func=AF.Exp, scale=-0.5),
                                       reads=[("sil", 0)], writes=[("sil", 1)])
                                    for kc in range(8):
                                        op("dve", lambda e, kc=kc: e.scalar_tensor_tensor(
                                            out=hst[:, kc % 4, :], in0=xt[:, kc, h * 512:(h + 1) * 512],
                                            scalar=vec[l][:, 8 + kc:9 + kc], in1=sil[1][:], op0=ALU.mult, op1=ALU.mult),
                                           reads=[("xt", kc), ("sil", 1), ("vec", l)], writes=[("hst", kc % 4)])
                                        op("sp", lambda e, kc=kc: e.dma_start(
                                            out=hmd.ap()[kc * 128:(kc + 1) * 128, c0 + h * 512:c0 + (h + 1) * 512], in_=hst[:, kc % 4, :]),
                                           reads=[("hst", kc % 4)], writes=[("hdram", (kc, tt, h))], dma=True)
                            pending.append(lambda ep=epilogue: ep(0))
                            pending.append(lambda ep=epilogue: ep(1))
                    else:
                        for h in range(2):
                            ps, pk = psnext()
                            for kc in range(8):
                                op("dve", lambda e, kc=kc: e.tensor_tensor(out=sqt[kc][:], in0=xt[:, kc, h * 512:(h + 1) * 512],
                                                                           in1=xt[:, kc, h * 512:(h + 1) * 512], op=ALU.mult),
                                   reads=[("xt", kc)], writes=[("sq", kc)])
                                op("pe", lambda e, kc=kc: e.matmul(ps[:], lhsT=ones, rhs=sqt[kc][:], start=(kc == 0), stop=(kc == 7)),
                                   reads=[("sq", kc), ("cb", None)], writes=[pk])
                            op("act", lambda e: e.activation(out=sil[0][:], in_=ps[:], func=AF.Ln, scale=1.0 / D, bias=1e-6),
                               reads=[pk], writes=[("sil", 0)])
                            op("act", lambda e: e.activation(out=sil[1][:], in_=sil[0][:], func=AF.Exp, scale=-0.5),
                               reads=[("sil", 0)], writes=[("sil", 1)])
                            for kc in range(8):
                                op("dve", lambda e, kc=kc: e.scalar_tensor_tensor(
                                    out=xt[:, kc, h * 512:(h + 1) * 512], in0=xt[:, kc, h * 512:(h + 1) * 512],
                                    scalar=vec[l][:, 24 + kc:25 + kc], in1=sil[1][:], op0=ALU.mult, op1=ALU.mult),
                                   reads=[("xt", kc), ("sil", 1), ("vec", l)], writes=[("xt", kc)])
                        for kc in range(8):
                            op("sp", lambda e, kc=kc, c0=c0: e.dma_start(out=outd.ap()[kc * 128:(kc + 1) * 128, c0:c0 + TT], in_=xt[:, kc, :]),
                               reads=[("xt", kc)], writes=[("odram", (kc, tt))], dma=True)
                while pending:
                    pending.pop(0)()
                sch.barrier()

        def evac(i, out_ap, in_ap, reads, writes, scale=None):
            if i % 2 == 0:
                if scale is None:
                    op("act", lambda e: e.copy(out=out_ap, in_=in_ap), reads=reads, writes=writes)
                else:
                    op("act", lambda e: e.mul(out=out_ap, in_=in_ap, mul=scale), reads=reads, writes=writes)
            else:
                if scale is None:
                    op("dve", lambda e: e.tensor_copy(out=out_ap, in_=in_ap), reads=reads, writes=writes)
                else:
                    op("dve", lambda e: e.tensor_scalar(out=out_ap, in0=in_ap, scalar1=scale, scalar2=None, op0=ALU.mult),
                       reads=reads, writes=writes)

        def mixer_phase(l, s, parts="abc"):
            c0s = s * S
            lam_init = 0.8 - 0.6 * math.exp(-0.3 * l)
            with contextlib.ExitStack() as es:
                ht = sb(es, "mht", [128, 8, S], BF16)
                cosF = sb(es, "cosF", [128, S], F32)
                sinF = sb(es, "sinF", [128, S], F32)
                lamv = sb(es, "lamv", [128, 8], F32)
                with contextlib.ExitStack() as es2:
                    rstd = sb(es2, "mrstd", [128, 512], F32)
                    for kc in range(8):
                        op("sp", lambda e, kc=kc: e.dma_start(out=ht[:, kc, :], in_=hmd.ap()[kc * 128:(kc + 1) * 128, c0s:c0s + S]),
                           reads=[("hdram", None)], writes=[("mht", (kc, tt)) for tt in range(4)], dma=True)
                    lt = rstd
                    for i in range(2):
                        op("dve", lambda e, i=i: e.tensor_tensor(out=lt[:, i * 64:(i + 1) * 64], in0=vec[l][:, 57 + 128 * i:57 + 128 * i + 64],
                                                                 in1=vec[l][:, 57 + 128 * i + 64:57 + 128 * i + 128], op=ALU.mult),
                           reads=[("vec", l), ("mrstd", None)], writes=[("mrstd", None)])
                        op("dve", lambda e, i=i: e.reduce_sum(out=lamv[:, 2 + i:3 + i], in_=lt[:, i * 64:(i + 1) * 64], axis=AX.X),
                           reads=[("mrstd", None)], writes=[("lamv", None)])
                    op("act", lambda e: e.activation(out=lamv[:, 4:6], in_=lamv[:, 2:4], func=AF.Exp),
                       reads=[("lamv", None)], writes=[("lamv", None)])
                    op("dve", lambda e: e.tensor_tensor(out=lamv[:, 0:1], in0=lamv[:, 5:6], in1=lamv[:, 4:5], op=ALU.subtract),
                       reads=[("lamv", None)], writes=[("lamv", None)])
                    op("dve", lambda e: e.tensor_scalar(out=lamv[:, 0:1], in0=lamv[:, 0:1], scalar1=-lam_init, scalar2=None, op0=ALU.add),
                       reads=[("lamv", None)], writes=[("lamv", None)])
                    op("dve", lambda e: e.tensor_scalar(out=lamv[:, 1:2], in0=vec[l][:, 56:57], scalar1=1.0 - lam_init, scalar2=None, op0=ALU.mult),
                       reads=[("lamv", None), ("vec", l)], writes=[("lamv", None)])
                sch.barrier()

                HS = S // 2
                tabs = {}

                def tables_alloc(scope):
                    tabs["tP"] = sb(scope, "tP", [128, HS], I32)
                    tabs["tT"] = [sb(scope, "tT%d" % i, [128, HS], F32) for i in range(2)]
                    tabs["tK"] = sb(scope, "tK", [128, HS], I32)
                C1 = 6.28125
                C2 = 2.0 * math.pi - 6.28125
                tab_jobs = []

                def tables_pool(hf):
                    cs = slice(hf * HS, (hf + 1) * HS)
                    tP, tT, tK = tabs["tP"], tabs["tT"], tabs["tK"]
                    tPf = tP[:].bitcast(F32)
                    tKf = tK[:].bitcast(F32)
                    tq.append(sch.defer("sp", lambda e: e.dma_start(out=tP[:], in_=posd.ap()[s][:, cs]), writes=[("tP", None)], dma=True))
                    tq.append(sch.defer("dve", lambda e: e.tensor_copy(out=tPf, in_=tP[:]), reads=[("tP", None)], writes=[("tP", None)]))
                    tq.append(sch.defer("dve", lambda e: e.tensor_scalar(out=tPf, in0=tPf, scalar1=cf[:, 0:1], scalar2=None, op0=ALU.mult),
                       reads=[("tP", None), ("cf", None)], writes=[("tP", None)]))
                    for wi, (shift, dst, dk, signed) in enumerate(((0.0, sinF, "tabs", True), (math.pi / 2, cosF, "tabc", False))):
                        t1 = tT[wi]
                        tk = ("tT", wi)
                        tq.append(sch.defer("dve", lambda e: e.tensor_scalar(out=t1[:], in0=tPf, scalar1=shift, scalar2=None, op0=ALU.add),
                           reads=[("tP", None)], writes=[tk]))
                        tq.append(sch.defer("dve", lambda e: e.tensor_scalar(out=tKf, in0=t1[:], scalar1=1.0 / (2 * math.pi), scalar2=None, op0=ALU.mult),
                           reads=[tk], writes=[("tK", None)]))
                        tq.append(sch.defer("dve", lambda e: e.tensor_copy(out=tK[:], in_=tKf), reads=[("tK", None)], writes=[("tK", None)]))
                        tq.append(sch.defer("dve", lambda e: e.tensor_copy(out=tKf, in_=tK[:]), reads=[("tK", None)], writes=[("tK", None)]))
                        tq.append(sch.defer("dve", lambda e: e.tensor_scalar(out=tKf, in0=tKf, scalar1=-C1, scalar2=None, op0=ALU.mult),
                           reads=[("tK", None)], writes=[("tK", None)]))
                        tq.append(sch.defer("dve", lambda e: e.tensor_tensor(out=t1[:], in0=t1[:], in1=tKf, op=ALU.add),
                           reads=[("tK", None), tk], writes=[tk]))
                        tq.append(sch.defer("dve", lambda e: e.tensor_scalar(out=tKf, in0=tKf, scalar1=C2 / C1, scalar2=None, op0=ALU.mult),
                           reads=[("tK", None)], writes=[("tK", None)]))
                        tq.append(sch.defer("dve", lambda e: e.tensor_tensor(out=t1[:], in0=t1[:], in1=tKf, op=ALU.add),
                           reads=[("tK", None), tk], writes=[tk]))
                        tq.append(sch.defer("dve", lambda e: e.tensor_scalar(out=tKf, in0=t1[:], scalar1=math.pi, scalar2=-2 * math.pi, op0=ALU.is_gt, op1=ALU.mult),
                           reads=[tk], writes=[("tK", None)]))
                        tq.append(sch.defer("dve", lambda e: e.tensor_tensor(out=t1[:], in0=t1[:], in1=tKf, op=ALU.add),
                           reads=[("tK", None), tk], writes=[tk]))
                        tq.append(sch.defer("dve", lambda e: e.tensor_scalar(out=tKf, in0=t1[:], scalar1=-math.pi, scalar2=2 * math.pi, op0=ALU.is_lt, op1=ALU.mult),
                           reads=[tk], writes=[("tK", None)]))
                        tq.append(sch.defer("dve", lambda e: e.tensor_tensor(out=t1[:], in0=t1[:], in1=tKf, op=ALU.add),
                           reads=[("tK", None), tk], writes=[tk]))
                        tq.append(sch.defer("dve", lambda e: e.tensor_scalar(out=t1[:], in0=t1[:], scalar1=-3.14159, scalar2=3.14159, op0=ALU.max, op1=ALU.min),
                           reads=[tk], writes=[tk]))
                        if signed:
                            tq.append(sch.defer("dve", lambda e: e.tensor_scalar(out=t1[:], in0=t1[:], scalar1=cf[:, 1:2], scalar2=None, op0=ALU.mult),
                               reads=[tk, ("cf", None)], writes=[tk]))

                def tables_act(hf):
                    cs = slice(hf * HS, (hf + 1) * HS)
                    tT = tabs["tT"]
                    tq.append(sch.defer("act", lambda e: e.activation(out=sinF[:, cs], in_=tT[0][:], func=AF.Sin), reads=[("tT", 0)], writes=[("tabs", None)]))
                    tq.append(sch.defer("act", lambda e: e.activation(out=cosF[:, cs], in_=tT[1][:], func=AF.Sin), reads=[("tT", 1)], writes=[("tabc", None)]))

                tstate = {"n": 0}
                tq = []

                def tables_drain(n=None):
                    k = 0
                    while tq and (n is None or k < n):
                        tq.pop(0)()
                        k += 1

                def tables_step():
                    k = tstate["n"]
                    tstate["n"] += 1
                    if k == 0:
                        tables_pool(0)
                    elif k == 1:
                        tables_act(0)
                        tables_pool(1)
                    elif k == 2:
                        tables_act(1)

                def tables_finish():
                    while tstate["n"] < 3:
                        tables_step()
                    tables_drain()

                def proj_fm(wb, wk, C, cidx, dst_fn, key, evi=[0]):
                    for tt in range(4):
                        ps, pk = psnext()
                        for kc in range(8):
                            op("pe", lambda e, kc=kc, tt=tt, ps=ps: e.matmul(
                                ps[:], lhsT=wb[:, kc * C + cidx * 128:kc * C + (cidx + 1) * 128],
                                rhs=ht[:, kc, tt * 512:(tt + 1) * 512], start=(kc == 0), stop=(kc == 7)),
                               reads=[wk, ("mht", (kc, tt))], writes=[pk])
                        dst_fn(tt, ps, pk)

                def proj_v(wb, wk, C, col0, ncols, vt, vkey, stride=1, tokfn=None):
                    for blk in range(16):
                        ps, pk = psnext()
                        tk = tokfn(blk)
                        for kc in range(8):
                            op("pe", lambda e, kc=kc, ps=ps, tk=tk: e.matmul(
                                ps[:, 0:ncols], lhsT=ht[:, kc, tk], rhs=wb[:, kc * C + col0:kc * C + col0 + ncols],
                                start=(kc == 0), stop=(kc == 7)),
                               reads=[wk, (("mht", (kc, blk // 4)) if stride == 1 else ("mht", None))], writes=[pk])
                        evac(blk, vt[:, blk, 0:ncols], ps[:, 0:ncols], [pk], [(vkey, blk)])

                rope_n = {"n": 0}

                def rope_evac(psP, pkP, psS, pkS, dst_ap, dkey, tt, rt, perm=None):
                    ri = rope_n["n"] % 2
                    rope_n["n"] += 1
                    a, b = rt[ri]
                    cs = slice(tt * 512, (tt + 1) * 512)
                    op("dve", lambda e: e.tensor_tensor(out=a[:], in0=psP[:], in1=cosF[:, cs], op=ALU.mult),
                       reads=[pkP, ("tabc", None)], writes=[("ra", ri)])
                    op("dve", lambda e: e.tensor_tensor(out=b[:], in0=psS[:], in1=sinF[:, cs], op=ALU.mult),
                       reads=[pkS, ("tabs", None)], writes=[("rb", ri)])
                    if perm is None:
                        op("dve", lambda e: e.tensor_tensor(out=dst_ap, in0=a[:], in1=b[:], op=ALU.add),
                           reads=[("ra", ri), ("rb", ri)], writes=[dkey])
                    else:
                        d = perm
                        op("dve", lambda e: e.tensor_tensor(out=dst_ap, in0=a[:].rearrange("p (m c) -> p c m", c=d),
                                                             in1=b[:].rearrange("p (m c) -> p c m", c=d), op=ALU.add),
                           reads=[("ra", ri), ("rb", ri)], writes=[dkey])

                def merge_all(branches):
                    with contextlib.ExitStack() as es3:
                        macc = sb(es3, "macc", [128, 8, S], BF16)
                        xt1 = sb(es3, "gx", [128, 8, 512], F32)
                        sg = [sb(es3, "gs%d" % i, [128, 512], F32) for i in range(2)]
                        pr = [sb(es3, "gp%d" % i, [128, 512], BF16) for i in range(2)]
                        n = 0
                        for bidx, (b, ychunks, ykey) in enumerate(branches):
                            nk = len(ychunks)
                            bi = "abc".index(b)
                            wg, wgk = wload(l, "g_" + b)
                            wu, wuk = wload(l, "up_" + b)
                            for tt in range(4):
                                cs = slice(tt * 512, (tt + 1) * 512)
                                for dc in range(8):
                                    pg, pgk = psnext()
                                    for kc in range(8):
                                        op("pe", lambda e, kc=kc: e.matmul(
                                            pg[:], lhsT=wg[:, kc * 1024 + dc * 128:kc * 1024 + (dc + 1) * 128],
                                            rhs=ht[:, kc, cs], start=(kc == 0), stop=(kc == 7)),
                                           reads=[wgk, ("mht", None)], writes=[pgk])
                                    sgt = sg[n % 2]
                                    prt = pr[n % 2]
                                    sgk = ("gs", n % 2)
                                    prk = ("gp", n % 2)
                                    n += 1
                                    op("act", lambda e: e.activation(out=sgt[:], in_=pg[:], func=AF.Sigmoid,
                                                                     bias=vec[l][:, 32 + bi * 8 + dc:33 + bi * 8 + dc]),
                                       reads=[pgk, ("vec", l)], writes=[sgk])
                                    pu, puk = psnext()
                                    for j in range(nk):
                                        op("pe", lambda e, j=j: e.matmul(
                                            pu[:], lhsT=wu[:, j * 1024 + dc * 128:j * 1024 + (dc + 1) * 128],
                                            rhs=ychunks[j][:, cs], start=(j == 0), stop=(j == nk - 1)),
                                           reads=[wuk, ykey], writes=[puk])
                                    if bidx == 0:
                                        op("dve", lambda e: e.tensor_tensor(out=macc[:, dc, cs], in0=pu[:], in1=sgt[:], op=ALU.mult),
                                           reads=[puk, sgk], writes=[("macc", (dc, tt))])
                                    else:
                                        op("dve", lambda e: e.tensor_tensor(out=prt[:], in0=pu[:], in1=sgt[:], op=ALU.mult),
                                           reads=[puk, sgk], writes=[prk])
                                        op("dve", lambda e: e.tensor_tensor(out=macc[:, dc, cs], in0=macc[:, dc, cs], in1=prt[:], op=ALU.add),
                                           reads=[prk, ("macc", (dc, tt))], writes=[("macc", (dc, tt))])
                        wo, wok = wload(l, "wo_a")
                        for tt in range(4):
                            cs = slice(tt * 512, (tt + 1) * 512)
                            for kc in range(8):
                                op("sp", lambda e, kc=kc: e.dma_start(
                                    out=xt1[:, kc, :], in_=xa.ap()[kc * 128:(kc + 1) * 128, c0s + tt * 512:c0s + (tt + 1) * 512]),
                                   reads=[("xdram", (kc, s, tt))], writes=[("gx", kc)], dma=True)
                            for dc2 in range(8):
                                pz, pzk = psnext()
                                for dc in range(8):
                                    op("pe", lambda e, dc=dc: e.matmul(
                                        pz[:], lhsT=wo[:, dc * 1024 + dc2 * 128:dc * 1024 + (dc2 + 1) * 128],
                                        rhs=macc[:, dc, cs], start=(dc == 0), stop=(dc == 7)),
                                       reads=[wok, ("macc", (dc, tt))], writes=[pzk])
                                op("dve", lambda e: e.tensor_tensor(out=xt1[:, dc2, :], in0=pz[:], in1=xt1[:, dc2, :], op=ALU.add),
                                   reads=[pzk, ("gx", dc2)], writes=[("gx", dc2)])
                                op("sp", lambda e: e.dma_start(
                                    out=xa.ap()[dc2 * 128:(dc2 + 1) * 128, c0s + tt * 512:c0s + (tt + 1) * 512], in_=xt1[:, dc2, :]),
                                   reads=[("gx", dc2)], writes=[("xdram", (dc2, s, tt))], dma=True)
                    sch.barrier()

                if "a" in parts:
                    if True:
                        ya = sb(es, "ya", [128, 4, S], BF16)
                        with contextlib.ExitStack() as es3:
                            tables_alloc(es3)
                            va = sb(es3, "va", [128, 16, 512], BF16)
                            qts = [sb(es3, "aq%d" % i, [128, S], BF16) for i in range(2)]
                            kts = [sb(es3, "ak%d" % i, [128, S], BF16) for i in range(2)]
                            et = [sb(es3, "ae%d" % i, [128, 2, 512], F32) for i in range(2)]
                            spt = [sb(es3, "asp%d" % i, [128, 2, 512], BF16) for i in range(3)]
                            wt = [sb(es3, "aw%d" % i, [128, 2, 512], BF16) for i in range(3)]
                            Rt = [sb(es3, "aR%d" % i, [128, 2, 512], BF16) for i in range(4)]
                            wb, wk = wload(l, "a_v")
                            proj_v(wb, wk, 512, 0, 512, va, "va", tokfn=lambda blk: slice(blk * 128, (blk + 1) * 128))
                            wb, wk = wload(l, "a_qk")
                            ZB = [0, 2, 4]
                            OB = [6, 7]
                            def proj_items(hp):
                                return [dict(kind="proj", hp=hp, tt=tt, which=which) for tt in range(4) for which in range(2)]

                            def step_items(hp, gi0):
                                out = []
                                gi = gi0
                                for qc in range(4):
                                    kbs = list(range(4 * qc + 3, -1, -1))
                                    n = len(kbs)

                                    def rng_of(kb, qc=qc):
                                        lo = 128 * (kb - 4 * qc) if kb >= 4 * qc else 0
                                        return lo, 512 - lo

                                    for i, kb in enumerate(kbs):
                                        lo, w = rng_of(kb)
                                        nxt = rng_of(kbs[i + 1]) if i + 1 < n else None
                                        out.append(dict(kind="step", hp=hp, qc=qc, i=i, n=n, kb=kb, lo=lo, w=w, nxt=nxt,
                                                        diag=(kb >= 4 * qc), gi=gi))
                                    gi += 1
                                return out, gi

                            items = proj_items(0)
                            gi = 0
                            for hp in range(4):
                                st, gi = step_items(hp, gi)
                                pj = proj_items(hp + 1) if hp + 1 < 4 else []
                                for k, it in enumerate(st):
                                    items.append(it)
                                    if pj and k % 4 == 3:
                                        items.append(pj.pop(0))
                                items.extend(pj)
                            prev_real = None
                            for g_, it in enumerate(items):
                                it["g"] = g_
                                if it["kind"] == "step":
                                    it["gprev"] = prev_real["g"] if (prev_real is not None and it["i"] > 0) else None
                                    if prev_real is not None and it["i"] > 0:
                                        prev_real["gnext"] = g_
                                    it["gnext"] = None
                                    prev_real = it
                            NI = len(items)

                            def zview(g_, w):
                                zb = ZB[g_ % 3]
                                return pst[:, zb * 512:(zb + 2) * 512].rearrange("p (h c) -> p h c", h=2)[:, :, 0:w]

                            def stA(g_):
                                it = items[g_]
                                zb = ZB[g_ % 3]
                                if it["kind"] == "proj":
                                    hp, tt, which = it["hp"], it["tt"], it["which"]
                                    dst = (qts if which == 0 else kts)[hp % 2]
                                    dk = "aq" if which == 0 else "ak"
                                    ps = PS[zb]
                                    for kc in range(8):
                                        op("pe", lambda e, kc=kc: e.matmul(
                                            ps[:], lhsT=wb[:, kc * 1024 + (2 * hp + which) * 128:kc * 1024 + (2 * hp + which + 1) * 128],
                                            rhs=ht[:, kc, tt * 512:(tt + 1) * 512], start=(kc == 0), stop=(kc == 7)),
                                           reads=[wk, ("mht", None)], writes=[("ps", zb)])
                                    evac(1, dst[:, tt * 512:(tt + 1) * 512], ps[:], [("ps", zb)], [(dk, (hp % 2, tt))],
                                         scale=(0.125 if which == 0 else None))
                                    return
                                hp, qc, i, n, kb, lo, w = it["hp"], it["qc"], it["i"], it["n"], it["kb"], it["lo"], it["w"]
                                qt, kt = qts[hp % 2], kts[hp % 2]
                                if g_ >= 12:
                                    tables_drain(1)
                                for hh in range(2):
                                    hb = slice(64 * hh, 64 * hh + 64)
                                    op("pe", lambda e: e.matmul(PS[zb + hh][:, 0:w], lhsT=kt[hb, kb * 128:(kb + 1) * 128],
                                                                rhs=qt[hb, qc * 512 + lo:qc * 512 + 512], start=True, stop=True),
                                       reads=[("ak", (hp % 2, kb // 4)), ("aq", (hp % 2, qc))], writes=[("ps", zb + hh)])
                                ee = et[g_ % 2]
                                sp = spt[g_ % 3]
                                op("act", lambda e: e.activation(out=ee[:, :, 0:w], in_=zview(g_, w), func=AF.Exp),
                                   reads=[("ps", zb), ("ps", zb + 1)], writes=[("ae", g_ % 2)])
                                op("act", lambda e: e.activation(out=sp[:, :, 0:w], in_=ee[:, :, 0:w], func=AF.Ln, bias=1.0),
                                   reads=[("ae", g_ % 2)], writes=[("asp", g_ % 3)])
                                if it["diag"]:
                                    for hh in range(2):
                                        op("dve", lambda e: e.tensor_tensor(out=sp[:, hh, 0:128], in0=sp[:, hh, 0:128], in1=mstrict, op=ALU.mult),
                                           reads=[("asp", g_ % 3), ("cb", None)], writes=[("asp", g_ % 3)])
                                if it["nxt"] is not None:
                                    lo2, w2 = it["nxt"]
                                    gn = it["gnext"]
                                    Rn = Rt[gn % 4]
                                    Rp = Rt[g_ % 4]
                                    if lo2 < lo:
                                        op("dve", lambda e: e.memset(Rn[:, :, 0:lo - lo2], 0.0), writes=[("aR", gn % 4)])
                                    if i == 0:
                                        op("dve", lambda e: e.tensor_copy(out=Rn[:, :, lo - lo2:w2], in_=sp[:, :, 0:w]),
                                           reads=[("asp", g_ % 3)], writes=[("aR", gn % 4)])
                                    else:
                                        op("dve", lambda e: e.tensor_tensor(out=Rn[:, :, lo - lo2:w2], in0=Rp[:, :, 0:w], in1=sp[:, :, 0:w], op=ALU.add),
                                           reads=[("asp", g_ % 3), ("aR", g_ % 4)], writes=[("aR", gn % 4)])

                            def stC(g_):
                                it = items[g_]
                                if it["kind"] == "proj":
                                    return
                                i, kb, lo, w = it["i"], it["kb"], it["lo"], it["w"]
                                zb = ZB[g_ % 3]
                                sp = spt[g_ % 3]
                                Rp = Rt[g_ % 4]
                                for hh in range(2):
                                    op("pe", lambda e: e.matmul(PS[zb + hh][:, 0:w], lhsT=trineg, rhs=sp[:, hh, 0:w], start=False, stop=(i == 0), skip_group_check=True),
                                       reads=[("asp", g_ % 3), ("cb", None)], writes=[("ps", zb + hh)])
                                    if i > 0:
                                        op("pe", lambda e: e.matmul(PS[zb + hh][:, 0:w], lhsT=negones, rhs=Rp[:, hh, 0:w], start=False, stop=True, skip_group_check=True),
                                           reads=[("aR", g_ % 4), ("cb", None)], writes=[("ps", zb + hh)])
                                ww = wt[g_ % 3]
                                op("act", lambda e: e.activation(out=ww[:, :, 0:w], in_=zview(g_, w), func=AF.Exp),
                                   reads=[("ps", zb), ("ps", zb + 1)], writes=[("aw", g_ % 3)])
                                if it["diag"]:
                                    for hh in range(2):
                                        op("dve", lambda e: e.tensor_tensor(out=ww[:, hh, 0:128], in0=ww[:, hh, 0:128], in1=mstrict, op=ALU.mult),
                                           reads=[("aw", g_ % 3), ("cb", None)], writes=[("aw", g_ % 3)])

                            def stE(g_):
                                it = items[g_]
                                if it["kind"] == "proj":
                                    return
                                hp, qc, i, n, kb, lo, w = it["hp"], it["qc"], it["i"], it["n"], it["kb"], it["lo"], it["w"]
                                ob = OB[it["gi"] % 2]
                                po, pok = PS[ob], ("ps", ob)
                                ww = wt[g_ % 3]
                                for hh in range(2):
                                    hb = slice(64 * hh, 64 * hh + 64)
                                    h = 2 * hp + hh
                                    op("pe", lambda e: e.matmul(po[hb, lo:512], lhsT=va[:, kb, h * 64:(h + 1) * 64], rhs=ww[:, hh, 0:w],
                                                                start=(i == 0), stop=(i == n - 1), skip_group_check=True),
                                       reads=[("aw", g_ % 3), ("va", kb)], writes=[pok])
                                if i == n - 1:
                                    evac(it["gi"], ya[:, hp, qc * 512:(qc + 1) * 512], po[:, :], [pok], [("ya", (hp, qc))])

                            while tstate["n"] < 3:
                                tables_step()
                            for t in range(NI + 2):
                                if t < NI:
                                    stA(t)
                                if 0 <= t - 1 < NI:
                                    stC(t - 1)
                                if 0 <= t - 2 < NI:
                                    stE(t - 2)
                        sch.barrier()

                if tstate["n"] < 3:
                    with contextlib.ExitStack() as est:
                        if "tP" not in tabs or "a" not in parts:
                            tables_alloc(est)
                        tables_finish()
                        sch.barrier()
                if "b" in parts:
                    if True:
                        yb = sb(es, "yb", [128, 2, S], BF16)
                        with contextlib.ExitStack() as es3:
                            vb = [sb(es3, "vb%d" % g, [128, 16, 256], BF16) for g in range(3)]
                            qbt = sb(es3, "bq", [128, S], BF16)
                            kbt = sb(es3, "bk", [128, S], BF16)
                            rt = [(sb(es3, "ra%d" % i, [128, 512], F32), sb(es3, "rb%d" % i, [128, 512], F32)) for i in range(2)]
                            pt = [sb(es3, "bp%d" % i, [128, 2, 256], BF16) for i in range(3)]
                            snum = sb(es3, "snum", [128, S], F32)
                            sden = sb(es3, "sden", [128, S], F32)
                            wb, wk = wload(l, "b_v")
                            for g, d in enumerate((1, 4, 16)):
                                Lg = S // d
                                nbg = Lg // 128

                                def tokfn(blk, d=d, nbg=nbg):
                                    c, b = blk // nbg, blk % nbg
                                    st = c + 128 * b * d
                                    return slice(st, st + 127 * d + 1, d) if d > 1 else slice(st, st + 128)

                                proj_v(wb, wk, 768, 256 * g, 256, vb[g], "vb%d" % g, stride=d, tokfn=tokfn)
                            wbs = [wload(l, "b_qk%d" % j) for j in range(3)]
                            SS = [(PS[0], ("ps", 0)), (PS[1], ("ps", 1))]
                            for pi in range(2):
                                for g, d in enumerate((1, 4, 16)):
                                    wb, wk = wbs[g]
                                    Lg = S // d
                                    nbg = Lg // 128
                                    for which, dstT, dkey in ((0, qbt, "bq"), (1, kbt, "bk")):
                                        for tt in range(4):
                                            pP, pPk = psnext()
                                            pS, pSk = psnext()
                                            for kc in range(8):
                                                for (pp, ppk, cidx) in ((pP, pPk, pi * 4 + which * 2), (pS, pSk, pi * 4 + which * 2 + 1)):
                                                    op("pe", lambda e, kc=kc, tt=tt, pp=pp, cidx=cidx, wb=wb: e.matmul(
                                                        pp[:], lhsT=wb[:, kc * 1024 + cidx * 128:kc * 1024 + (cidx + 1) * 128],
                                                        rhs=ht[:, kc, tt * 512:(tt + 1) * 512], start=(kc == 0), stop=(kc == 7)),
                                                       reads=[wk, ("mht", (kc, tt))], writes=[ppk])
                                            if d == 1:
                                                dst = dstT[:, tt * 512:(tt + 1) * 512]
                                                rope_evac(pP, pPk, pS, pSk, dst, (dkey, None), tt, rt)
                                            else:
                                                m = 512 // d
                                                dst = dstT[:].rearrange("p (c l) -> p c l", c=d)[:, :, tt * m:(tt + 1) * m]
                                                rope_evac(pP, pPk, pS, pSk, dst, (dkey, None), tt, rt, perm=d)
                                    jobs = []
                                    gi = 0
                                    for c in range(d):
                                        for QT in range((Lg + 511) // 512):
                                            wqt = min(512, Lg - 512 * QT)
                                            kbs = list(range(max(0, 4 * QT - 1), min(4 * QT + 3, nbg - 1) + 1))
                                            for ji, kb in enumerate(kbs):
                                                lo = max(128 * kb, 512 * QT) - 512 * QT
                                                hi = min(128 * kb + 256, 512 * QT + wqt) - 512 * QT
                                                mc0 = (512 * QT + lo) - 128 * kb
                                                jobs.append(dict(c=c, QT=QT, kb=kb, lo=lo, hi=hi, mc0=mc0, first=(ji == 0),
                                                                 last=(ji == len(kbs) - 1), gi=gi, wqt=wqt))
                                            gi += 1
                                    nj = len(jobs)

                                    def s0(i):
                                        J = jobs[i]
                                        w = J["hi"] - J["lo"]
                                        zb = 2 * (i % 2)
                                        kc0 = J["c"] * Lg + 128 * J["kb"]
                                        base = J["c"] * Lg + 512 * J["QT"]
                                        for hh in range(2):
                                            hb = slice(64 * hh, 64 * hh + 64)
                                            op("pe", lambda e: e.matmul(PS[zb + hh][:, 0:w], lhsT=kbt[hb, kc0:kc0 + 128],
                                                                        rhs=qbt[hb, base + J["lo"]:base + J["hi"]], start=True, stop=True),
                                               reads=[("bk", None), ("bq", None)], writes=[("ps", zb + hh)])
                                        p = pt[i % 3]
                                        zv = pst[:, zb * 512:(zb + 2) * 512].rearrange("p (h c) -> p h c", h=2)[:, :, 0:w]
                                        op("act", lambda e: e.activation(out=p[:, :, 0:w], in_=zv, func=AF.Exp, scale=0.125),
                                           reads=[("ps", zb), ("ps", zb + 1)], writes=[("bp", i % 3)])
                                        op("dve", lambda e: e.tensor_tensor(out=p[:, :, 0:w], in0=p[:, :, 0:w], in1=mband2[:, :, J["mc0"]:J["mc0"] + w], op=ALU.mult),
                                           reads=[("bp", i % 3), ("cb", None)], writes=[("bp", i % 3)])

                                    def s1(i):
                                        J = jobs[i]
                                        w = J["hi"] - J["lo"]
                                        lo, hi = J["lo"], J["hi"]
                                        p = pt[i % 3]
                                        blk = J["c"] * nbg + J["kb"]
                                        nb_ = 4 + (J["gi"] % 2)
                                        db_ = 6 + (J["gi"] % 2)
                                        pn, pd = PS[nb_], PS[db_]
                                        for hh in range(2):
                                            hb = slice(64 * hh, 64 * hh + 64)
                                            hloc = 2 * pi + hh
                                            op("pe", lambda e: e.matmul(pn[hb, lo:hi], lhsT=vb[g][:, blk, hloc * 64:(hloc + 1) * 64], rhs=p[:, hh, 0:w],
                                                                        start=J["first"], stop=J["last"], skip_group_check=True),
                                               reads=[("bp", i % 3), ("vb%d" % g, blk)], writes=[("ps", nb_)])
                                            op("pe", lambda e: e.matmul(pd[hb, lo:hi], lhsT=ones[:, 0:64], rhs=p[:, hh, 0:w],
                                                                        start=J["first"], stop=J["last"], skip_group_check=True),
                                               reads=[("bp", i % 3), ("cb", None)], writes=[("ps", db_)])
                                        if J["last"]:
                                            wqt = J["wqt"]
                                            n0 = 512 * J["QT"] * d + J["c"]
                                            nsl = slice(n0, n0 + wqt) if d == 1 else slice(n0, n0 + (wqt - 1) * d + 1, d)
                                            if g == 0:
                                                op("act", lambda e: e.copy(out=snum[:, nsl], in_=pn[:, 0:wqt]), reads=[("ps", nb_)], writes=[("snum", None)])
                                                op("dve", lambda e: e.tensor_copy(out=sden[:, nsl], in_=pd[:, 0:wqt]), reads=[("ps", db_)], writes=[("sden", None)])
                                            else:
                                                op("dve", lambda e: e.tensor_tensor(out=snum[:, nsl], in0=pn[:, 0:wqt], in1=snum[:, nsl], op=ALU.add),
                                                   reads=[("ps", nb_), ("snum", None)], writes=[("snum", None)])
                                                op("dve", lambda e: e.tensor_tensor(out=sden[:, nsl], in0=pd[:, 0:wqt], in1=sden[:, nsl], op=ALU.add),
                                                   reads=[("ps", db_), ("sden", None)], writes=[("sden", None)])

                                    for t in range(nj + 2):
                                        if t < nj:
                                            s0(t)
                                        if t >= 2:
                                            s1(t - 2)
                                op("dve", lambda e: e.reciprocal(out=sden[:], in_=sden[:]), reads=[("sden", None)], writes=[("sden", None)])
                                op("dve", lambda e, pi=pi: e.tensor_tensor(out=yb[:, pi, :], in0=snum[:], in1=sden[:], op=ALU.mult),
                                   reads=[("sden", None), ("snum", None)], writes=[("yb", pi)])
                        sch.barrier()

                if "c" in parts:
                    if True:
                        yc = sb(es, "yc", [128, 4, S], BF16)
                        with contextlib.ExitStack() as es3:
                            vc = sb(es3, "vc", [128, 16, 512], BF16)
                            qk = [sb(es3, "cqk%d" % i, [128, S], BF16) for i in range(4)]
                            rt = [(sb(es3, "ra%d" % i, [128, 512], F32), sb(es3, "rb%d" % i, [128, 512], F32)) for i in range(2)]
                            pt = [sb(es3, "cp%d" % i, [128, 2, 512], BF16) for i in range(3)]
                            lnd = sb(es3, "clnd", [128, 512], F32)
                            rr = [sb(es3, "crr%d" % i, [128, 512], F32) for i in range(2)]
                            oo = [sb(es3, "coo%d" % i, [128, 512], F32) for i in range(3)]
                            osq = sb(es3, "cosq", [128, 512], BF16)
                            wb, wk = wload(l, "c_v")
                            proj_v(wb, wk, 512, 0, 512, vc, "vc", tokfn=lambda blk: slice(blk * 128, (blk + 1) * 128))
                            PO = [(PS[0], ("ps", 0)), (PS[1], ("ps", 1))]
                            PD = [(PS[2], ("ps", 2)), (PS[3], ("ps", 3))]
                            wbs = {}
                            items = []
                            for hp in range(2):
                                for m4 in range(4):
                                    for tt in range(4):
                                        items.append(dict(kind="proj", hp=hp, m4=m4, tt=tt))
                                for hh in range(2):
                                    for qc in range(4):
                                        nsteps = 4 * qc + 4
                                        for kb in range(nsteps):
                                            lo = 128 * (kb - 4 * qc) if kb >= 4 * qc else 0
                                            items.append(dict(kind="step", hp=hp, hh=hh, qc=qc, kb=kb, lo=lo, w=512 - lo,
                                                              first=(kb == 0), last=(kb == nsteps - 1), diag=(kb >= 4 * qc)))
                                        items.append(dict(kind="post", hp=hp, hh=hh, qc=qc))
                            NI = len(items)
                            deferred = {}

                            def c0(g_):
                                it = items[g_]
                                zb = 4 + 2 * (g_ % 2)
                                if it["kind"] == "proj":
                                    hp, m4, tt = it["hp"], it["m4"], it["tt"]
                                    if hp not in wbs:
                                        wbs[hp] = wload(l, "c_qk%d" % hp)
                                    wb, wk = wbs[hp]
                                    pP, pPk = PS[zb], ("ps", zb)
                                    pS, pSk = PS[zb + 1], ("ps", zb + 1)
                                    for kc in range(8):
                                        for (pp, ppk, cidx) in ((pP, pPk, 2 * m4), (pS, pSk, 2 * m4 + 1)):
                                            op("pe", lambda e: e.matmul(
                                                pp[:], lhsT=wb[:, kc * 1024 + cidx * 128:kc * 1024 + (cidx + 1) * 128],
                                                rhs=ht[:, kc, tt * 512:(tt + 1) * 512], start=(kc == 0), stop=(kc == 7)),
                                               reads=[wk, ("mht", None)], writes=[ppk])
                                    rope_evac(pP, pPk, pS, pSk, qk[m4][:, tt * 512:(tt + 1) * 512], ("cqk", m4), tt, rt)
                                    return
                                if it["kind"] == "post":
                                    return
                                hb = slice(64 * it["hh"], 64 * it["hh"] + 64)
                                qc, kb, lo, w = it["qc"], it["kb"], it["lo"], it["w"]
                                p = pt[g_ % 3]
                                for m in range(2):
                                    op("pe", lambda e: e.matmul(PS[zb + m][:, 0:w], lhsT=qk[2 + m][hb, kb * 128:(kb + 1) * 128],
                                                                rhs=qk[m][hb, qc * 512 + lo:qc * 512 + 512], start=True, stop=True),
                                       reads=[("cqk", 2 + m), ("cqk", m)], writes=[("ps", zb + m)])
                                zv = pst[:, zb * 512:(zb + 2) * 512].rearrange("p (h c) -> p h c", h=2)[:, :, 0:w]
                                op("act", lambda e: e.activation(out=p[:, :, 0:w], in_=zv, func=AF.Exp, scale=0.125),
                                   reads=[("ps", zb), ("ps", zb + 1)], writes=[("cp", g_ % 3)])
                                if it["diag"]:
                                    for m in range(2):
                                        op("dve", lambda e: e.tensor_tensor(out=p[:, m, 0:128], in0=p[:, m, 0:128], in1=mincl, op=ALU.mult),
                                           reads=[("cp", g_ % 3), ("cb", None)], writes=[("cp", g_ % 3)])

                            def c1(g_):
                                it = items[g_]
                                if it["kind"] == "proj":
                                    return
                                h = 2 * it["hp"] + it["hh"]
                                qc = it["qc"]
                                if it["kind"] == "step":
                                    kb, lo, w = it["kb"], it["lo"], it["w"]
                                    p = pt[g_ % 3]
                                    for m in range(2):
                                        po, pok = PO[m]
                                        pd, pdk = PD[m]
                                        op("pe", lambda e: e.matmul(po[:, lo:512], lhsT=vc[:, kb, h * 128:(h + 1) * 128], rhs=p[:, m, 0:w],
                                                                    start=it["first"], stop=it["last"]),
                                           reads=[("cp", g_ % 3), ("vc", kb)], writes=[pok])
                                        op("pe", lambda e: e.matmul(pd[:, lo:512], lhsT=ones, rhs=p[:, m, 0:w],
                                                                    start=it["first"], stop=it["last"]),
                                           reads=[("cp", g_ % 3), ("cb", None)], writes=[pdk])
                                    return
                                for m in range(2):
                                    pd, pdk = PD[m]
                                    op("act", lambda e: e.activation(out=rr[m][:], in_=pd[:], func=AF.Ln), reads=[pdk], writes=[("crr", m)])
                                for m in range(2):
                                    po, pok = PO[m]
                                    op("dve", lambda e: e.tensor_copy(out=oo[m][:], in_=po[:]), reads=[pok], writes=[("coo", m)])
                                for m in range(2):
                                    op("act", lambda e: e.activation(out=rr[m][:], in_=rr[m][:], func=AF.Exp, scale=-1.0),
                                       reads=[("crr", m)], writes=[("crr", m)])
                                op("dve", lambda e: e.tensor_tensor(out=oo[1][:], in0=oo[1][:], in1=rr[1][:], op=ALU.mult),
                                   reads=[("coo", 1), ("crr", 1)], writes=[("coo", 1)])
                                op("dve", lambda e: e.tensor_tensor(out=oo[0][:], in0=oo[0][:], in1=rr[0][:], op=ALU.mult),
                                   reads=[("coo", 0), ("crr", 0)], writes=[("coo", 0)])
                                op("dve", lambda e: e.scalar_tensor_tensor(out=oo[2][:], in0=oo[1][:], scalar=lamv[:, 0:1], in1=oo[0][:],
                                                                            op0=ALU.mult, op1=ALU.add),
                                   reads=[("coo", 0), ("coo", 1), ("lamv", None)], writes=[("coo", 2)])
                                op("dve", lambda e: e.tensor_tensor(out=osq[:], in0=oo[2][:], in1=oo[2][:], op=ALU.mult),
                                   reads=[("coo", 2)], writes=[("cosq", None)])
                                def post2(g2, h=h, qc=qc):
                                    zb = 4 + 2 * (g2 % 2)
                                    pss, pssk = PS[zb], ("ps", zb)
                                    op("pe", lambda e: e.matmul(pss[:], lhsT=ones, rhs=osq[:], start=True, stop=True),
                                       reads=[("cosq", None), ("cb", None)], writes=[pssk])
                                    op("act", lambda e: e.activation(out=lnd[:], in_=pss[:], func=AF.Ln, scale=1.0 / 128, bias=1e-5),
                                       reads=[pssk], writes=[("clnd", None)])
                                    op("act", lambda e: e.activation(out=lnd[:], in_=lnd[:], func=AF.Exp, scale=-0.5),
                                       reads=[("clnd", None)], writes=[("clnd", None)])
                                    op("dve", lambda e: e.scalar_tensor_tensor(out=yc[:, h, qc * 512:(qc + 1) * 512], in0=oo[2][:], scalar=lamv[:, 1:2],
                                                                                in1=lnd[:], op0=ALU.mult, op1=ALU.mult),
                                       reads=[("coo", 2), ("clnd", None), ("lamv", None)], writes=[("yc", (h, qc))])

                                deferred[g_ + 3] = post2

                            for t in range(NI + 6):
                                if t < NI:
                                    c0(t)
                                if 2 <= t < NI + 2:
                                    c1(t - 2)
                                if (t - 2) in deferred:
                                    deferred.pop(t - 2)(t - 2)
                            assert not deferred
                        sch.barrier()
                branches = []
                if "a" in parts:
                    branches.append(("a", [ya[:, j, :] for j in range(4)], ("ya", None)))
                if "b" in parts:
                    branches.append(("b", [yb[:, j, :] for j in range(2)], ("yb", None)))
                if "c" in parts:
                    branches.append(("c", [yc[:, j, :] for j in range(4)], ("yc", None)))
                merge_all(branches)
            sch.barrier()

        prog = []
        for l in range(DEPTH):
            prog.append(("ffn", l, 1))
            for s in range(NSEQ):
                prog.append(("mix", l, s))
            prog.append(("ffn", l, 2))
        if stop_after is not None:
            prog = prog[:stop_after]
        src = xin
        for pi, item in enumerate(prog):
            last = pi == len(prog) - 1
            if item[0] == "ffn":
                ffn_phase(item[1], item[2], src, final=last, emit_hm=(item[2] == 1 and not last))
                src = xa
            else:
                mixer_phase(item[1], item[2], parts=MIX_PARTS)
                if last:
                    for kc in range(8):
                        op("sp", lambda e, kc=kc: e.dma_start(out=outd.ap()[kc * 128:(kc + 1) * 128, :], in_=xa.ap()[kc * 128:(kc + 1) * 128, :]),
                           reads=[("xdram", None)], writes=[("odram", None)], dma=True)
        sch.emit()
    return nc


MIX_PARTS = "abc"
_CACHE = {}


def kernel(**inputs):
    inp = {k: np.asarray(v) for k, v in inputs.items()}
    B = inp["x"].shape[0]
    nseq = B // NCORES
    depth = inp["w_in"].shape[0]
    key = (nseq, depth)
    if key not in _CACHE:
        _CACHE[key] = build_program(NSEQ=nseq, DEPTH=depth)
    nc = _CACHE[key]
    consts = _build_consts()
    wimgs = [_build_wimg(inp, l) for l in range(depth)]
    vecs = [_build_vecs(inp, l) for l in range(depth)]
    in_maps = []
    for c in range(NCORES):
        xb = inp["x"][c * nseq:(c + 1) * nseq]
        xT = np.ascontiguousarray(xb.reshape(nseq * S, D).T)
        pos = inp["positions"][c * nseq:(c + 1) * nseq].astype(np.int32)
        posb = np.ascontiguousarray(np.broadcast_to(pos[:, None, :], (nseq, 128, S)))
        m = {"xT": xT, "pos": posb, "consts": consts}
        for l in range(depth):
            m["wimg%d" % l] = wimgs[l]
            m["vecs%d" % l] = vecs[l]
        in_maps.append(m)
    res = run_bass_kernel_spmd(nc, in_maps, core_ids=list(range(NCORES)))
    outs = [np.asarray(r["outT"]).T.reshape(nseq, S, D) for r in res.results]
    return np.ascontiguousarray(np.concatenate(outs, axis=0).astype(np.float32))
```

```python
import contextlib
import math
import numpy as np
import concourse.bass as bass
import concourse.mybir as mybir
from concourse.bass_utils import run_bass_kernel_spmd

F32 = mybir.dt.float32
BF16 = mybir.dt.bfloat16
I32 = mybir.dt.int32
AF = mybir.ActivationFunctionType
ALU = mybir.AluOpType
AX = mybir.AxisListType

D = 1024
S = 2048
DFF = 2816
NFC = DFF // 128
NCORES = 8
COMPUTE = ("pe", "act", "dve", "pool")
NVEC = 57 + 256
NCONST = 128 * 5 + 512


class _Rec:
    def __init__(self):
        self.call = None

    def __getattr__(self, name):
        def f(*a, **k):
            self.call = (name, a, k)
        return f


class Sched:
    def __init__(self, nc, n_dma_sems=24):
        self.nc = nc
        self.ops = []
        self.res = {}
        self.n_dma_sems = n_dma_sems
        self.last = {}
        self.dmas = []
        self.pending = {}

    def _conf(self, key):
        name, sub = key
        d = self.res.setdefault(name, {})
        if sub is None:
            return list(d.values())
        out = []
        if None in d:
            out.append(d[None])
        if sub in d:
            out.append(d[sub])
        return out

    def barrier(self, engs=("pe", "act", "dve", "sp")):
        deps = set(v for k, v in self.last.items() if k in engs)
        deps |= set(d for d in self.dmas if self.ops[d]["eng"] in engs)
        self.dmas = [d for d in self.dmas if self.ops[d]["eng"] not in engs]
        for e in engs:
            self.pending[e] = set(self.pending.get(e, set())) | deps

    def op(self, eng, fn, reads=(), writes=(), dma=False):
        oid = len(self.ops)
        raw, other = set(), set()
        for key in reads:
            for st in self._conf(key):
                if st[0] is not None:
                    raw.add(st[0])
        for key in writes:
            for st in self._conf(key):
                if st[0] is not None:
                    other.add(st[0])
                other.update(st[1])
        for key in reads:
            d = self.res.setdefault(key[0], {})
            if key[1] not in d:
                d[key[1]] = [None, []]
            for st in self._conf(key):
                st[1].append(oid)
        for key in writes:
            d = self.res.setdefault(key[0], {})
            if key[1] not in d:
                d[key[1]] = [None, []]
            for st in self._conf(key):
                st[0] = oid
                st[1] = []
        if eng in self.pending:
            other |= self.pending.pop(eng)
        other -= raw
        raw.discard(oid)
        other.discard(oid)
        rec = _Rec()
        fn(rec)
        cname, ca, ck = rec.call
        self.ops.append(dict(eng=eng, fn=(lambda e: getattr(e, cname)(*ca, **ck)), raw=raw, other=other, dma=dma))
        self.last[eng] = oid
        if dma:
            self.dmas.append(oid)
        return oid

    def defer(self, eng, fn, reads=(), writes=(), dma=False):
        rec = _Rec()
        fn(rec)
        cname, ca, ck = rec.call
        return lambda: self.op(eng, (lambda e: getattr(e, cname)(*ca, **ck)), reads, writes, dma)

    def emit(self):
        nc = self.nc
        ops = self.ops
        need = []
        for o in ops:
            w = set()
            for d in o["raw"]:
                od = ops[d]
                if od["dma"] or od["eng"] != o["eng"] or o["dma"] or o["eng"] != "pe":
                    w.add(d)
            for d in o["other"]:
                od = ops[d]
                if od["dma"] or od["eng"] != o["eng"] or o["dma"] or o["eng"] != "pe":
                    w.add(d)
            need.append(w)
        signal = [False] * len(ops)
        for w in need:
            for d in w:
                signal[d] = True
        sig_idx = [0] * len(ops)
        cnt = {e: 0 for e in COMPUTE}
        dma_rr, dma_sem, slot_val = {}, [None] * len(ops), {}
        for i, o in enumerate(ops):
            if o["dma"]:
                q = o["eng"]
                k = dma_rr.get(q, 0)
                dma_rr[q] = k + 1
                slot = (q, k % self.n_dma_sems)
                v = slot_val.get(slot, 0) + 16
                slot_val[slot] = v
                dma_sem[i] = (slot, v)
            elif signal[i]:
                cnt[o["eng"]] += 1
                sig_idx[i] = cnt[o["eng"]]
        with contextlib.ExitStack() as es:
            sems = {e: es.enter_context(nc.semaphore("s_" + e)) for e in COMPUTE}
            dsems = {}
            for q in dma_rr:
                for k in range(min(self.n_dma_sems, dma_rr[q])):
                    dsems[(q, k)] = es.enter_context(nc.semaphore("d_%s_%d" % (q, k)))
            block = es.enter_context(nc.Block())

            def run(engname, e):
                waited = {}
                for i, o in enumerate(ops):
                    if o["eng"] != engname:
                        continue
                    tgt = {}
                    for d in need[i]:
                        od = ops[d]
                        if od["dma"]:
                            slot, v = dma_sem[d]
                            key = ("d", slot)
                        else:
                            key = ("c", od["eng"])
                            v = sig_idx[d]
                        if v > tgt.get(key, 0):
                            tgt[key] = v
                    if o["dma"]:
                        slot, v = dma_sem[i]
                        if v > 16 and v - 16 > tgt.get(("d", slot), 0):
                            tgt[("d", slot)] = v - 16
                    for key, v in tgt.items():
                        if waited.get(key, 0) >= v:
                            continue
                        waited[key] = v
                        e.wait_ge(dsems[key[1]] if key[0] == "d" else sems[key[1]], v)
                    ins = o["fn"](e)
                    if o["dma"]:
                        ins.then_inc(dsems[dma_sem[i][0]], 16)
                    elif signal[i]:
                        ins.then_inc(sems[engname], 1)
                for slot, v in slot_val.items():
                    if slot[0] == engname and waited.get(("d", slot), 0) < v:
                        e.wait_ge(dsems[slot], v)

            block.tensor(lambda e: run("pe", e))
            block.scalar(lambda e: run("act", e))
            block.vector(lambda e: run("dve", e))
            block.gpsimd(lambda e: run("pool", e))
            block.sync(lambda e: run("sp", e))


def _img(W):
    nk = W.shape[0] // 128
    C = W.shape[1]
    return np.ascontiguousarray(W.reshape(nk, 128, C).transpose(1, 0, 2).reshape(128, nk * C))


def _swapcols(base):
    cols = []
    for hh in range(2):
        b = base + hh * 64
        cols += list(range(b + 8, b + 16)) + list(range(b, b + 8)) + list(range(b + 16, b + 64))
    return cols


def _rng(a, n=128):
    return list(range(a, a + n))


QA, KA, VA = 0, 512, 1024
QB, KB, VB = 1536, 2304, 3072
Q1, Q2, K1, K2 = 3840, 4096, 4352, 4608
VC = 4864
GATE = 5376


def _load_plan():
    plan = []

    def ffn(w):
        for g in range(6):
            plan.append(("f%d_in%d" % (w, g), "ffn_in", (w, g)))
        for q in range(4):
            plan.append(("f%d_out%d" % (w, q), "ffn_out", (w, q)))

    ffn(1)
    plan.append(("a_v", "cols", _rng(VA, 512)))
    cols = []
    for hp in range(4):
        cols += _rng(QA + hp * 128) + _rng(KA + hp * 128)
    plan.append(("a_qk", "cols", cols))
    plan.append(("g_a", "cols", _rng(GATE, 1024)))
    plan.append(("up_a", "up", "a"))
    plan.append(("wo_a", "wo", None))
    plan.append(("b_v", "cols", _rng(VB, 768)))
    for j in range(3):
        cols = []
        for c in (2 * j, 2 * j + 1):
            cols += _rng(QB + c * 128) + _swapcols(QB + c * 128) + _rng(KB + c * 128) + _swapcols(KB + c * 128)
        plan.append(("b_qk%d" % j, "cols", cols))
    plan.append(("g_b", "cols", _rng(GATE + 1024, 1024)))
    plan.append(("up_b", "up", "b"))
    plan.append(("c_v", "cols", _rng(VC, 512)))
    for hp in range(2):
        cols = []
        for base in (Q1, Q2, K1, K2):
            cols += _rng(base + hp * 128) + _swapcols(base + hp * 128)
        plan.append(("c_qk%d" % hp, "cols", cols))
    plan.append(("g_c", "cols", _rng(GATE + 2048, 1024)))
    plan.append(("up_c", "up", "c"))
    ffn(2)
    return plan


def _ffn_in_cols(g):
    ncg = min(4, NFC - 4 * g)
    return ncg, _rng(512 * g, 128 * ncg) + _rng(DFF + 512 * g, 128 * ncg)


def _plan_sizes():
    sizes = {}
    off = 0
    for name, kind, spec in _load_plan():
        if kind == "ffn_in":
            ncg, cols = _ffn_in_cols(spec[1])
            E = 8 * len(cols)
        elif kind == "ffn_out":
            E = NFC * 256
        elif kind == "cols":
            E = 8 * len(spec)
        elif kind == "up":
            E = {"a": 4, "b": 2, "c": 4}[spec] * 1024
        else:
            E = 8192
        sizes[name] = (off, E)
        off += E
    return sizes, off


def _build_wimg(inp, l):
    sizes, tot = _plan_sizes()
    out = np.empty((128, tot), np.float32)
    for name, kind, spec in _load_plan():
        off, E = sizes[name]
        if kind == "ffn_in":
            W = {1: inp["ffn1_w_in"], 2: inp["ffn2_w_in"]}[spec[0]][l]
            _, cols = _ffn_in_cols(spec[1])
            im = _img(W[:, cols])
        elif kind == "ffn_out":
            W = {1: inp["ffn1_w_out"], 2: inp["ffn2_w_out"]}[spec[0]][l]
            im = _img(W[:, spec[1] * 256:(spec[1] + 1) * 256])
        elif kind == "cols":
            im = _img(inp["w_in"][l][:, spec])
        elif kind == "up":
            im = _img({"a": inp["w_up_a"], "b": inp["w_up_b"], "c": inp["w_up_c"]}[spec][l])
        else:
            im = _img(inp["w_out"][l])
        assert im.shape == (128, E), (name, im.shape, E)
        out[:, off:off + E] = im
    return out


def _build_vecs(inp, l):
    v = np.zeros((128, NVEC), np.float32)
    v[:, 0:8] = inp["ffn1_norm"][l].reshape(8, 128).T
    v[:, 8:16] = inp["mix_norm"][l].reshape(8, 128).T
    v[:, 16:24] = inp["ffn2_norm"][l].reshape(8, 128).T
    v[:, 24:32] = inp["final_norm"].reshape(8, 128).T
    v[:, 32:56] = inp["b_gate"][l].reshape(24, 128).T
    v[:, 56] = inp["diff_subln"][l]
    for i, lv in enumerate((inp["lam_q1"], inp["lam_k1"], inp["lam_q2"], inp["lam_k2"])):
        v[:, 57 + 64 * i:57 + 64 * (i + 1)] = np.broadcast_to(lv[l][None, :], (128, 64))
    return v


def _build_consts():
    c = np.zeros((128, NCONST + 2), np.float32)
    r = np.arange(128)[:, None]
    col = np.arange(128)[None, :]
    c[:, 0:128] = 1.0
    c[:, 128:256] = np.where(r >= col, -1.0, 0.0)
    c[:, 256:384] = -1.0
    c[:, 384:512] = np.where(col > r, 1.0, 0.0)
    c[:, 512:640] = np.where(col >= r, 1.0, 0.0)
    col2 = np.arange(256)[None, :]
    c[:, 640:896] = np.where((col2 >= r) & (col2 <= r + 128), 1.0, 0.0)
    c[:, 896:1152] = c[:, 640:896]
    inv_freq = (500000.0 ** (-np.arange(0, 16, 2, dtype=np.float32) / np.float32(16))).astype(np.float32)
    for p in range(128):
        q = p % 64
        if q < 16:
            c[p, NCONST] = inv_freq[q % 8]
            c[p, NCONST + 1] = -1.0 if q < 8 else 1.0
    return c


def build_program(NSEQ=2, DEPTH=2, stop_after=None):
    T = NSEQ * S
    nc = bass.Bass("TRN2", target_bir_lowering=False, dynamic_dma_scratch_size=8192)
    sizes, WTOT = _plan_sizes()
    xin = nc.dram_tensor("xT", [D, T], F32, kind="ExternalInput")
    posd = nc.dram_tensor("pos", [NSEQ, 128, S], I32, kind="ExternalInput")
    wimg = [nc.dram_tensor("wimg%d" % l, [128, WTOT], F32, kind="ExternalInput") for l in range(DEPTH)]
    vecd = [nc.dram_tensor("vecs%d" % l, [128, NVEC], F32, kind="ExternalInput") for l in range(DEPTH)]
    cstd = nc.dram_tensor("consts", [128, NCONST + 2], F32, kind="ExternalInput")
    outd = nc.dram_tensor("outT", [D, T], F32, kind="ExternalOutput")
    xa = nc.dram_tensor("xa", [D, T], F32, kind="Internal")
    hmd = nc.dram_tensor("hmd", [D, T], BF16, kind="Internal")

    sch = Sched(nc)
    op = sch.op
    top = contextlib.ExitStack()
    with top:
        uniq = {"n": 0}

        def sb(es, name, shape, dt):
            uniq["n"] += 1
            return es.enter_context(nc.sbuf_tensor("%s_%d" % (name, uniq["n"]), shape, dt))

        WB = [sb(top, "wb%d" % i, [128, 8192], BF16) for i in range(4)]
        pst = top.enter_context(nc.psum_tensor("pst", [128, 4096], F32))
        PS = [pst[:, i * 512:(i + 1) * 512] for i in range(8)]
        cb = sb(top, "cb", [128, NCONST], BF16)
        cf = sb(top, "cf", [128, 2], F32)
        vec = [sb(top, "vec%d" % l, [128, NVEC], F32) for l in range(DEPTH)]
        ones = cb[:, 0:128]
        trineg = cb[:, 128:256]
        negones = cb[:, 256:384]
        mstrict = cb[:, 384:512]
        mincl = cb[:, 512:640]
        mband = cb[:, 640:896]
        mband2 = cb[:, 640:1152].rearrange("p (h c) -> p h c", h=2)

        op("pool", lambda e: e.dma_start(out=cb[:], in_=cstd.ap()[:, 0:NCONST]), writes=[("cb", None)], dma=True)
        op("sp", lambda e: e.dma_start(out=cf[:], in_=cstd.ap()[:, NCONST:NCONST + 2]), writes=[("cf", None)], dma=True)
        for l in range(DEPTH):
            op("sp", lambda e, l=l: e.dma_start(out=vec[l][:], in_=vecd[l].ap()), writes=[("vec", l)], dma=True)

        wstate = {"n": 0}

        def wload(l, name):
            off, E = sizes[name]
            b = wstate["n"] % 4
            wstate["n"] += 1
            op("pool", lambda e: e.dma_start(out=WB[b][:, 0:E], in_=wimg[l].ap()[:, off:off + E]),
               writes=[("wb", b)], dma=True)
            return WB[b], ("wb", b)

        psrr = {"n": 0}

        def psnext():
            b = psrr["n"] % 8
            psrr["n"] += 1
            return PS[b], ("ps", b)

        def rms_rstd(es_name, src_chunks, src_keys, rstd_ap, rstd_key, sqt, width, inv_n, eps, lnt):
            ps, pk = psnext()
            n = len(src_chunks)
            for i, (a, k) in enumerate(zip(src_chunks, src_keys)):
                sq = sqt[i % 2]
                sk = ("sq" + es_name, i % 2)
                if i % 2 == 0:
                    op("act", lambda e, a=a, sq=sq: e.activation(out=sq[:, 0:width], in_=a, func=AF.Square),
                       reads=[k], writes=[sk])
                else:
                    op("dve", lambda e, a=a, sq=sq: e.tensor_tensor(out=sq[:, 0:width], in0=a, in1=a, op=ALU.mult),
                       reads=[k], writes=[sk])
                op("pe", lambda e, sq=sq, i=i: e.matmul(ps[:, 0:width], lhsT=ones, rhs=sq[:, 0:width],
                                                        start=(i == 0), stop=(i == n - 1)),
                   reads=[sk, ("cb", None)], writes=[pk])
            lk = ("ln" + es_name, None)
            op("act", lambda e: e.activation(out=lnt[:, 0:width], in_=ps[:, 0:width], func=AF.Ln, scale=inv_n, bias=eps),
               reads=[pk], writes=[lk])
            op("act", lambda e: e.activation(out=rstd_ap, in_=lnt[:, 0:width], func=AF.Exp, scale=-0.5),
               reads=[lk], writes=[rstd_key])

        def ffn_phase(l, w, src, final=False, emit_hm=False):
            gcol = 0 if w == 1 else 16
            TT = 1024
            NT = T // TT
            with contextlib.ExitStack() as es:
                xt = sb(es, "xt", [128, 8, TT], F32)
                xst = sb(es, "xst", [128, 8, 512], F32)
                hts = [sb(es, "ht%d" % i, [128, 8, TT], BF16) for i in range(2)]
                act = sb(es, "act", [128, NFC, TT], BF16)
                sqt = [sb(es, "sq%d" % i, [128, 512], BF16) for i in range(8)]
                lnt = sb(es, "lnt", [128, 512], F32)
                rstd = sb(es, "rstd", [128, 512], F32)
                sil = [sb(es, "sil%d" % i, [128, 512], F32) for i in range(2)]
                if emit_hm:
                    hst = sb(es, "hst", [128, 4, 512], BF16)

                def pre_load(tt, h):
                    c0 = tt * TT + h * 512
                    for kc in range(8):
                        op("sp", lambda e, kc=kc: e.dma_start(out=xst[:, kc, :], in_=src.ap()[kc * 128:(kc + 1) * 128, c0:c0 + 512]),
                           reads=[("xdram", (kc, tt))], writes=[("xst", kc)], dma=True)
                    for kc in range(8):
                        if kc % 2 == 0:
                            op("act", lambda e, kc=kc: e.activation(out=sqt[kc][:], in_=xst[:, kc, :], func=AF.Square),
                               reads=[("xst", kc)], writes=[("sq", kc)])
                        else:
                            op("dve", lambda e, kc=kc: e.tensor_tensor(out=sqt[kc][:], in0=xst[:, kc, :], in1=xst[:, kc, :], op=ALU.mult),
                               reads=[("xst", kc)], writes=[("sq", kc)])

                def pre_norm(tt, h):
                    ps, pk = psnext()
                    for kc in range(8):
                        op("pe", lambda e, kc=kc: e.matmul(ps[:], lhsT=ones, rhs=sqt[kc][:], start=(kc == 0), stop=(kc == 7)),
                           reads=[("sq", kc), ("cb", None)], writes=[pk])
                    op("act", lambda e: e.activation(out=lnt[:], in_=ps[:], func=AF.Ln, scale=1.0 / D, bias=1e-6),
                       reads=[pk], writes=[("lnt", None)])
                    op("act", lambda e: e.activation(out=rstd[:], in_=lnt[:], func=AF.Exp, scale=-0.5),
                       reads=[("lnt", None)], writes=[("rstd", None)])
                    hb_ = hts[tt % 2]
                    for kc in range(8):
                        op("dve", lambda e, kc=kc: e.scalar_tensor_tensor(
                            out=hb_[:, kc, h * 512:(h + 1) * 512], in0=xst[:, kc, :],
                            scalar=vec[l][:, gcol + kc:gcol + kc + 1], in1=rstd[:], op0=ALU.mult, op1=ALU.mult),
                           reads=[("xst", kc), ("rstd", None), ("vec", l)], writes=[("ht", (tt % 2, kc, h))])

                for h in range(2):
                    pre_load(0, h)
                    pre_norm(0, h)
                pending = []
                for tt in range(NT):
                    c0 = tt * TT
                    ht = hts[tt % 2]
                    si = 0
                    for g in range(6):
                        if g == 1 and pending:
                            pending.pop(0)()
                        if g == 2:
                            while pending:
                                pending.pop(0)()
                            for kc in range(8):
                                op("sp", lambda e, kc=kc, c0=c0: e.dma_start(out=xt[:, kc, :], in_=src.ap()[kc * 128:(kc + 1) * 128, c0:c0 + TT]),
                                   reads=[("xdram", (kc, tt))], writes=[("xt", kc)], dma=True)
                        wb, wk = wload(l, "f%d_in%d" % (w, g))
                        ncg = min(4, NFC - 4 * g)
                        C = 2 * 128 * ncg
                        for j in range(ncg):
                            fc = 4 * g + j
                            for h in range(2):
                                pg, pgk = psnext()
                                pu, puk = psnext()
                                for kc in range(8):
                                    op("pe", lambda e, kc=kc: e.matmul(
                                        pg[:], lhsT=wb[:, kc * C + j * 128:kc * C + (j + 1) * 128],
                                        rhs=ht[:, kc, h * 512:(h + 1) * 512], start=(kc == 0), stop=(kc == 7)),
                                       reads=[wk, ("ht", (tt % 2, kc, h))], writes=[pgk])
                                for kc in range(8):
                                    op("pe", lambda e, kc=kc: e.matmul(
                                        pu[:], lhsT=wb[:, kc * C + (ncg + j) * 128:kc * C + (ncg + j + 1) * 128],
                                        rhs=ht[:, kc, h * 512:(h + 1) * 512], start=(kc == 0), stop=(kc == 7)),
                                       reads=[wk, ("ht", (tt % 2, kc, h))], writes=[puk])
                                st = sil[si % 2]
                                stk = ("sil", si % 2)
                                si += 1
                                op("act", lambda e: e.activation(out=st[:], in_=pg[:], func=AF.Silu), reads=[pgk], writes=[stk])
                                op("dve", lambda e: e.tensor_tensor(out=act[:, fc, h * 512:(h + 1) * 512], in0=pu[:], in1=st[:], op=ALU.mult),
                                   reads=[puk, stk], writes=[("act", (fc, h))])
                    if tt + 1 < NT:
                        pre_load(tt + 1, 0)
                    for q in range(4):
                        wb, wk = wload(l, "f%d_out%d" % (w, q))
                        for jj in range(2):
                            dc = 2 * q + jj
                            for h in range(2):
                                py, pyk = psnext()
                                for fc in range(NFC):
                                    op("pe", lambda e, fc=fc: e.matmul(
                                        py[:], lhsT=wb[:, fc * 256 + jj * 128:fc * 256 + (jj + 1) * 128],
                                        rhs=act[:, fc, h * 512:(h + 1) * 512], start=(fc == 0), stop=(fc == NFC - 1)),
                                       reads=[wk, ("act", (fc, h))], writes=[pyk])
                                op("dve", lambda e: e.scalar_tensor_tensor(
                                    out=xt[:, dc, h * 512:(h + 1) * 512], in0=py[:], scalar=0.5,
                                    in1=xt[:, dc, h * 512:(h + 1) * 512], op0=ALU.mult, op1=ALU.add),
                                   reads=[pyk, ("xt", dc)], writes=[("xt", dc)])
                        if tt + 1 < NT:
                            if q == 0:
                                pre_norm(tt + 1, 0)
                                pre_load(tt + 1, 1)
                            elif q == 1:
                                pre_norm(tt + 1, 1)
                    if not final:
                        for kc in range(8):
                            op("sp", lambda e, kc=kc, c0=c0: e.dma_start(out=xa.ap()[kc * 128:(kc + 1) * 128, c0:c0 + TT], in_=xt[:, kc, :]),
                               reads=[("xt", kc)], writes=[("xdram", (kc, tt))], dma=True)
                        if emit_hm:
                            def epilogue(h, tt=tt, c0=c0):
                                if True:
                                    ps, pk = psnext()
                                    for kc in range(8):
                                        if kc % 2 == 0:
                                            op("act", lambda e, kc=kc: e.activation(out=sqt[kc][:], in_=xt[:, kc, h * 512:(h + 1) * 512], func=AF.Square),
                                               reads=[("xt", kc)], writes=[("sq", kc)])
                                        else:
                                            op("dve", lambda e, kc=kc: e.tensor_tensor(out=sqt[kc][:], in0=xt[:, kc, h * 512:(h + 1) * 512],
                                                                                       in1=xt[:, kc, h * 512:(h + 1) * 512], op=ALU.mult),
                                               reads=[("xt", kc)], writes=[("sq", kc)])
                                        op("pe", lambda e, kc=kc: e.matmul(ps[:], lhsT=ones, rhs=sqt[kc][:], start=(kc == 0), stop=(kc == 7)),
                                           reads=[("sq", kc), ("cb", None)], writes=[pk])
                                    op("act", lambda e: e.activation(out=sil[0][:], in_=ps[:], func=AF.Ln, scale=1.0 / D, bias=1e-6),
                                       reads=[pk], writes=[("sil", 0)])
                                    op("act", lambda e: e.activation(out=sil[1][:], in_=sil[0][:], func=AF.Exp, scale=-0.5),
                                       reads=[("sil", 0)], writes=[("sil", 1)])
                                    for kc in range(8):
                                        op("dve", lambda e, kc=kc: e.scalar_tensor_tensor(
                                            out=hst[:, kc % 4, :], in0=xt[:, kc, h * 512:(h + 1) * 512],
                                            scalar=vec[l][:, 8 + kc:9 + kc], in1=sil[1][:], op0=ALU.mult, op1=ALU.mult),
                                           reads=[("xt", kc), ("sil", 1), ("vec", l)], writes=[("hst", kc % 4)])
                                        op("sp", lambda e, kc=kc: e.dma_start(
                                            out=hmd.ap()[kc * 128:(kc + 1) * 128, c0 + h * 512:c0 + (h + 1) * 512], in_=hst[:, kc % 4, :]),
                                           reads=[("hst", kc % 4)], writes=[("hdram", (kc, tt, h))], dma=True)
                            pending.append(lambda ep=epilogue: ep(0))
                            pending.append(lambda ep=epilogue: ep(1))
                    else:
                        for h in range(2):
                            ps, pk = psnext()
                            for kc in range(8):
                                op("dve", lambda e, kc=kc: e.tensor_tensor(out=sqt[kc][:], in0=xt[:, kc, h * 512:(h + 1) * 512],
                                                                           in1=xt[:, kc, h * 512:(h + 1) * 512], op=ALU.mult),
                                   reads=[("xt", kc)], writes=[("sq", kc)])
                                op("pe", lambda e, kc=kc: e.matmul(ps[:], lhsT=ones, rhs=sqt[kc][:], start=(kc == 0), stop=(kc == 7)),
                                   reads=[("sq", kc), ("cb", None)], writes=[pk])
                            op("act", lambda e: e.activation(out=sil[0][:], in_=ps[:], func=AF.Ln, scale=1.0 / D, bias=1e-6),
                               reads=[pk], writes=[("sil", 0)])
                            op("act", lambda e: e.activation(out=sil[1][:], in_=sil[0][:], func=AF.Exp, scale=-0.5),
                               reads=[("sil", 0)], writes=[("sil", 1)])
                            for kc in range(8):
                                op("dve", lambda e, kc=kc: e.scalar_tensor_tensor(
                                    out=xt[:, kc, h * 512:(h + 1) * 512], in0=xt[:, kc, h * 512:(h + 1) * 512],
                                    scalar=vec[l][:, 24 + kc:25 + kc], in1=sil[1][:], op0=ALU.mult, op1=ALU.mult),
                                   reads=[("xt", kc), ("sil", 1), ("vec", l)], writes=[("xt", kc)])
                        for kc in range(8):
                            op("sp", lambda e, kc=kc, c0=c0: e.dma_start(out=outd.ap()[kc * 128:(kc + 1) * 128, c0:c0 + TT], in_=xt[:, kc, :]),
                               reads=[("xt", kc)], writes=[("odram", (kc, tt))], dma=True)
                while pending:
                    pending.pop(0)()
                sch.barrier()

        def evac(i, out_ap, in_ap, reads, writes, scale=None):
            if i % 2 == 0:
                if scale is None:
                    op("act", lambda e: e.copy(out=out_ap, in_=in_ap), reads=reads, writes=writes)
                else:
                    op("act", lambda e: e.mul(out=out_ap, in_=in_ap, mul=scale), reads=reads, writes=writes)
            else:
                if scale is None:
                    op("dve", lambda e: e.tensor_copy(out=out_ap, in_=in_ap), reads=reads, writes=writes)
                else:
                    op("dve", lambda e: e.tensor_scalar(out=out_ap, in0=in_ap, scalar1=scale, scalar2=None, op0=ALU.mult),
                       reads=reads, writes=writes)

        def mixer_phase(l, s, parts="abc"):
            c0s = s * S
            lam_init = 0.8 - 0.6 * math.exp(-0.3 * l)
            with contextlib.ExitStack() as es:
                ht = sb(es, "mht", [128, 8, S], BF16)
                cosF = sb(es, "cosF", [128, S], F32)
                sinF = sb(es, "sinF", [128, S], F32)
                lamv = sb(es, "lamv", [128, 8], F32)
                if True:
                    rstd = sb(es, "mrstd", [128, 128], F32)
                    for kc in range(8):
                        op("sp", lambda e, kc=kc: e.dma_start(out=ht[:, kc, :], in_=hmd.ap()[kc * 128:(kc + 1) * 128, c0s:c0s + S]),
                           reads=[("hdram", None)], writes=[("mht", (kc, tt)) for tt in range(4)], dma=True)
                    lt = rstd
                    for i in range(2):
                        op("dve", lambda e, i=i: e.tensor_tensor(out=lt[:, i * 64:(i + 1) * 64], in0=vec[l][:, 57 + 128 * i:57 + 128 * i + 64],
                                                                 in1=vec[l][:, 57 + 128 * i + 64:57 + 128 * i + 128], op=ALU.mult),
                           reads=[("vec", l), ("mrstd", None)], writes=[("mrstd", None)])
                        op("dve", lambda e, i=i: e.reduce_sum(out=lamv[:, 2 + i:3 + i], in_=lt[:, i * 64:(i + 1) * 64], axis=AX.X),
                           reads=[("mrstd", None)], writes=[("lamv", None)])
                    op("act", lambda e: e.activation(out=lamv[:, 4:6], in_=lamv[:, 2:4], func=AF.Exp),
                       reads=[("lamv", None)], writes=[("lamv", None)])
                    op("dve", lambda e: e.tensor_tensor(out=lamv[:, 0:1], in0=lamv[:, 5:6], in1=lamv[:, 4:5], op=ALU.subtract),
                       reads=[("lamv", None)], writes=[("lamv", None)])
                    op("dve", lambda e: e.tensor_scalar(out=lamv[:, 0:1], in0=lamv[:, 0:1], scalar1=-lam_init, scalar2=None, op0=ALU.add),
                       reads=[("lamv", None)], writes=[("lamv", None)])
                    op("dve", lambda e: e.tensor_scalar(out=lamv[:, 1:2], in0=vec[l][:, 56:57], scalar1=1.0 - lam_init, scalar2=None, op0=ALU.mult),
                       reads=[("lamv", None), ("vec", l)], writes=[("lamv", None)])

                HS = S // 2
                tabs = {}

                def tables_alloc(scope):
                    tabs["tP"] = sb(scope, "tP", [128, HS], I32)
                    tabs["tT"] = [sb(scope, "tT%d" % i, [128, HS], F32) for i in range(2)]
                    tabs["tK"] = sb(scope, "tK", [128, HS], I32)
                C1 = 6.28125
                C2 = 2.0 * math.pi - 6.28125
                tab_jobs = []

                def tables_pool(hf):
                    cs = slice(hf * HS, (hf + 1) * HS)
                    tP, tT, tK = tabs["tP"], tabs["tT"], tabs["tK"]
                    tPf = tP[:].bitcast(F32)
                    tKf = tK[:].bitcast(F32)
                    tq.append(sch.defer("sp", lambda e: e.dma_start(out=tP[:], in_=posd.ap()[s][:, cs]), writes=[("tP", None)], dma=True))
                    tq.append(sch.defer("dve", lambda e: e.tensor_copy(out=tPf, in_=tP[:]), reads=[("tP", None)], writes=[("tP", None)]))
                    tq.append(sch.defer("dve", lambda e: e.tensor_scalar(out=tPf, in0=tPf, scalar1=cf[:, 0:1], scalar2=None, op0=ALU.mult),
                       reads=[("tP", None), ("cf", None)], writes=[("tP", None)]))
                    for wi, (shift, dst, dk, signed) in enumerate(((0.0, sinF, "tabs", True), (math.pi / 2, cosF, "tabc", False))):
                        t1 = tT[wi]
                        tk = ("tT", wi)
                        tq.append(sch.defer("dve", lambda e: e.tensor_scalar(out=t1[:], in0=tPf, scalar1=shift, scalar2=None, op0=ALU.add),
                           reads=[("tP", None)], writes=[tk]))
                        tq.append(sch.defer("dve", lambda e: e.tensor_scalar(out=tKf, in0=t1[:], scalar1=1.0 / (2 * math.pi), scalar2=None, op0=ALU.mult),
                           reads=[tk], writes=[("tK", None)]))
                        tq.append(sch.defer("dve", lambda e: e.tensor_copy(out=tK[:], in_=tKf), reads=[("tK", None)], writes=[("tK", None)]))
                        tq.append(sch.defer("dve", lambda e: e.tensor_copy(out=tKf, in_=tK[:]), reads=[("tK", None)], writes=[("tK", None)]))
                        tq.append(sch.defer("dve", lambda e: e.tensor_scalar(out=tKf, in0=tKf, scalar1=-C1, scalar2=None, op0=ALU.mult),
                           reads=[("tK", None)], writes=[("tK", None)]))
                        tq.append(sch.defer("dve", lambda e: e.tensor_tensor(out=t1[:], in0=t1[:], in1=tKf, op=ALU.add),
                           reads=[("tK", None), tk], writes=[tk]))
                        tq.append(sch.defer("dve", lambda e: e.tensor_scalar(out=tKf, in0=tKf, scalar1=C2 / C1, scalar2=None, op0=ALU.mult),
                           reads=[("tK", None)], writes=[("tK", None)]))
                        tq.append(sch.defer("dve", lambda e: e.tensor_tensor(out=t1[:], in0=t1[:], in1=tKf, op=ALU.add),
                           reads=[("tK", None), tk], writes=[tk]))
                        tq.append(sch.defer("dve", lambda e: e.tensor_scalar(out=tKf, in0=t1[:], scalar1=math.pi, scalar2=-2 * math.pi, op0=ALU.is_gt, op1=ALU.mult),
                           reads=[tk], writes=[("tK", None)]))
                        tq.append(sch.defer("dve", lambda e: e.tensor_tensor(out=t1[:], in0=t1[:], in1=tKf, op=ALU.add),
                           reads=[("tK", None), tk], writes=[tk]))
                        tq.append(sch.defer("dve", lambda e: e.tensor_scalar(out=tKf, in0=t1[:], scalar1=-math.pi, scalar2=2 * math.pi, op0=ALU.is_lt, op1=ALU.mult),
                           reads=[tk], writes=[("tK", None)]))
                        tq.append(sch.defer("dve", lambda e: e.tensor_tensor(out=t1[:], in0=t1[:], in1=tKf, op=ALU.add),
                           reads=[("tK", None), tk], writes=[tk]))
                        tq.append(sch.defer("dve", lambda e: e.tensor_scalar(out=t1[:], in0=t1[:], scalar1=-3.14159, scalar2=3.14159, op0=ALU.max, op1=ALU.min),
                           reads=[tk], writes=[tk]))
                        if signed:
                            tq.append(sch.defer("dve", lambda e: e.tensor_scalar(out=t1[:], in0=t1[:], scalar1=cf[:, 1:2], scalar2=None, op0=ALU.mult),
                               reads=[tk, ("cf", None)], writes=[tk]))

                def tables_act(hf):
                    cs = slice(hf * HS, (hf + 1) * HS)
                    tT = tabs["tT"]
                    tq.append(sch.defer("act", lambda e: e.activation(out=sinF[:, cs], in_=tT[0][:], func=AF.Sin), reads=[("tT", 0)], writes=[("tabs", None)]))
                    tq.append(sch.defer("act", lambda e: e.activation(out=cosF[:, cs], in_=tT[1][:], func=AF.Sin), reads=[("tT", 1)], writes=[("tabc", None)]))

                tstate = {"n": 0}
                tq = []

                def tables_drain(n=None):
                    k = 0
                    while tq and (n is None or k < n):
                        tq.pop(0)()
                        k += 1

                def tables_step():
                    k = tstate["n"]
                    tstate["n"] += 1
                    if k == 0:
                        tables_pool(0)
                    elif k == 1:
                        tables_act(0)
                        tables_pool(1)
                    elif k == 2:
                        tables_act(1)

                def tables_finish():
                    while tstate["n"] < 3:
                        tables_step()
                    tables_drain()

                def proj_fm(wb, wk, C, cidx, dst_fn, key, evi=[0]):
                    for tt in range(4):
                        ps, pk = psnext()
                        for kc in range(8):
                            op("pe", lambda e, kc=kc, tt=tt, ps=ps: e.matmul(
                                ps[:], lhsT=wb[:, kc * C + cidx * 128:kc * C + (cidx + 1) * 128],
                                rhs=ht[:, kc, tt * 512:(tt + 1) * 512], start=(kc == 0), stop=(kc == 7)),
                               reads=[wk, ("mht", (kc, tt))], writes=[pk])
                        dst_fn(tt, ps, pk)

                def proj_v(wb, wk, C, col0, ncols, vt, vkey, stride=1, tokfn=None):
                    for blk in range(16):
                        ps, pk = psnext()
                        tk = tokfn(blk)
                        for kc in range(8):
                            op("pe", lambda e, kc=kc, ps=ps, tk=tk: e.matmul(
                                ps[:, 0:ncols], lhsT=ht[:, kc, tk], rhs=wb[:, kc * C + col0:kc * C + col0 + ncols],
                                start=(kc == 0), stop=(kc == 7)),
                               reads=[wk, (("mht", (kc, blk // 4)) if stride == 1 else ("mht", None))], writes=[pk])
                        evac(blk, vt[:, blk, 0:ncols], ps[:, 0:ncols], [pk], [(vkey, blk)])

                rope_n = {"n": 0}

                def rope_evac(psP, pkP, psS, pkS, dst_ap, dkey, tt, rt, perm=None):
                    ri = rope_n["n"] % 2
                    rope_n["n"] += 1
                    a, b = rt[ri]
                    cs = slice(tt * 512, (tt + 1) * 512)
                    op("dve", lambda e: e.tensor_tensor(out=a[:], in0=psP[:], in1=cosF[:, cs], op=ALU.mult),
                       reads=[pkP, ("tabc", None)], writes=[("ra", ri)])
                    op("dve", lambda e: e.tensor_tensor(out=b[:], in0=psS[:], in1=sinF[:, cs], op=ALU.mult),
                       reads=[pkS, ("tabs", None)], writes=[("rb", ri)])
                    if perm is None:
                        op("dve", lambda e: e.tensor_tensor(out=dst_ap, in0=a[:], in1=b[:], op=ALU.add),
                           reads=[("ra", ri), ("rb", ri)], writes=[dkey])
                    else:
                        d = perm
                        op("dve", lambda e: e.tensor_tensor(out=dst_ap, in0=a[:].rearrange("p (m c) -> p c m", c=d),
                                                             in1=b[:].rearrange("p (m c) -> p c m", c=d), op=ALU.add),
                           reads=[("ra", ri), ("rb", ri)], writes=[dkey])

                def merge_all(branches):
                    with contextlib.ExitStack() as es3:
                        macc = sb(es3, "macc", [128, 8, S], BF16)
                        xt1 = sb(es3, "gx", [128, 8, 512], F32)
                        sg = [sb(es3, "gs%d" % i, [128, 512], F32) for i in range(2)]
                        pr = [sb(es3, "gp%d" % i, [128, 512], BF16) for i in range(2)]
                        n = 0
                        for bidx, (b, ychunks, ykey) in enumerate(branches):
                            nk = len(ychunks)
                            bi = "abc".index(b)
                            wg, wgk = wload(l, "g_" + b)
                            wu, wuk = wload(l, "up_" + b)
                            for tt in range(4):
                                cs = slice(tt * 512, (tt + 1) * 512)
                                for dc in range(8):
                                    pg, pgk = psnext()
                                    for kc in range(8):
                                        op("pe", lambda e, kc=kc: e.matmul(
                                            pg[:], lhsT=wg[:, kc * 1024 + dc * 128:kc * 1024 + (dc + 1) * 128],
                                            rhs=ht[:, kc, cs], start=(kc == 0), stop=(kc == 7)),
                                           reads=[wgk, ("mht", None)], writes=[pgk])
                                    sgt = sg[n % 2]
                                    prt = pr[n % 2]
                                    sgk = ("gs", n % 2)
                                    prk = ("gp", n % 2)
                                    n += 1
                                    op("act", lambda e: e.activation(out=sgt[:], in_=pg[:], func=AF.Sigmoid,
                                                                     bias=vec[l][:, 32 + bi * 8 + dc:33 + bi * 8 + dc]),
                                       reads=[pgk, ("vec", l)], writes=[sgk])
                                    pu, puk = psnext()
                                    for j in range(nk):
                                        op("pe", lambda e, j=j: e.matmul(
                                            pu[:], lhsT=wu[:, j * 1024 + dc * 128:j * 1024 + (dc + 1) * 128],
                                            rhs=ychunks[j][:, cs], start=(j == 0), stop=(j == nk - 1)),
                                           reads=[wuk, ykey], writes=[puk])
                                    if bidx == 0:
                                        op("dve", lambda e: e.tensor_tensor(out=macc[:, dc, cs], in0=pu[:], in1=sgt[:], op=ALU.mult),
                                           reads=[puk, sgk], writes=[("macc", (dc, tt))])
                                    else:
                                        op("dve", lambda e: e.tensor_tensor(out=prt[:], in0=pu[:], in1=sgt[:], op=ALU.mult),
                                           reads=[puk, sgk], writes=[prk])
                                        op("dve", lambda e: e.tensor_tensor(out=macc[:, dc, cs], in0=macc[:, dc, cs], in1=prt[:], op=ALU.add),
                                           reads=[prk, ("macc", (dc, tt))], writes=[("macc", (dc, tt))])
                        wo, wok = wload(l, "wo_a")
                        for tt in range(4):
                            cs = slice(tt * 512, (tt + 1) * 512)
                            for kc in range(8):
                                op("sp", lambda e, kc=kc: e.dma_start(
                                    out=xt1[:, kc, :], in_=xa.ap()[kc * 128:(kc + 1) * 128, c0s + tt * 512:c0s + (tt + 1) * 512]),
                                   reads=[("xdram", (kc, s, tt))], writes=[("gx", kc)], dma=True)
                            for dc2 in range(8):
                                pz, pzk = psnext()
                                for dc in range(8):
                                    op("pe", lambda e, dc=dc: e.matmul(
                                        pz[:], lhsT=wo[:, dc * 1024 + dc2 * 128:dc * 1024 + (dc2 + 1) * 128],
                                        rhs=macc[:, dc, cs], start=(dc == 0), stop=(dc == 7)),
                                       reads=[wok, ("macc", (dc, tt))], writes=[pzk])
                                op("dve", lambda e: e.tensor_tensor(out=xt1[:, dc2, :], in0=pz[:], in1=xt1[:, dc2, :], op=ALU.add),
                                   reads=[pzk, ("gx", dc2)], writes=[("gx", dc2)])
                                op("sp", lambda e: e.dma_start(
                                    out=xa.ap()[dc2 * 128:(dc2 + 1) * 128, c0s + tt * 512:c0s + (tt + 1) * 512], in_=xt1[:, dc2, :]),
                                   reads=[("gx", dc2)], writes=[("xdram", (dc2, s, tt))], dma=True)
                    sch.barrier()

                if "a" in parts:
                    if True:
                        ya = sb(es, "ya", [128, 4, S], BF16)
                        with contextlib.ExitStack() as es3:
                            tables_alloc(es3)
                            va = sb(es3, "va", [128, 16, 512], BF16)
                            qts = [sb(es3, "aq%d" % i, [128, S], BF16) for i in range(2)]
                            kts = [sb(es3, "ak%d" % i, [128, S], BF16) for i in range(2)]
                            et = [sb(es3, "ae%d" % i, [128, 2, 512], F32) for i in range(2)]
                            spt = [sb(es3, "asp%d" % i, [128, 2, 512], BF16) for i in range(3)]
                            wt = [sb(es3, "aw%d" % i, [128, 2, 512], BF16) for i in range(3)]
                            Rt = [sb(es3, "aR%d" % i, [128, 2, 512], BF16) for i in range(4)]
                            wb, wk = wload(l, "a_v")
                            proj_v(wb, wk, 512, 0, 512, va, "va", tokfn=lambda blk: slice(blk * 128, (blk + 1) * 128))
                            wb, wk = wload(l, "a_qk")
                            ZB = [0, 2, 4]
                            OB = [6, 7]
                            def proj_items(hp):
                                return [dict(kind="proj", hp=hp, tt=tt, which=which) for tt in range(4) for which in range(2)]

                            def step_items(hp, gi0):
                                out = []
                                gi = gi0
                                for qc in range(4):
                                    kbs = list(range(4 * qc + 3, -1, -1))
                                    n = len(kbs)

                                    def rng_of(kb, qc=qc):
                                        lo = 128 * (kb - 4 * qc) if kb >= 4 * qc else 0
                                        return lo, 512 - lo

                                    for i, kb in enumerate(kbs):
                                        lo, w = rng_of(kb)
                                        nxt = rng_of(kbs[i + 1]) if i + 1 < n else None
                                        out.append(dict(kind="step", hp=hp, qc=qc, i=i, n=n, kb=kb, lo=lo, w=w, nxt=nxt,
                                                        diag=(kb >= 4 * qc), gi=gi))
                                    gi += 1
                                return out, gi

                            items = proj_items(0)
                            gi = 0
                            for hp in range(4):
                                st, gi = step_items(hp, gi)
                                pj = proj_items(hp + 1) if hp + 1 < 4 else []
                                for k, it in enumerate(st):
                                    items.append(it)
                                    if pj and k % 4 == 3:
                                        items.append(pj.pop(0))
                                items.extend(pj)
                            prev_real = None
                            for g_, it in enumerate(items):
                                it["g"] = g_
                                if it["kind"] == "step":
                                    it["gprev"] = prev_real["g"] if (prev_real is not None and it["i"] > 0) else None
                                    if prev_real is not None and it["i"] > 0:
                                        prev_real["gnext"] = g_
                                    it["gnext"] = None
                                    prev_real = it
                            NI = len(items)

                            def zview(g_, w):
                                zb = ZB[g_ % 3]
                                return pst[:, zb * 512:(zb + 2) * 512].rearrange("p (h c) -> p h c", h=2)[:, :, 0:w]

                            def stA(g_):
                                it = items[g_]
                                zb = ZB[g_ % 3]
                                if it["kind"] == "proj":
                                    hp, tt, which = it["hp"], it["tt"], it["which"]
                                    dst = (qts if which == 0 else kts)[hp % 2]
                                    dk = "aq" if which == 0 else "ak"
                                    ps = PS[zb]
                                    for kc in range(8):
                                        op("pe", lambda e, kc=kc: e.matmul(
                                            ps[:], lhsT=wb[:, kc * 1024 + (2 * hp + which) * 128:kc * 1024 + (2 * hp + which + 1) * 128],
                                            rhs=ht[:, kc, tt * 512:(tt + 1) * 512], start=(kc == 0), stop=(kc == 7)),
                                           reads=[wk, ("mht", None)], writes=[("ps", zb)])
                                    evac(1, dst[:, tt * 512:(tt + 1) * 512], ps[:], [("ps", zb)], [(dk, (hp % 2, tt))],
                                         scale=(0.125 if which == 0 else None))
                                    return
                                hp, qc, i, n, kb, lo, w = it["hp"], it["qc"], it["i"], it["n"], it["kb"], it["lo"], it["w"]
                                qt, kt = qts[hp % 2], kts[hp % 2]
                                if g_ >= 12:
                                    tables_drain(1)
                                for hh in range(2):
                                    hb = slice(64 * hh, 64 * hh + 64)
                                    op("pe", lambda e: e.matmul(PS[zb + hh][:, 0:w], lhsT=kt[hb, kb * 128:(kb + 1) * 128],
                                                                rhs=qt[hb, qc * 512 + lo:qc * 512 + 512], start=True, stop=True),
                                       reads=[("ak", (hp % 2, kb // 4)), ("aq", (hp % 2, qc))], writes=[("ps", zb + hh)])
                                ee = et[g_ % 2]
                                sp = spt[g_ % 3]
                                op("act", lambda e: e.activation(out=ee[:, :, 0:w], in_=zview(g_, w), func=AF.Exp),
                                   reads=[("ps", zb), ("ps", zb + 1)], writes=[("ae", g_ % 2)])
                                op("act", lambda e: e.activation(out=sp[:, :, 0:w], in_=ee[:, :, 0:w], func=AF.Ln, bias=1.0),
                                   reads=[("ae", g_ % 2)], writes=[("asp", g_ % 3)])
                                if it["diag"]:
                                    for hh in range(2):
                                        op("dve", lambda e: e.tensor_tensor(out=sp[:, hh, 0:128], in0=sp[:, hh, 0:128], in1=mstrict, op=ALU.mult),
                                           reads=[("asp", g_ % 3), ("cb", None)], writes=[("asp", g_ % 3)])
                                if it["nxt"] is not None:
                                    lo2, w2 = it["nxt"]
                                    gn = it["gnext"]
                                    Rn = Rt[gn % 4]
                                    Rp = Rt[g_ % 4]
                                    if lo2 < lo:
                                        op("dve", lambda e: e.memset(Rn[:, :, 0:lo - lo2], 0.0), writes=[("aR", gn % 4)])
                                    if i == 0:
                                        op("dve", lambda e: e.tensor_copy(out=Rn[:, :, lo - lo2:w2], in_=sp[:, :, 0:w]),
                                           reads=[("asp", g_ % 3)], writes=[("aR", gn % 4)])
                                    else:
                                        op("dve", lambda e: e.tensor_tensor(out=Rn[:, :, lo - lo2:w2], in0=Rp[:, :, 0:w], in1=sp[:, :, 0:w], op=ALU.add),
                                           reads=[("asp", g_ % 3), ("aR", g_ % 4)], writes=[("aR", gn % 4)])

                            def stC(g_):
                                it = items[g_]
                                if it["kind"] == "proj":
                                    return
                                i, kb, lo, w = it["i"], it["kb"], it["lo"], it["w"]
                                zb = ZB[g_ % 3]
                                sp = spt[g_ % 3]
                                Rp = Rt[g_ % 4]
                                for hh in range(2):
                                    op("pe", lambda e: e.matmul(PS[zb + hh][:, 0:w], lhsT=trineg, rhs=sp[:, hh, 0:w], start=False, stop=(i == 0), skip_group_check=True),
                                       reads=[("asp", g_ % 3), ("cb", None)], writes=[("ps", zb + hh)])
                                    if i > 0:
                                        op("pe", lambda e: e.matmul(PS[zb + hh][:, 0:w], lhsT=negones, rhs=Rp[:, hh, 0:w], start=False, stop=True, skip_group_check=True),
                                           reads=[("aR", g_ % 4), ("cb", None)], writes=[("ps", zb + hh)])
                                ww = wt[g_ % 3]
                                op("act", lambda e: e.activation(out=ww[:, :, 0:w], in_=zview(g_, w), func=AF.Exp),
                                   reads=[("ps", zb), ("ps", zb + 1)], writes=[("aw", g_ % 3)])
                                if it["diag"]:
                                    for hh in range(2):
                                        op("dve", lambda e: e.tensor_tensor(out=ww[:, hh, 0:128], in0=ww[:, hh, 0:128], in1=mstrict, op=ALU.mult),
                                           reads=[("aw", g_ % 3), ("cb", None)], writes=[("aw", g_ % 3)])

                            def stE(g_):
                                it = items[g_]
                                if it["kind"] == "proj":
                                    return
                                hp, qc, i, n, kb, lo, w = it["hp"], it["qc"], it["i"], it["n"], it["kb"], it["lo"], it["w"]
                                ob = OB[it["gi"] % 2]
                                po, pok = PS[ob], ("ps", ob)
                                ww = wt[g_ % 3]
                                for hh in range(2):
                                    hb = slice(64 * hh, 64 * hh + 64)
                                    h = 2 * hp + hh
                                    op("pe", lambda e: e.matmul(po[hb, lo:512], lhsT=va[:, kb, h * 64:(h + 1) * 64], rhs=ww[:, hh, 0:w],
                                                                start=(i == 0), stop=(i == n - 1), skip_group_check=True),
                                       reads=[("aw", g_ % 3), ("va", kb)], writes=[pok])
                                if i == n - 1:
                                    evac(it["gi"], ya[:, hp, qc * 512:(qc + 1) * 512], po[:, :], [pok], [("ya", (hp, qc))])

                            while tstate["n"] < 3:
                                tables_step()
                            for t in range(NI + 2):
                                if t < NI:
                                    stA(t)
                                if 0 <= t - 1 < NI:
                                    stC(t - 1)
                                if 0 <= t - 2 < NI:
                                    stE(t - 2)
                        sch.barrier()

                if tstate["n"] < 3:
                    with contextlib.ExitStack() as est:
                        if "tP" not in tabs or "a" not in parts:
                            tables_alloc(est)
                        tables_finish()
                        sch.barrier()
                if "b" in parts:
                    if True:
                        yb = sb(es, "yb", [128, 2, S], BF16)
                        with contextlib.ExitStack() as es3:
                            vb = [sb(es3, "vb%d" % g, [128, 16, 256], BF16) for g in range(3)]
                            qbt = sb(es3, "bq", [128, S], BF16)
                            kbt = sb(es3, "bk", [128, S], BF16)
                            rt = [(sb(es3, "ra%d" % i, [128, 512], F32), sb(es3, "rb%d" % i, [128, 512], F32)) for i in range(2)]
                            pt = [sb(es3, "bp%d" % i, [128, 2, 256], BF16) for i in range(3)]
                            snum = sb(es3, "snum", [128, S], F32)
                            sden = sb(es3, "sden", [128, S], F32)
                            wb, wk = wload(l, "b_v")
                            for g, d in enumerate((1, 4, 16)):
                                Lg = S // d
                                nbg = Lg // 128

                                def tokfn(blk, d=d, nbg=nbg):
                                    c, b = blk // nbg, blk % nbg
                                    st = c + 128 * b * d
                                    return slice(st, st + 127 * d + 1, d) if d > 1 else slice(st, st + 128)

                                proj_v(wb, wk, 768, 256 * g, 256, vb[g], "vb%d" % g, stride=d, tokfn=tokfn)
                            wbs = [wload(l, "b_qk%d" % j) for j in range(3)]
                            SS = [(PS[0], ("ps", 0)), (PS[1], ("ps", 1))]
                            for pi in range(2):
                                for g, d in enumerate((1, 4, 16)):
                                    wb, wk = wbs[g]
                                    Lg = S // d
                                    nbg = Lg // 128
                                    for which, dstT, dkey in ((0, qbt, "bq"), (1, kbt, "bk")):
                                        for tt in range(4):
                                            pP, pPk = psnext()
                                            pS, pSk = psnext()
                                            for kc in range(8):
                                                for (pp, ppk, cidx) in ((pP, pPk, pi * 4 + which * 2), (pS, pSk, pi * 4 + which * 2 + 1)):
                                                    op("pe", lambda e, kc=kc, tt=tt, pp=pp, cidx=cidx, wb=wb: e.matmul(
                                                        pp[:], lhsT=wb[:, kc * 1024 + cidx * 128:kc * 1024 + (cidx + 1) * 128],
                                                        rhs=ht[:, kc, tt * 512:(tt + 1) * 512], start=(kc == 0), stop=(kc == 7)),
                                                       reads=[wk, ("mht", (kc, tt))], writes=[ppk])
                                            if d == 1:
                                                dst = dstT[:, tt * 512:(tt + 1) * 512]
                                                rope_evac(pP, pPk, pS, pSk, dst, (dkey, None), tt, rt)
                                            else:
                                                m = 512 // d
                                                dst = dstT[:].rearrange("p (c l) -> p c l", c=d)[:, :, tt * m:(tt + 1) * m]
                                                rope_evac(pP, pPk, pS, pSk, dst, (dkey, None), tt, rt, perm=d)
                                    jobs = []
                                    gi = 0
                                    for c in range(d):
                                        for QT in range((Lg + 511) // 512):
                                            wqt = min(512, Lg - 512 * QT)
                                            kbs = list(range(max(0, 4 * QT - 1), min(4 * QT + 3, nbg - 1) + 1))
                                            for ji, kb in enumerate(kbs):
                                                lo = max(128 * kb, 512 * QT) - 512 * QT
                                                hi = min(128 * kb + 256, 512 * QT + wqt) - 512 * QT
                                                mc0 = (512 * QT + lo) - 128 * kb
                                                jobs.append(dict(c=c, QT=QT, kb=kb, lo=lo, hi=hi, mc0=mc0, first=(ji == 0),
                                                                 last=(ji == len(kbs) - 1), gi=gi, wqt=wqt))
                                            gi += 1
                                    nj = len(jobs)

                                    def s0(i):
                                        J = jobs[i]
                                        w = J["hi"] - J["lo"]
                                        zb = 2 * (i % 2)
                                        kc0 = J["c"] * Lg + 128 * J["kb"]
                                        base = J["c"] * Lg + 512 * J["QT"]
                                        for hh in range(2):
                                            hb = slice(64 * hh, 64 * hh + 64)
                                            op("pe", lambda e: e.matmul(PS[zb + hh][:, 0:w], lhsT=kbt[hb, kc0:kc0 + 128],
                                                                        rhs=qbt[hb, base + J["lo"]:base + J["hi"]], start=True, stop=True),
                                               reads=[("bk", None), ("bq", None)], writes=[("ps", zb + hh)])
                                        p = pt[i % 3]
                                        zv = pst[:, zb * 512:(zb + 2) * 512].rearrange("p (h c) -> p h c", h=2)[:, :, 0:w]
                                        op("act", lambda e: e.activation(out=p[:, :, 0:w], in_=zv, func=AF.Exp, scale=0.125),
                                           reads=[("ps", zb), ("ps", zb + 1)], writes=[("bp", i % 3)])
                                        op("dve", lambda e: e.tensor_tensor(out=p[:, :, 0:w], in0=p[:, :, 0:w], in1=mband2[:, :, J["mc0"]:J["mc0"] + w], op=ALU.mult),
                                           reads=[("bp", i % 3), ("cb", None)], writes=[("bp", i % 3)])

                                    def s1(i):
                                        J = jobs[i]
                                        w = J["hi"] - J["lo"]
                                        lo, hi = J["lo"], J["hi"]
                                        p = pt[i % 3]
                                        blk = J["c"] * nbg + J["kb"]
                                        nb_ = 4 + (J["gi"] % 2)
                                        db_ = 6 + (J["gi"] % 2)
                                        pn, pd = PS[nb_], PS[db_]
                                        for hh in range(2):
                                            hb = slice(64 * hh, 64 * hh + 64)
                                            hloc = 2 * pi + hh
                                            op("pe", lambda e: e.matmul(pn[hb, lo:hi], lhsT=vb[g][:, blk, hloc * 64:(hloc + 1) * 64], rhs=p[:, hh, 0:w],
                                                                        start=J["first"], stop=J["last"], skip_group_check=True),
                                               reads=[("bp", i % 3), ("vb%d" % g, blk)], writes=[("ps", nb_)])
                                            op("pe", lambda e: e.matmul(pd[hb, lo:hi], lhsT=ones[:, 0:64], rhs=p[:, hh, 0:w],
                                                                        start=J["first"], stop=J["last"], skip_group_check=True),
                                               reads=[("bp", i % 3), ("cb", None)], writes=[("ps", db_)])
                                        if J["last"]:
                                            wqt = J["wqt"]
                                            n0 = 512 * J["QT"] * d + J["c"]
                                            nsl = slice(n0, n0 + wqt) if d == 1 else slice(n0, n0 + (wqt - 1) * d + 1, d)
                                            if g == 0:
                                                op("act", lambda e: e.copy(out=snum[:, nsl], in_=pn[:, 0:wqt]), reads=[("ps", nb_)], writes=[("snum", None)])
                                                op("dve", lambda e: e.tensor_copy(out=sden[:, nsl], in_=pd[:, 0:wqt]), reads=[("ps", db_)], writes=[("sden", None)])
                                            else:
                                                op("dve", lambda e: e.tensor_tensor(out=snum[:, nsl], in0=pn[:, 0:wqt], in1=snum[:, nsl], op=ALU.add),
                                                   reads=[("ps", nb_), ("snum", None)], writes=[("snum", None)])
                                                op("dve", lambda e: e.tensor_tensor(out=sden[:, nsl], in0=pd[:, 0:wqt], in1=sden[:, nsl], op=ALU.add),
                                                   reads=[("ps", db_), ("sden", None)], writes=[("sden", None)])

                                    for t in range(nj + 2):
                                        if t < nj:
                                            s0(t)
                                        if t >= 2:
                                            s1(t - 2)
                                op("act", lambda e: e.activation(out=sden[:], in_=sden[:], func=AF.Ln), reads=[("sden", None)], writes=[("sden", None)])
                                op("act", lambda e: e.activation(out=sden[:], in_=sden[:], func=AF.Exp, scale=-1.0), reads=[("sden", None)], writes=[("sden", None)])
                                op("dve", lambda e, pi=pi: e.tensor_tensor(out=yb[:, pi, :], in0=snum[:], in1=sden[:], op=ALU.mult),
                                   reads=[("sden", None), ("snum", None)], writes=[("yb", pi)])
                        sch.barrier()

                if "c" in parts:
                    if True:
                        yc = sb(es, "yc", [128, 4, S], BF16)
                        with contextlib.ExitStack() as es3:
                            vc = sb(es3, "vc", [128, 16, 512], BF16)
                            qk = [sb(es3, "cqk%d" % i, [128, S], BF16) for i in range(4)]
                            rt = [(sb(es3, "ra%d" % i, [128, 512], F32), sb(es3, "rb%d" % i, [128, 512], F32)) for i in range(2)]
                            pt = [sb(es3, "cp%d" % i, [128, 2, 512], BF16) for i in range(3)]
                            rr = [sb(es3, "crr%d" % i, [128, 512], F32) for i in range(2)]
                            oo = [sb(es3, "coo%d" % i, [128, 512], F32) for i in range(3)]
                            osq = sb(es3, "cosq", [128, 512], BF16)
                            wb, wk = wload(l, "c_v")
                            proj_v(wb, wk, 512, 0, 512, vc, "vc", tokfn=lambda blk: slice(blk * 128, (blk + 1) * 128))
                            PO = [(PS[0], ("ps", 0)), (PS[1], ("ps", 1))]
                            PD = [(PS[2], ("ps", 2)), (PS[3], ("ps", 3))]
                            wbs = {}
                            items = []
                            for hp in range(2):
                                for m4 in range(4):
                                    for tt in range(4):
                                        items.append(dict(kind="proj", hp=hp, m4=m4, tt=tt))
                                for hh in range(2):
                                    for qc in range(4):
                                        nsteps = 4 * qc + 4
                                        for kb in range(nsteps):
                                            lo = 128 * (kb - 4 * qc) if kb >= 4 * qc else 0
                                            items.append(dict(kind="step", hp=hp, hh=hh, qc=qc, kb=kb, lo=lo, w=512 - lo,
                                                              first=(kb == 0), last=(kb == nsteps - 1), diag=(kb >= 4 * qc)))
                                        items.append(dict(kind="post", hp=hp, hh=hh, qc=qc))
                            NI = len(items)
                            deferred = {}

                            def c0(g_):
                                it = items[g_]
                                zb = 4 + 2 * (g_ % 2)
                                if it["kind"] == "proj":
                                    hp, m4, tt = it["hp"], it["m4"], it["tt"]
                                    if hp not in wbs:
                                        wbs[hp] = wload(l, "c_qk%d" % hp)
                                    wb, wk = wbs[hp]
                                    pP, pPk = PS[zb], ("ps", zb)
                                    pS, pSk = PS[zb + 1], ("ps", zb + 1)
                                    for kc in range(8):
                                        for (pp, ppk, cidx) in ((pP, pPk, 2 * m4), (pS, pSk, 2 * m4 + 1)):
                                            op("pe", lambda e: e.matmul(
                                                pp[:], lhsT=wb[:, kc * 1024 + cidx * 128:kc * 1024 + (cidx + 1) * 128],
                                                rhs=ht[:, kc, tt * 512:(tt + 1) * 512], start=(kc == 0), stop=(kc == 7)),
                                               reads=[wk, ("mht", None)], writes=[ppk])
                                    rope_evac(pP, pPk, pS, pSk, qk[m4][:, tt * 512:(tt + 1) * 512], ("cqk", m4), tt, rt)
                                    return
                                if it["kind"] == "post":
                                    return
                                hb = slice(64 * it["hh"], 64 * it["hh"] + 64)
                                qc, kb, lo, w = it["qc"], it["kb"], it["lo"], it["w"]
                                p = pt[g_ % 3]
                                for m in range(2):
                                    op("pe", lambda e: e.matmul(PS[zb + m][:, 0:w], lhsT=qk[2 + m][hb, kb * 128:(kb + 1) * 128],
                                                                rhs=qk[m][hb, qc * 512 + lo:qc * 512 + 512], start=True, stop=True),
                                       reads=[("cqk", 2 + m), ("cqk", m)], writes=[("ps", zb + m)])
                                zv = pst[:, zb * 512:(zb + 2) * 512].rearrange("p (h c) -> p h c", h=2)[:, :, 0:w]
                                op("act", lambda e: e.activation(out=p[:, :, 0:w], in_=zv, func=AF.Exp, scale=0.125),
                                   reads=[("ps", zb), ("ps", zb + 1)], writes=[("cp", g_ % 3)])
                                if it["diag"]:
                                    for m in range(2):
                                        op("dve", lambda e: e.tensor_tensor(out=p[:, m, 0:128], in0=p[:, m, 0:128], in1=mincl, op=ALU.mult),
                                           reads=[("cp", g_ % 3), ("cb", None)], writes=[("cp", g_ % 3)])

                            def c1(g_):
                                it = items[g_]
                                if it["kind"] == "proj":
                                    return
                                h = 2 * it["hp"] + it["hh"]
                                qc = it["qc"]
                                if it["kind"] == "step":
                                    kb, lo, w = it["kb"], it["lo"], it["w"]
                                    p = pt[g_ % 3]
                                    for m in range(2):
                                        po, pok = PO[m]
                                        pd, pdk = PD[m]
                                        op("pe", lambda e: e.matmul(po[:, lo:512], lhsT=vc[:, kb, h * 128:(h + 1) * 128], rhs=p[:, m, 0:w],
                                                                    start=it["first"], stop=it["last"]),
                                           reads=[("cp", g_ % 3), ("vc", kb)], writes=[pok])
                                        op("pe", lambda e: e.matmul(pd[:, lo:512], lhsT=ones, rhs=p[:, m, 0:w],
                                                                    start=it["first"], stop=it["last"]),
                                           reads=[("cp", g_ % 3), ("cb", None)], writes=[pdk])
                                    return
                                for m in range(2):
                                    pd, pdk = PD[m]
                                    op("act", lambda e: e.activation(out=rr[m][:], in_=pd[:], func=AF.Ln), reads=[pdk], writes=[("crr", m)])
                                for m in range(2):
                                    po, pok = PO[m]
                                    op("dve", lambda e: e.tensor_copy(out=oo[m][:], in_=po[:]), reads=[pok], writes=[("coo", m)])
                                for m in range(2):
                                    op("act", lambda e: e.activation(out=rr[m][:], in_=rr[m][:], func=AF.Exp, scale=-1.0),
                                       reads=[("crr", m)], writes=[("crr", m)])
                                op("dve", lambda e: e.tensor_tensor(out=oo[1][:], in0=oo[1][:], in1=rr[1][:], op=ALU.mult),
                                   reads=[("coo", 1), ("crr", 1)], writes=[("coo", 1)])
                                op("dve", lambda e: e.tensor_tensor(out=oo[0][:], in0=oo[0][:], in1=rr[0][:], op=ALU.mult),
                                   reads=[("coo", 0), ("crr", 0)], writes=[("coo", 0)])
                                op("dve", lambda e: e.scalar_tensor_tensor(out=oo[2][:], in0=oo[1][:], scalar=lamv[:, 0:1], in1=oo[0][:],
                                                                            op0=ALU.mult, op1=ALU.add),
                                   reads=[("coo", 0), ("coo", 1), ("lamv", None)], writes=[("coo", 2)])
                                op("dve", lambda e: e.tensor_tensor(out=osq[:], in0=oo[2][:], in1=oo[2][:], op=ALU.mult),
                                   reads=[("coo", 2)], writes=[("cosq", None)])
                                def post2(g2, h=h, qc=qc):
                                    zb = 4 + 2 * (g2 % 2)
                                    pss, pssk = PS[zb], ("ps", zb)
                                    op("pe", lambda e: e.matmul(pss[:], lhsT=ones, rhs=osq[:], start=True, stop=True),
                                       reads=[("cosq", None), ("cb", None)], writes=[pssk])
                                    op("act", lambda e: e.activation(out=oo[0][:], in_=pss[:], func=AF.Ln, scale=1.0 / 128, bias=1e-5),
                                       reads=[pssk], writes=[("coo", 0)])
                                    op("act", lambda e: e.activation(out=oo[0][:], in_=oo[0][:], func=AF.Exp, scale=-0.5),
                                       reads=[("coo", 0)], writes=[("coo", 0)])
                                    op("dve", lambda e: e.scalar_tensor_tensor(out=yc[:, h, qc * 512:(qc + 1) * 512], in0=oo[2][:], scalar=lamv[:, 1:2],
                                                                                in1=oo[0][:], op0=ALU.mult, op1=ALU.mult),
                                       reads=[("coo", 2), ("coo", 0), ("lamv", None)], writes=[("yc", (h, qc))])

                                deferred[g_ + 3] = post2

                            for t in range(NI + 6):
                                if t < NI:
                                    c0(t)
                                if 2 <= t < NI + 2:
                                    c1(t - 2)
                                if (t - 2) in deferred:
                                    deferred.pop(t - 2)(t - 2)
                            assert not deferred
                        sch.barrier()
                branches = []
                if "a" in parts:
                    branches.append(("a", [ya[:, j, :] for j in range(4)], ("ya", None)))
                if "b" in parts:
                    branches.append(("b", [yb[:, j, :] for j in range(2)], ("yb", None)))
                if "c" in parts:
                    branches.append(("c", [yc[:, j, :] for j in range(4)], ("yc", None)))
                merge_all(branches)
            sch.barrier()

        prog = []
        for l in range(DEPTH):
            prog.append(("ffn", l, 1))
            for s in range(NSEQ):
                prog.append(("mix", l, s))
            prog.append(("ffn", l, 2))
        if stop_after is not None:
            prog = prog[:stop_after]
        src = xin
        for pi, item in enumerate(prog):
            last = pi == len(prog) - 1
            if item[0] == "ffn":
                ffn_phase(item[1], item[2], src, final=last, emit_hm=(item[2] == 1 and not last))
                src = xa
            else:
                mixer_phase(item[1], item[2], parts=MIX_PARTS)
                if last:
                    for kc in range(8):
                        op("sp", lambda e, kc=kc: e.dma_start(out=outd.ap()[kc * 128:(kc + 1) * 128, :], in_=xa.ap()[kc * 128:(kc + 1) * 128, :]),
                           reads=[("xdram", None)], writes=[("odram", None)], dma=True)
        sch.emit()
    return nc


MIX_PARTS = "abc"
_CACHE = {}


def kernel(**inputs):
    inp = {k: np.asarray(v) for k, v in inputs.items()}
    B = inp["x"].shape[0]
    nseq = B // NCORES
    depth = inp["w_in"].shape[0]
    key = (nseq, depth)
    if key not in _CACHE:
        _CACHE[key] = build_program(NSEQ=nseq, DEPTH=depth)
    nc = _CACHE[key]
    consts = _build_consts()
    wimgs = [_build_wimg(inp, l) for l in range(depth)]
    vecs = [_build_vecs(inp, l) for l in range(depth)]
    in_maps = []
    for c in range(NCORES):
        xb = inp["x"][c * nseq:(c + 1) * nseq]
        xT = np.ascontiguousarray(xb.reshape(nseq * S, D).T)
        pos = inp["positions"][c * nseq:(c + 1) * nseq].astype(np.int32)
        posb = np.ascontiguousarray(np.broadcast_to(pos[:, None, :], (nseq, 128, S)))
        m = {"xT": xT, "pos": posb, "consts": consts}
        for l in range(depth):
            m["wimg%d" % l] = wimgs[l]
            m["vecs%d" % l] = vecs[l]
        in_maps.append(m)
    res = run_bass_kernel_spmd(nc, in_maps, core_ids=list(range(NCORES)))
    outs = [np.asarray(r["outT"]).T.reshape(nseq, S, D) for r in res.results]
    return np.ascontiguousarray(np.concatenate(outs, axis=0).astype(np.float32))
```

```python
import contextlib
import math
import numpy as np
import concourse.bass as bass
import concourse.mybir as mybir
from concourse.bass_utils import run_bass_kernel_spmd

F32 = mybir.dt.float32
BF16 = mybir.dt.bfloat16
I32 = mybir.dt.int32
AF = mybir.ActivationFunctionType
ALU = mybir.AluOpType
AX = mybir.AxisListType

D = 1024
S = 2048
DFF = 2816
NFC = DFF // 128
NCORES = 8
COMPUTE = ("pe", "act", "dve", "pool")
NVEC = 57 + 256
NCONST = 128 * 5 + 512


class _Rec:
    def __init__(self):
        self.call = None

    def __getattr__(self, name):
        def f(*a, **k):
            self.call = (name, a, k)
        return f


class Sched:
    def __init__(self, nc, n_dma_sems=24):
        self.nc = nc
        self.ops = []
        self.res = {}
        self.n_dma_sems = n_dma_sems
        self.last = {}
        self.dmas = []
        self.pending = {}

    def _conf(self, key):
        name, sub = key
        d = self.res.setdefault(name, {})
        if sub is None:
            return list(d.values())
        out = []
        if None in d:
            out.append(d[None])
        if sub in d:
            out.append(d[sub])
        return out

    def barrier(self, engs=("pe", "act", "dve", "sp")):
        deps = set(v for k, v in self.last.items() if k in engs)
        deps |= set(d for d in self.dmas if self.ops[d]["eng"] in engs)
        self.dmas = [d for d in self.dmas if self.ops[d]["eng"] not in engs]
        for e in engs:
            self.pending[e] = set(self.pending.get(e, set())) | deps

    def op(self, eng, fn, reads=(), writes=(), dma=False):
        oid = len(self.ops)
        raw, other = set(), set()
        for key in reads:
            for st in self._conf(key):
                if st[0] is not None:
                    raw.add(st[0])
        for key in writes:
            for st in self._conf(key):
                if st[0] is not None:
                    other.add(st[0])
                other.update(st[1])
        for key in reads:
            d = self.res.setdefault(key[0], {})
            if key[1] not in d:
                d[key[1]] = [None, []]
            for st in self._conf(key):
                st[1].append(oid)
        for key in writes:
            d = self.res.setdefault(key[0], {})
            if key[1] not in d:
                d[key[1]] = [None, []]
            for st in self._conf(key):
                st[0] = oid
                st[1] = []
        if eng in self.pending:
            other |= self.pending.pop(eng)
        other -= raw
        raw.discard(oid)
        other.discard(oid)
        rec = _Rec()
        fn(rec)
        cname, ca, ck = rec.call
        self.ops.append(dict(eng=eng, fn=(lambda e: getattr(e, cname)(*ca, **ck)), raw=raw, other=other, dma=dma))
        self.last[eng] = oid
        if dma:
            self.dmas.append(oid)
        return oid

    def defer(self, eng, fn, reads=(), writes=(), dma=False):
        rec = _Rec()
        fn(rec)
        cname, ca, ck = rec.call
        return lambda: self.op(eng, (lambda e: getattr(e, cname)(*ca, **ck)), reads, writes, dma)

    def emit(self):
        nc = self.nc
        ops = self.ops
        need = []
        for o in ops:
            w = set()
            for d in o["raw"]:
                od = ops[d]
                if od["dma"] or od["eng"] != o["eng"] or o["dma"] or o["eng"] != "pe":
                    w.add(d)
            for d in o["other"]:
                od = ops[d]
                if od["dma"] or od["eng"] != o["eng"] or o["dma"] or o["eng"] != "pe":
                    w.add(d)
            need.append(w)
        signal = [False] * len(ops)
        for w in need:
            for d in w:
                signal[d] = True
        sig_idx = [0] * len(ops)
        cnt = {e: 0 for e in COMPUTE}
        dma_rr, dma_sem, slot_val = {}, [None] * len(ops), {}
        for i, o in enumerate(ops):
            if o["dma"]:
                q = o["eng"]
                k = dma_rr.get(q, 0)
                dma_rr[q] = k + 1
                slot = (q, k % self.n_dma_sems)
                v = slot_val.get(slot, 0) + 16
                slot_val[slot] = v
                dma_sem[i] = (slot, v)
            elif signal[i]:
                cnt[o["eng"]] += 1
                sig_idx[i] = cnt[o["eng"]]
        with contextlib.ExitStack() as es:
            sems = {e: es.enter_context(nc.semaphore("s_" + e)) for e in COMPUTE}
            dsems = {}
            for q in dma_rr:
                for k in range(min(self.n_dma_sems, dma_rr[q])):
                    dsems[(q, k)] = es.enter_context(nc.semaphore("d_%s_%d" % (q, k)))
            block = es.enter_context(nc.Block())

            def run(engname, e):
                waited = {}
                for i, o in enumerate(ops):
                    if o["eng"] != engname:
                        continue
                    tgt = {}
                    for d in need[i]:
                        od = ops[d]
                        if od["dma"]:
                            slot, v = dma_sem[d]
                            key = ("d", slot)
                        else:
                            key = ("c", od["eng"])
                            v = sig_idx[d]
                        if v > tgt.get(key, 0):
                            tgt[key] = v
                    if o["dma"]:
                        slot, v = dma_sem[i]
                        if v > 16 and v - 16 > tgt.get(("d", slot), 0):
                            tgt[("d", slot)] = v - 16
                    for key, v in tgt.items():
                        if waited.get(key, 0) >= v:
                            continue
                        waited[key] = v
                        e.wait_ge(dsems[key[1]] if key[0] == "d" else sems[key[1]], v)
                    ins = o["fn"](e)
                    if o["dma"]:
                        ins.then_inc(dsems[dma_sem[i][0]], 16)
                    elif signal[i]:
                        ins.then_inc(sems[engname], 1)
                for slot, v in slot_val.items():
                    if slot[0] == engname and waited.get(("d", slot), 0) < v:
                        e.wait_ge(dsems[slot], v)

            block.tensor(lambda e: run("pe", e))
            block.scalar(lambda e: run("act", e))
            block.vector(lambda e: run("dve", e))
            block.gpsimd(lambda e: run("pool", e))
            block.sync(lambda e: run("sp", e))


def _img(W):
    nk = W.shape[0] // 128
    C = W.shape[1]
    return np.ascontiguousarray(W.reshape(nk, 128, C).transpose(1, 0, 2).reshape(128, nk * C))


def _swapcols(base):
    cols = []
    for hh in range(2):
        b = base + hh * 64
        cols += list(range(b + 8, b + 16)) + list(range(b, b + 8)) + list(range(b + 16, b + 64))
    return cols


def _rng(a, n=128):
    return list(range(a, a + n))


QA, KA, VA = 0, 512, 1024
QB, KB, VB = 1536, 2304, 3072
Q1, Q2, K1, K2 = 3840, 4096, 4352, 4608
VC = 4864
GATE = 5376


def _load_plan():
    plan = []

    def ffn(w):
        for g in range(6):
            plan.append(("f%d_in%d" % (w, g), "ffn_in", (w, g)))
        for q in range(4):
            plan.append(("f%d_out%d" % (w, q), "ffn_out", (w, q)))

    ffn(1)
    plan.append(("a_v", "cols", _rng(VA, 512)))
    cols = []
    for hp in range(4):
        cols += _rng(QA + hp * 128) + _rng(KA + hp * 128)
    plan.append(("a_qk", "cols", cols))
    plan.append(("g_a", "cols", _rng(GATE, 1024)))
    plan.append(("up_a", "up", "a"))
    plan.append(("wo_a", "wo", None))
    plan.append(("b_v", "cols", _rng(VB, 768)))
    for j in range(3):
        cols = []
        for c in (2 * j, 2 * j + 1):
            cols += _rng(QB + c * 128) + _swapcols(QB + c * 128) + _rng(KB + c * 128) + _swapcols(KB + c * 128)
        plan.append(("b_qk%d" % j, "cols", cols))
    plan.append(("g_b", "cols", _rng(GATE + 1024, 1024)))
    plan.append(("up_b", "up", "b"))
    plan.append(("c_v", "cols", _rng(VC, 512)))
    for hp in range(2):
        cols = []
        for base in (Q1, Q2, K1, K2):
            cols += _rng(base + hp * 128) + _swapcols(base + hp * 128)
        plan.append(("c_qk%d" % hp, "cols", cols))
    plan.append(("g_c", "cols", _rng(GATE + 2048, 1024)))
    plan.append(("up_c", "up", "c"))
    ffn(2)
    return plan


def _ffn_in_cols(g):
    ncg = min(4, NFC - 4 * g)
    return ncg, _rng(512 * g, 128 * ncg) + _rng(DFF + 512 * g, 128 * ncg)


def _plan_sizes():
    sizes = {}
    off = 0
    for name, kind, spec in _load_plan():
        if kind == "ffn_in":
            ncg, cols = _ffn_in_cols(spec[1])
            E = 8 * len(cols)
        elif kind == "ffn_out":
            E = NFC * 256
        elif kind == "cols":
            E = 8 * len(spec)
        elif kind == "up":
            E = {"a": 4, "b": 2, "c": 4}[spec] * 1024
        else:
            E = 8192
        sizes[name] = (off, E)
        off += E
    return sizes, off


def _build_wimg(inp, l):
    sizes, tot = _plan_sizes()
    out = np.empty((128, tot), np.float32)
    for name, kind, spec in _load_plan():
        off, E = sizes[name]
        if kind == "ffn_in":
            W = {1: inp["ffn1_w_in"], 2: inp["ffn2_w_in"]}[spec[0]][l]
            _, cols = _ffn_in_cols(spec[1])
            im = _img(W[:, cols])
        elif kind == "ffn_out":
            W = {1: inp["ffn1_w_out"], 2: inp["ffn2_w_out"]}[spec[0]][l]
            im = _img(W[:, spec[1] * 256:(spec[1] + 1) * 256])
        elif kind == "cols":
            im = _img(inp["w_in"][l][:, spec])
        elif kind == "up":
            im = _img({"a": inp["w_up_a"], "b": inp["w_up_b"], "c": inp["w_up_c"]}[spec][l])
        else:
            im = _img(inp["w_out"][l])
        assert im.shape == (128, E), (name, im.shape, E)
        out[:, off:off + E] = im
    return out


def _build_vecs(inp, l):
    v = np.zeros((128, NVEC), np.float32)
    v[:, 0:8] = inp["ffn1_norm"][l].reshape(8, 128).T
    v[:, 8:16] = inp["mix_norm"][l].reshape(8, 128).T
    v[:, 16:24] = inp["ffn2_norm"][l].reshape(8, 128).T
    v[:, 24:32] = inp["final_norm"].reshape(8, 128).T
    v[:, 32:56] = inp["b_gate"][l].reshape(24, 128).T
    v[:, 56] = inp["diff_subln"][l]
    for i, lv in enumerate((inp["lam_q1"], inp["lam_k1"], inp["lam_q2"], inp["lam_k2"])):
        v[:, 57 + 64 * i:57 + 64 * (i + 1)] = np.broadcast_to(lv[l][None, :], (128, 64))
    return v


def _build_consts():
    c = np.zeros((128, NCONST + 2), np.float32)
    r = np.arange(128)[:, None]
    col = np.arange(128)[None, :]
    c[:, 0:128] = 1.0
    c[:, 128:256] = np.where(r >= col, -1.0, 0.0)
    c[:, 256:384] = -1.0
    c[:, 384:512] = np.where(col > r, 1.0, 0.0)
    c[:, 512:640] = np.where(col >= r, 1.0, 0.0)
    col2 = np.arange(256)[None, :]
    c[:, 640:896] = np.where((col2 >= r) & (col2 <= r + 128), 1.0, 0.0)
    c[:, 896:1152] = c[:, 640:896]
    inv_freq = (500000.0 ** (-np.arange(0, 16, 2, dtype=np.float32) / np.float32(16))).astype(np.float32)
    for p in range(128):
        q = p % 64
        if q < 16:
            c[p, NCONST] = inv_freq[q % 8]
            c[p, NCONST + 1] = -1.0 if q < 8 else 1.0
    return c


def build_program(NSEQ=2, DEPTH=2, stop_after=None):
    T = NSEQ * S
    nc = bass.Bass("TRN2", target_bir_lowering=False, dynamic_dma_scratch_size=8192)
    sizes, WTOT = _plan_sizes()
    xin = nc.dram_tensor("xT", [D, T], F32, kind="ExternalInput")
    posd = nc.dram_tensor("pos", [NSEQ, 128, S], I32, kind="ExternalInput")
    wimg = [nc.dram_tensor("wimg%d" % l, [128, WTOT], F32, kind="ExternalInput") for l in range(DEPTH)]
    vecd = [nc.dram_tensor("vecs%d" % l, [128, NVEC], F32, kind="ExternalInput") for l in range(DEPTH)]
    cstd = nc.dram_tensor("consts", [128, NCONST + 2], F32, kind="ExternalInput")
    outd = nc.dram_tensor("outT", [D, T], F32, kind="ExternalOutput")
    xa = nc.dram_tensor("xa", [D, T], F32, kind="Internal")
    hmd = nc.dram_tensor("hmd", [D, T], BF16, kind="Internal")

    sch = Sched(nc)
    op = sch.op
    top = contextlib.ExitStack()
    with top:
        uniq = {"n": 0}

        def sb(es, name, shape, dt):
            uniq["n"] += 1
            return es.enter_context(nc.sbuf_tensor("%s_%d" % (name, uniq["n"]), shape, dt))

        WB = [sb(top, "wb%d" % i, [128, 8192], BF16) for i in range(4)]
        pst = top.enter_context(nc.psum_tensor("pst", [128, 4096], F32))
        PS = [pst[:, i * 512:(i + 1) * 512] for i in range(8)]
        cb = sb(top, "cb", [128, NCONST], BF16)
        cf = sb(top, "cf", [128, 2], F32)
        vec = [sb(top, "vec%d" % l, [128, NVEC], F32) for l in range(DEPTH)]
        ones = cb[:, 0:128]
        trineg = cb[:, 128:256]
        negones = cb[:, 256:384]
        mstrict = cb[:, 384:512]
        mincl = cb[:, 512:640]
        mband = cb[:, 640:896]
        mband2 = cb[:, 640:1152].rearrange("p (h c) -> p h c", h=2)

        op("pool", lambda e: e.dma_start(out=cb[:], in_=cstd.ap()[:, 0:NCONST]), writes=[("cb", None)], dma=True)
        op("sp", lambda e: e.dma_start(out=cf[:], in_=cstd.ap()[:, NCONST:NCONST + 2]), writes=[("cf", None)], dma=True)
        for l in range(DEPTH):
            op("sp", lambda e, l=l: e.dma_start(out=vec[l][:], in_=vecd[l].ap()), writes=[("vec", l)], dma=True)

        wstate = {"n": 0}

        def wload(l, name):
            off, E = sizes[name]
            b = wstate["n"] % 4
            wstate["n"] += 1
            op("pool", lambda e: e.dma_start(out=WB[b][:, 0:E], in_=wimg[l].ap()[:, off:off + E]),
               writes=[("wb", b)], dma=True)
            return WB[b], ("wb", b)

        psrr = {"n": 0}

        def psnext():
            b = psrr["n"] % 8
            psrr["n"] += 1
            return PS[b], ("ps", b)

        def rms_rstd(es_name, src_chunks, src_keys, rstd_ap, rstd_key, sqt, width, inv_n, eps, lnt):
            ps, pk = psnext()
            n = len(src_chunks)
            for i, (a, k) in enumerate(zip(src_chunks, src_keys)):
                sq = sqt[i % 2]
                sk = ("sq" + es_name, i % 2)
                if i % 2 == 0:
                    op("act", lambda e, a=a, sq=sq: e.activation(out=sq[:, 0:width], in_=a, func=AF.Square),
                       reads=[k], writes=[sk])
                else:
                    op("dve", lambda e, a=a, sq=sq: e.tensor_tensor(out=sq[:, 0:width], in0=a, in1=a, op=ALU.mult),
                       reads=[k], writes=[sk])
                op("pe", lambda e, sq=sq, i=i: e.matmul(ps[:, 0:width], lhsT=ones, rhs=sq[:, 0:width],
                                                        start=(i == 0), stop=(i == n - 1)),
                   reads=[sk, ("cb", None)], writes=[pk])
            lk = ("ln" + es_name, None)
            op("act", lambda e: e.activation(out=lnt[:, 0:width], in_=ps[:, 0:width], func=AF.Ln, scale=inv_n, bias=eps),
               reads=[pk], writes=[lk])
            op("act", lambda e: e.activation(out=rstd_ap, in_=lnt[:, 0:width], func=AF.Exp, scale=-0.5),
               reads=[lk], writes=[rstd_key])

        def ffn_phase(l, w, src, final=False, emit_hm=False):
            gcol = 0 if w == 1 else 16
            TT = 1024
            NT = T // TT
            with contextlib.ExitStack() as es:
                xt = sb(es, "xt", [128, 8, TT], F32)
                xst = sb(es, "xst", [128, 8, 512], F32)
                hts = [sb(es, "ht%d" % i, [128, 8, TT], BF16) for i in range(2)]
                act = sb(es, "act", [128, NFC, TT], BF16)
                sqt = [sb(es, "sq%d" % i, [128, 512], BF16) for i in range(8)]
                lnt = sb(es, "lnt", [128, 512], F32)
                rstd = sb(es, "rstd", [128, 512], F32)
                sil = [sb(es, "sil%d" % i, [128, 512], F32) for i in range(2)]
                if emit_hm:
                    hst = sb(es, "hst", [128, 4, 512], BF16)

                def pre_load(tt, h):
                    c0 = tt * TT + h * 512
                    for kc in range(8):
                        op("sp", lambda e, kc=kc: e.dma_start(out=xst[:, kc, :], in_=src.ap()[kc * 128:(kc + 1) * 128, c0:c0 + 512]),
                           reads=[("xdram", (kc, tt))], writes=[("xst", kc)], dma=True)
                    for kc in range(8):
                        if kc % 2 == 0:
                            op("act", lambda e, kc=kc: e.activation(out=sqt[kc][:], in_=xst[:, kc, :], func=AF.Square),
                               reads=[("xst", kc)], writes=[("sq", kc)])
                        else:
                            op("dve", lambda e, kc=kc: e.tensor_tensor(out=sqt[kc][:], in0=xst[:, kc, :], in1=xst[:, kc, :], op=ALU.mult),
                               reads=[("xst", kc)], writes=[("sq", kc)])

                def pre_norm(tt, h):
                    ps, pk = psnext()
                    for kc in range(8):
                        op("pe", lambda e, kc=kc: e.matmul(ps[:], lhsT=ones, rhs=sqt[kc][:], start=(kc == 0), stop=(kc == 7)),
                           reads=[("sq", kc), ("cb", None)], writes=[pk])
                    op("act", lambda e: e.activation(out=lnt[:], in_=ps[:], func=AF.Ln, scale=1.0 / D, bias=1e-6),
                       reads=[pk], writes=[("lnt", None)])
                    op("act", lambda e: e.activation(out=rstd[:], in_=lnt[:], func=AF.Exp, scale=-0.5),
                       reads=[("lnt", None)], writes=[("rstd", None)])
                    hb_ = hts[tt % 2]
                    for kc in range(8):
                        op("dve", lambda e, kc=kc: e.scalar_tensor_tensor(
                            out=hb_[:, kc, h * 512:(h + 1) * 512], in0=xst[:, kc, :],
                            scalar=vec[l][:, gcol + kc:gcol + kc + 1], in1=rstd[:], op0=ALU.mult, op1=ALU.mult),
                           reads=[("xst", kc), ("rstd", None), ("vec", l)], writes=[("ht", (tt % 2, kc, h))])

                for h in range(2):
                    pre_load(0, h)
                    pre_norm(0, h)
                pending = []
                for tt in range(NT):
                    c0 = tt * TT
                    ht = hts[tt % 2]
                    si = 0
                    for g in range(6):
                        if g == 1 and pending:
                            pending.pop(0)()
                        if g == 2:
                            while pending:
                                pending.pop(0)()
                            for kc in range(8):
                                op("sp", lambda e, kc=kc, c0=c0: e.dma_start(out=xt[:, kc, :], in_=src.ap()[kc * 128:(kc + 1) * 128, c0:c0 + TT]),
                                   reads=[("xdram", (kc, tt))], writes=[("xt", kc)], dma=True)
                        wb, wk = wload(l, "f%d_in%d" % (w, g))
                        ncg = min(4, NFC - 4 * g)
                        C = 2 * 128 * ncg
                        jh = [(j, h) for j in range(ncg) for h in range(2)]
                        if tt == 0 and g == 0:
                            jh = [(j, h) for h in range(2) for j in range(ncg)]
                        for (j, h) in jh:
                            fc = 4 * g + j
                            if True:
                                pg, pgk = psnext()
                                pu, puk = psnext()
                                for kc in range(8):
                                    op("pe", lambda e, kc=kc: e.matmul(
                                        pg[:], lhsT=wb[:, kc * C + j * 128:kc * C + (j + 1) * 128],
                                        rhs=ht[:, kc, h * 512:(h + 1) * 512], start=(kc == 0), stop=(kc == 7)),
                                       reads=[wk, ("ht", (tt % 2, kc, h))], writes=[pgk])
                                for kc in range(8):
                                    op("pe", lambda e, kc=kc: e.matmul(
                                        pu[:], lhsT=wb[:, kc * C + (ncg + j) * 128:kc * C + (ncg + j + 1) * 128],
                                        rhs=ht[:, kc, h * 512:(h + 1) * 512], start=(kc == 0), stop=(kc == 7)),
                                       reads=[wk, ("ht", (tt % 2, kc, h))], writes=[puk])
                                st = sil[si % 2]
                                stk = ("sil", si % 2)
                                si += 1
                                op("act", lambda e: e.activation(out=st[:], in_=pg[:], func=AF.Silu), reads=[pgk], writes=[stk])
                                op("dve", lambda e: e.tensor_tensor(out=act[:, fc, h * 512:(h + 1) * 512], in0=pu[:], in1=st[:], op=ALU.mult),
                                   reads=[puk, stk], writes=[("act", (fc, h))])
                    if tt + 1 < NT:
                        pre_load(tt + 1, 0)
                    for q in range(4):
                        wb, wk = wload(l, "f%d_out%d" % (w, q))
                        for jj in range(2):
                            dc = 2 * q + jj
                            for h in range(2):
                                py, pyk = psnext()
                                for fc in range(NFC):
                                    op("pe", lambda e, fc=fc: e.matmul(
                                        py[:], lhsT=wb[:, fc * 256 + jj * 128:fc * 256 + (jj + 1) * 128],
                                        rhs=act[:, fc, h * 512:(h + 1) * 512], start=(fc == 0), stop=(fc == NFC - 1)),
                                       reads=[wk, ("act", (fc, h))], writes=[pyk])
                                op("dve", lambda e: e.scalar_tensor_tensor(
                                    out=xt[:, dc, h * 512:(h + 1) * 512], in0=py[:], scalar=0.5,
                                    in1=xt[:, dc, h * 512:(h + 1) * 512], op0=ALU.mult, op1=ALU.add),
                                   reads=[pyk, ("xt", dc)], writes=[("xt", dc)])
                        if tt + 1 < NT:
                            if q == 0:
                                pre_norm(tt + 1, 0)
                                pre_load(tt + 1, 1)
                            elif q == 1:
                                pre_norm(tt + 1, 1)
                    if not final:
                        for kc in range(8):
                            op("sp", lambda e, kc=kc, c0=c0: e.dma_start(out=xa.ap()[kc * 128:(kc + 1) * 128, c0:c0 + TT], in_=xt[:, kc, :]),
                               reads=[("xt", kc)], writes=[("xdram", (kc, tt))], dma=True)
                        if emit_hm:
                            def epilogue(h, tt=tt, c0=c0):
                                if True:
                                    ps, pk = psnext()
                                    for kc in range(8):
                                        if kc % 2 == 0:
                                            op("act", lambda e, kc=kc: e.activation(out=sqt[kc][:], in_=xt[:, kc, h * 512:(h + 1) * 512], func=AF.Square),
                                               reads=[("xt", kc)], writes=[("sq", kc)])
                                        else:
                                            op("dve", lambda e, kc=kc: e.tensor_tensor(out=sqt[kc][:], in0=xt[:, kc, h * 512:(h + 1) * 512],
                                                                                       in1=xt[:, kc, h * 512:(h + 1) * 512], op=ALU.mult),
                                               reads=[("xt", kc)], writes=[("sq", kc)])
                                        op("pe", lambda e, kc=kc: e.matmul(ps[:], lhsT=ones, rhs=sqt[kc][:], start=(kc == 0), stop=(kc == 7)),
                                           reads=[("sq", kc), ("cb", None)], writes=[pk])
                                    op("act", lambda e: e.activation(out=sil[0][:], in_=ps[:], func=AF.Ln, scale=1.0 / D, bias=1e-6),
                                       reads=[pk], writes=[("sil", 0)])
                                    op("act", lambda e: e.activation(out=sil[1][:], in_=sil[0][:], func=AF.Exp, scale=-0.5),
                                       reads=[("sil", 0)], writes=[("sil", 1)])
                                    for kc in range(8):
                                        op("dve", lambda e, kc=kc: e.scalar_tensor_tensor(
                                            out=hst[:, kc % 4, :], in0=xt[:, kc, h * 512:(h + 1) * 512],
                                            scalar=vec[l][:, 8 + kc:9 + kc], in1=sil[1][:], op0=ALU.mult, op1=ALU.mult),
                                           reads=[("xt", kc), ("sil", 1), ("vec", l)], writes=[("hst", kc % 4)])
                                        op("sp", lambda e, kc=kc: e.dma_start(
                                            out=hmd.ap()[kc * 128:(kc + 1) * 128, c0 + h * 512:c0 + (h + 1) * 512], in_=hst[:, kc % 4, :]),
                                           reads=[("hst", kc % 4)], writes=[("hdram", (kc, tt, h))], dma=True)
                            pending.append(lambda ep=epilogue: ep(0))
                            pending.append(lambda ep=epilogue: ep(1))
                    else:
                        for h in range(2):
                            ps, pk = psnext()
                            for kc in range(8):
                                op("dve", lambda e, kc=kc: e.tensor_tensor(out=sqt[kc][:], in0=xt[:, kc, h * 512:(h + 1) * 512],
                                                                           in1=xt[:, kc, h * 512:(h + 1) * 512], op=ALU.mult),
                                   reads=[("xt", kc)], writes=[("sq", kc)])
                                op("pe", lambda e, kc=kc: e.matmul(ps[:], lhsT=ones, rhs=sqt[kc][:], start=(kc == 0), stop=(kc == 7)),
                                   reads=[("sq", kc), ("cb", None)], writes=[pk])
                            op("act", lambda e: e.activation(out=sil[0][:], in_=ps[:], func=AF.Ln, scale=1.0 / D, bias=1e-6),
                               reads=[pk], writes=[("sil", 0)])
                            op("act", lambda e: e.activation(out=sil[1][:], in_=sil[0][:], func=AF.Exp, scale=-0.5),
                               reads=[("sil", 0)], writes=[("sil", 1)])
                            for kc in range(8):
                                op("dve", lambda e, kc=kc: e.scalar_tensor_tensor(
                                    out=xt[:, kc, h * 512:(h + 1) * 512], in0=xt[:, kc, h * 512:(h + 1) * 512],
                                    scalar=vec[l][:, 24 + kc:25 + kc], in1=sil[1][:], op0=ALU.mult, op1=ALU.mult),
                                   reads=[("xt", kc), ("sil", 1), ("vec", l)], writes=[("xt", kc)])
                        for kc in range(8):
                            op("sp", lambda e, kc=kc, c0=c0: e.dma_start(out=outd.ap()[kc * 128:(kc + 1) * 128, c0:c0 + TT], in_=xt[:, kc, :]),
                               reads=[("xt", kc)], writes=[("odram", (kc, tt))], dma=True)
                while pending:
                    pending.pop(0)()
                sch.barrier()

        def evac(i, out_ap, in_ap, reads, writes, scale=None):
            if i % 2 == 0:
                if scale is None:
                    op("act", lambda e: e.copy(out=out_ap, in_=in_ap), reads=reads, writes=writes)
                else:
                    op("act", lambda e: e.mul(out=out_ap, in_=in_ap, mul=scale), reads=reads, writes=writes)
            else:
                if scale is None:
                    op("dve", lambda e: e.tensor_copy(out=out_ap, in_=in_ap), reads=reads, writes=writes)
                else:
                    op("dve", lambda e: e.tensor_scalar(out=out_ap, in0=in_ap, scalar1=scale, scalar2=None, op0=ALU.mult),
                       reads=reads, writes=writes)

        def mixer_phase(l, s, parts="abc"):
            c0s = s * S
            lam_init = 0.8 - 0.6 * math.exp(-0.3 * l)
            with contextlib.ExitStack() as es:
                ht = sb(es, "mht", [128, 8, S], BF16)
                cosF = sb(es, "cosF", [128, S], F32)
                sinF = sb(es, "sinF", [128, S], F32)
                lamv = sb(es, "lamv", [128, 8], F32)
                if True:
                    rstd = sb(es, "mrstd", [128, 128], F32)
                    for kc in range(8):
                        op("sp", lambda e, kc=kc: e.dma_start(out=ht[:, kc, :], in_=hmd.ap()[kc * 128:(kc + 1) * 128, c0s:c0s + S]),
                           reads=[("hdram", None)], writes=[("mht", (kc, tt)) for tt in range(4)], dma=True)
                    lt = rstd
                    for i in range(2):
                        op("dve", lambda e, i=i: e.tensor_tensor(out=lt[:, i * 64:(i + 1) * 64], in0=vec[l][:, 57 + 128 * i:57 + 128 * i + 64],
                                                                 in1=vec[l][:, 57 + 128 * i + 64:57 + 128 * i + 128], op=ALU.mult),
                           reads=[("vec", l), ("mrstd", None)], writes=[("mrstd", None)])
                        op("dve", lambda e, i=i: e.reduce_sum(out=lamv[:, 2 + i:3 + i], in_=lt[:, i * 64:(i + 1) * 64], axis=AX.X),
                           reads=[("mrstd", None)], writes=[("lamv", None)])
                    op("act", lambda e: e.activation(out=lamv[:, 4:6], in_=lamv[:, 2:4], func=AF.Exp),
                       reads=[("lamv", None)], writes=[("lamv", None)])
                    op("dve", lambda e: e.tensor_tensor(out=lamv[:, 0:1], in0=lamv[:, 5:6], in1=lamv[:, 4:5], op=ALU.subtract),
                       reads=[("lamv", None)], writes=[("lamv", None)])
                    op("dve", lambda e: e.tensor_scalar(out=lamv[:, 0:1], in0=lamv[:, 0:1], scalar1=-lam_init, scalar2=None, op0=ALU.add),
                       reads=[("lamv", None)], writes=[("lamv", None)])
                    op("dve", lambda e: e.tensor_scalar(out=lamv[:, 1:2], in0=vec[l][:, 56:57], scalar1=1.0 - lam_init, scalar2=None, op0=ALU.mult),
                       reads=[("lamv", None), ("vec", l)], writes=[("lamv", None)])

                HS = S // 2
                tabs = {}

                def tables_alloc(scope):
                    tabs["tP"] = sb(scope, "tP", [128, HS], I32)
                    tabs["tT"] = [sb(scope, "tT%d" % i, [128, HS], F32) for i in range(2)]
                    tabs["tK"] = sb(scope, "tK", [128, HS], I32)
                C1 = 6.28125
                C2 = 2.0 * math.pi - 6.28125
                tab_jobs = []

                def tables_pool(hf):
                    cs = slice(hf * HS, (hf + 1) * HS)
                    tP, tT, tK = tabs["tP"], tabs["tT"], tabs["tK"]
                    tPf = tP[:].bitcast(F32)
                    tKf = tK[:].bitcast(F32)
                    tq.append(sch.defer("sp", lambda e: e.dma_start(out=tP[:], in_=posd.ap()[s][:, cs]), writes=[("tP", None)], dma=True))
                    tq.append(sch.defer("dve", lambda e: e.tensor_copy(out=tPf, in_=tP[:]), reads=[("tP", None)], writes=[("tP", None)]))
                    tq.append(sch.defer("dve", lambda e: e.tensor_scalar(out=tPf, in0=tPf, scalar1=cf[:, 0:1], scalar2=None, op0=ALU.mult),
                       reads=[("tP", None), ("cf", None)], writes=[("tP", None)]))
                    for wi, (shift, dst, dk, signed) in enumerate(((0.0, sinF, "tabs", True), (math.pi / 2, cosF, "tabc", False))):
                        t1 = tT[wi]
                        tk = ("tT", wi)
                        tq.append(sch.defer("dve", lambda e: e.tensor_scalar(out=t1[:], in0=tPf, scalar1=shift, scalar2=None, op0=ALU.add),
                           reads=[("tP", None)], writes=[tk]))
                        tq.append(sch.defer("dve", lambda e: e.tensor_scalar(out=tKf, in0=t1[:], scalar1=1.0 / (2 * math.pi), scalar2=None, op0=ALU.mult),
                           reads=[tk], writes=[("tK", None)]))
                        tq.append(sch.defer("dve", lambda e: e.tensor_copy(out=tK[:], in_=tKf), reads=[("tK", None)], writes=[("tK", None)]))
                        tq.append(sch.defer("dve", lambda e: e.tensor_copy(out=tKf, in_=tK[:]), reads=[("tK", None)], writes=[("tK", None)]))
                        tq.append(sch.defer("dve", lambda e: e.tensor_scalar(out=tKf, in0=tKf, scalar1=-C1, scalar2=None, op0=ALU.mult),
                           reads=[("tK", None)], writes=[("tK", None)]))
                        tq.append(sch.defer("dve", lambda e: e.tensor_tensor(out=t1[:], in0=t1[:], in1=tKf, op=ALU.add),
                           reads=[("tK", None), tk], writes=[tk]))
                        tq.append(sch.defer("dve", lambda e: e.tensor_scalar(out=tKf, in0=tKf, scalar1=C2 / C1, scalar2=None, op0=ALU.mult),
                           reads=[("tK", None)], writes=[("tK", None)]))
                        tq.append(sch.defer("dve", lambda e: e.tensor_tensor(out=t1[:], in0=t1[:], in1=tKf, op=ALU.add),
                           reads=[("tK", None), tk], writes=[tk]))
                        tq.append(sch.defer("dve", lambda e: e.tensor_scalar(out=tKf, in0=t1[:], scalar1=math.pi, scalar2=-2 * math.pi, op0=ALU.is_gt, op1=ALU.mult),
                           reads=[tk], writes=[("tK", None)]))
                        tq.append(sch.defer("dve", lambda e: e.tensor_tensor(out=t1[:], in0=t1[:], in1=tKf, op=ALU.add),
                           reads=[("tK", None), tk], writes=[tk]))
                        tq.append(sch.defer("dve", lambda e: e.tensor_scalar(out=tKf, in0=t1[:], scalar1=-math.pi, scalar2=2 * math.pi, op0=ALU.is_lt, op1=ALU.mult),
                           reads=[tk], writes=[("tK", None)]))
                        tq.append(sch.defer("dve", lambda e: e.tensor_tensor(out=t1[:], in0=t1[:], in1=tKf, op=ALU.add),
                           reads=[("tK", None), tk], writes=[tk]))
                        tq.append(sch.defer("dve", lambda e: e.tensor_scalar(out=t1[:], in0=t1[:], scalar1=-3.14159, scalar2=3.14159, op0=ALU.max, op1=ALU.min),
                           reads=[tk], writes=[tk]))
                        if signed:
                            tq.append(sch.defer("dve", lambda e: e.tensor_scalar(out=t1[:], in0=t1[:], scalar1=cf[:, 1:2], scalar2=None, op0=ALU.mult),
                               reads=[tk, ("cf", None)], writes=[tk]))

                def tables_act(hf):
                    cs = slice(hf * HS, (hf + 1) * HS)
                    tT = tabs["tT"]
                    tq.append(sch.defer("act", lambda e: e.activation(out=sinF[:, cs], in_=tT[0][:], func=AF.Sin), reads=[("tT", 0)], writes=[("tabs", None)]))
                    tq.append(sch.defer("act", lambda e: e.activation(out=cosF[:, cs], in_=tT[1][:], func=AF.Sin), reads=[("tT", 1)], writes=[("tabc", None)]))

                tstate = {"n": 0}
                tq = []

                def tables_drain(n=None):
                    k = 0
                    while tq and (n is None or k < n):
                        tq.pop(0)()
                        k += 1

                def tables_step():
                    k = tstate["n"]
                    tstate["n"] += 1
                    if k == 0:
                        tables_pool(0)
                    elif k == 1:
                        tables_act(0)
                        tables_pool(1)
                    elif k == 2:
                        tables_act(1)

                def tables_finish():
                    while tstate["n"] < 3:
                        tables_step()
                    tables_drain()

                def proj_fm(wb, wk, C, cidx, dst_fn, key, evi=[0]):
                    for tt in range(4):
                        ps, pk = psnext()
                        for kc in range(8):
                            op("pe", lambda e, kc=kc, tt=tt, ps=ps: e.matmul(
                                ps[:], lhsT=wb[:, kc * C + cidx * 128:kc * C + (cidx + 1) * 128],
                                rhs=ht[:, kc, tt * 512:(tt + 1) * 512], start=(kc == 0), stop=(kc == 7)),
                               reads=[wk, ("mht", (kc, tt))], writes=[pk])
                        dst_fn(tt, ps, pk)

                def proj_v(wb, wk, C, col0, ncols, vt, vkey, stride=1, tokfn=None):
                    for blk in range(16):
                        ps, pk = psnext()
                        tk = tokfn(blk)
                        for kc in range(8):
                            op("pe", lambda e, kc=kc, ps=ps, tk=tk: e.matmul(
                                ps[:, 0:ncols], lhsT=ht[:, kc, tk], rhs=wb[:, kc * C + col0:kc * C + col0 + ncols],
                                start=(kc == 0), stop=(kc == 7)),
                               reads=[wk, (("mht", (kc, blk // 4)) if stride == 1 else ("mht", None))], writes=[pk])
                        evac(blk, vt[:, blk, 0:ncols], ps[:, 0:ncols], [pk], [(vkey, blk)])

                rope_n = {"n": 0}

                def rope_evac(psP, pkP, psS, pkS, dst_ap, dkey, tt, rt, perm=None):
                    ri = rope_n["n"] % 2
                    rope_n["n"] += 1
                    a, b = rt[ri]
                    cs = slice(tt * 512, (tt + 1) * 512)
                    op("dve", lambda e: e.tensor_tensor(out=a[:], in0=psP[:], in1=cosF[:, cs], op=ALU.mult),
                       reads=[pkP, ("tabc", None)], writes=[("ra", ri)])
                    op("dve", lambda e: e.tensor_tensor(out=b[:], in0=psS[:], in1=sinF[:, cs], op=ALU.mult),
                       reads=[pkS, ("tabs", None)], writes=[("rb", ri)])
                    if perm is None:
                        op("dve", lambda e: e.tensor_tensor(out=dst_ap, in0=a[:], in1=b[:], op=ALU.add),
                           reads=[("ra", ri), ("rb", ri)], writes=[dkey])
                    else:
                        d = perm
                        op("dve", lambda e: e.tensor_tensor(out=dst_ap, in0=a[:].rearrange("p (m c) -> p c m", c=d),
                                                             in1=b[:].rearrange("p (m c) -> p c m", c=d), op=ALU.add),
                           reads=[("ra", ri), ("rb", ri)], writes=[dkey])

                def merge_all(branches):
                    with contextlib.ExitStack() as es3:
                        macc = sb(es3, "macc", [128, 8, S], BF16)
                        xt1 = sb(es3, "gx", [128, 8, 512], F32)
                        sg = [sb(es3, "gs%d" % i, [128, 512], F32) for i in range(2)]
                        pr = [sb(es3, "gp%d" % i, [128, 512], BF16) for i in range(2)]
                        n = 0
                        for bidx, (b, ychunks, ykey) in enumerate(branches):
                            nk = len(ychunks)
                            bi = "abc".index(b)
                            wg, wgk = wload(l, "g_" + b)
                            wu, wuk = wload(l, "up_" + b)
                            for tt in range(4):
                                cs = slice(tt * 512, (tt + 1) * 512)
                                for dc in range(8):
                                    pg, pgk = psnext()
                                    for kc in range(8):
                                        op("pe", lambda e, kc=kc: e.matmul(
                                            pg[:], lhsT=wg[:, kc * 1024 + dc * 128:kc * 1024 + (dc + 1) * 128],
                                            rhs=ht[:, kc, cs], start=(kc == 0), stop=(kc == 7)),
                                           reads=[wgk, ("mht", None)], writes=[pgk])
                                    sgt = sg[n % 2]
                                    prt = pr[n % 2]
                                    sgk = ("gs", n % 2)
                                    prk = ("gp", n % 2)
                                    n += 1
                                    op("act", lambda e: e.activation(out=sgt[:], in_=pg[:], func=AF.Sigmoid,
                                                                     bias=vec[l][:, 32 + bi * 8 + dc:33 + bi * 8 + dc]),
                                       reads=[pgk, ("vec", l)], writes=[sgk])
                                    pu, puk = psnext()
                                    for j in range(nk):
                                        op("pe", lambda e, j=j: e.matmul(
                                            pu[:], lhsT=wu[:, j * 1024 + dc * 128:j * 1024 + (dc + 1) * 128],
                                            rhs=ychunks[j][:, cs], start=(j == 0), stop=(j == nk - 1)),
                                           reads=[wuk, ykey], writes=[puk])
                                    if bidx == 0:
                                        op("dve", lambda e: e.tensor_tensor(out=macc[:, dc, cs], in0=pu[:], in1=sgt[:], op=ALU.mult),
                                           reads=[puk, sgk], writes=[("macc", (dc, tt))])
                                    else:
                                        op("dve", lambda e: e.tensor_tensor(out=prt[:], in0=pu[:], in1=sgt[:], op=ALU.mult),
                                           reads=[puk, sgk], writes=[prk])
                                        op("dve", lambda e: e.tensor_tensor(out=macc[:, dc, cs], in0=macc[:, dc, cs], in1=prt[:], op=ALU.add),
                                           reads=[prk, ("macc", (dc, tt))], writes=[("macc", (dc, tt))])
                        wo, wok = wload(l, "wo_a")
                        for tt in range(4):
                            cs = slice(tt * 512, (tt + 1) * 512)
                            for kc in range(8):
                                op("sp", lambda e, kc=kc: e.dma_start(
                                    out=xt1[:, kc, :], in_=xa.ap()[kc * 128:(kc + 1) * 128, c0s + tt * 512:c0s + (tt + 1) * 512]),
                                   reads=[("xdram", (kc, s, tt))], writes=[("gx", kc)], dma=True)
                            for dc2 in range(8):
                                pz, pzk = psnext()
                                for dc in range(8):
                                    op("pe", lambda e, dc=dc: e.matmul(
                                        pz[:], lhsT=wo[:, dc * 1024 + dc2 * 128:dc * 1024 + (dc2 + 1) * 128],
                                        rhs=macc[:, dc, cs], start=(dc == 0), stop=(dc == 7)),
                                       reads=[wok, ("macc", (dc, tt))], writes=[pzk])
                                op("dve", lambda e: e.tensor_tensor(out=xt1[:, dc2, :], in0=pz[:], in1=xt1[:, dc2, :], op=ALU.add),
                                   reads=[pzk, ("gx", dc2)], writes=[("gx", dc2)])
                                op("sp", lambda e: e.dma_start(
                                    out=xa.ap()[dc2 * 128:(dc2 + 1) * 128, c0s + tt * 512:c0s + (tt + 1) * 512], in_=xt1[:, dc2, :]),
                                   reads=[("gx", dc2)], writes=[("xdram", (dc2, s, tt))], dma=True)
                    sch.barrier()

                if "a" in parts:
                    if True:
                        ya = sb(es, "ya", [128, 4, S], BF16)
                        with contextlib.ExitStack() as es3:
                            tables_alloc(es3)
                            va = sb(es3, "va", [128, 16, 512], BF16)
                            qts = [sb(es3, "aq%d" % i, [128, S], BF16) for i in range(2)]
                            kts = [sb(es3, "ak%d" % i, [128, S], BF16) for i in range(2)]
                            et = [sb(es3, "ae%d" % i, [128, 2, 512], F32) for i in range(2)]
                            spt = [sb(es3, "asp%d" % i, [128, 2, 512], BF16) for i in range(3)]
                            wt = [sb(es3, "aw%d" % i, [128, 2, 512], BF16) for i in range(3)]
                            Rt = [sb(es3, "aR%d" % i, [128, 2, 512], BF16) for i in range(4)]
                            wb, wk = wload(l, "a_v")
                            proj_v(wb, wk, 512, 0, 512, va, "va", tokfn=lambda blk: slice(blk * 128, (blk + 1) * 128))
                            wb, wk = wload(l, "a_qk")
                            ZB = [0, 2, 4]
                            OB = [6, 7]
                            def proj_items(hp):
                                return [dict(kind="proj", hp=hp, tt=tt, which=which) for tt in range(4) for which in range(2)]

                            def step_items(hp, gi0):
                                out = []
                                gi = gi0
                                for qc in range(4):
                                    kbs = list(range(4 * qc + 3, -1, -1))
                                    n = len(kbs)

                                    def rng_of(kb, qc=qc):
                                        lo = 128 * (kb - 4 * qc) if kb >= 4 * qc else 0
                                        return lo, 512 - lo

                                    for i, kb in enumerate(kbs):
                                        lo, w = rng_of(kb)
                                        nxt = rng_of(kbs[i + 1]) if i + 1 < n else None
                                        out.append(dict(kind="step", hp=hp, qc=qc, i=i, n=n, kb=kb, lo=lo, w=w, nxt=nxt,
                                                        diag=(kb >= 4 * qc), gi=gi))
                                    gi += 1
                                return out, gi

                            items = proj_items(0)
                            gi = 0
                            for hp in range(4):
                                st, gi = step_items(hp, gi)
                                pj = proj_items(hp + 1) if hp + 1 < 4 else []
                                for k, it in enumerate(st):
                                    items.append(it)
                                    if pj and k % 4 == 3:
                                        items.append(pj.pop(0))
                                items.extend(pj)
                            prev_real = None
                            for g_, it in enumerate(items):
                                it["g"] = g_
                                if it["kind"] == "step":
                                    it["gprev"] = prev_real["g"] if (prev_real is not None and it["i"] > 0) else None
                                    if prev_real is not None and it["i"] > 0:
                                        prev_real["gnext"] = g_
                                    it["gnext"] = None
                                    prev_real = it
                            NI = len(items)

                            def zview(g_, w):
                                zb = ZB[g_ % 3]
                                return pst[:, zb * 512:(zb + 2) * 512].rearrange("p (h c) -> p h c", h=2)[:, :, 0:w]

                            def stA(g_):
                                it = items[g_]
                                zb = ZB[g_ % 3]
                                if it["kind"] == "proj":
                                    hp, tt, which = it["hp"], it["tt"], it["which"]
                                    dst = (qts if which == 0 else kts)[hp % 2]
                                    dk = "aq" if which == 0 else "ak"
                                    ps = PS[zb]
                                    for kc in range(8):
                                        op("pe", lambda e, kc=kc: e.matmul(
                                            ps[:], lhsT=wb[:, kc * 1024 + (2 * hp + which) * 128:kc * 1024 + (2 * hp + which + 1) * 128],
                                            rhs=ht[:, kc, tt * 512:(tt + 1) * 512], start=(kc == 0), stop=(kc == 7)),
                                           reads=[wk, ("mht", None)], writes=[("ps", zb)])
                                    evac(1, dst[:, tt * 512:(tt + 1) * 512], ps[:], [("ps", zb)], [(dk, (hp % 2, tt))],
                                         scale=(0.125 if which == 0 else None))
                                    return
                                hp, qc, i, n, kb, lo, w = it["hp"], it["qc"], it["i"], it["n"], it["kb"], it["lo"], it["w"]
                                qt, kt = qts[hp % 2], kts[hp % 2]
                                if g_ >= 12:
                                    tables_drain(1)
                                for hh in range(2):
                                    hb = slice(64 * hh, 64 * hh + 64)
                                    op("pe", lambda e: e.matmul(PS[zb + hh][:, 0:w], lhsT=kt[hb, kb * 128:(kb + 1) * 128],
                                                                rhs=qt[hb, qc * 512 + lo:qc * 512 + 512], start=True, stop=True),
                                       reads=[("ak", (hp % 2, kb // 4)), ("aq", (hp % 2, qc))], writes=[("ps", zb + hh)])
                                ee = et[g_ % 2]
                                sp = spt[g_ % 3]
                                op("act", lambda e: e.activation(out=ee[:, :, 0:w], in_=zview(g_, w), func=AF.Exp),
                                   reads=[("ps", zb), ("ps", zb + 1)], writes=[("ae", g_ % 2)])
                                op("act", lambda e: e.activation(out=sp[:, :, 0:w], in_=ee[:, :, 0:w], func=AF.Ln, bias=1.0),
                                   reads=[("ae", g_ % 2)], writes=[("asp", g_ % 3)])
                                if it["diag"]:
                                    for hh in range(2):
                                        op("dve", lambda e: e.tensor_tensor(out=sp[:, hh, 0:128], in0=sp[:, hh, 0:128], in1=mstrict, op=ALU.mult),
                                           reads=[("asp", g_ % 3), ("cb", None)], writes=[("asp", g_ % 3)])
                                if it["nxt"] is not None:
                                    lo2, w2 = it["nxt"]
                                    gn = it["gnext"]
                                    Rn = Rt[gn % 4]
                                    Rp = Rt[g_ % 4]
                                    if lo2 < lo:
                                        op("dve", lambda e: e.memset(Rn[:, :, 0:lo - lo2], 0.0), writes=[("aR", gn % 4)])
                                    if i == 0:
                                        op("dve", lambda e: e.tensor_copy(out=Rn[:, :, lo - lo2:w2], in_=sp[:, :, 0:w]),
                                           reads=[("asp", g_ % 3)], writes=[("aR", gn % 4)])
                                    else:
                                        op("dve", lambda e: e.tensor_tensor(out=Rn[:, :, lo - lo2:w2], in0=Rp[:, :, 0:w], in1=sp[:, :, 0:w], op=ALU.add),
                                           reads=[("asp", g_ % 3), ("aR", g_ % 4)], writes=[("aR", gn % 4)])

                            def stC(g_):
                                it = items[g_]
                                if it["kind"] == "proj":
                                    return
                                i, kb, lo, w = it["i"], it["kb"], it["lo"], it["w"]
                                zb = ZB[g_ % 3]
                                sp = spt[g_ % 3]
                                Rp = Rt[g_ % 4]
                                for hh in range(2):
                                    op("pe", lambda e: e.matmul(PS[zb + hh][:, 0:w], lhsT=trineg, rhs=sp[:, hh, 0:w], start=False, stop=(i == 0), skip_group_check=True),
                                       reads=[("asp", g_ % 3), ("cb", None)], writes=[("ps", zb + hh)])
                                    if i > 0:
                                        op("pe", lambda e: e.matmul(PS[zb + hh][:, 0:w], lhsT=negones, rhs=Rp[:, hh, 0:w], start=False, stop=True, skip_group_check=True),
                                           reads=[("aR", g_ % 4), ("cb", None)], writes=[("ps", zb + hh)])
                                ww = wt[g_ % 3]
                                op("act", lambda e: e.activation(out=ww[:, :, 0:w], in_=zview(g_, w), func=AF.Exp),
                                   reads=[("ps", zb), ("ps", zb + 1)], writes=[("aw", g_ % 3)])
                                if it["diag"]:
                                    for hh in range(2):
                                        op("dve", lambda e: e.tensor_tensor(out=ww[:, hh, 0:128], in0=ww[:, hh, 0:128], in1=mstrict, op=ALU.mult),
                                           reads=[("aw", g_ % 3), ("cb", None)], writes=[("aw", g_ % 3)])

                            def stE(g_):
                                it = items[g_]
                                if it["kind"] == "proj":
                                    return
                                hp, qc, i, n, kb, lo, w = it["hp"], it["qc"], it["i"], it["n"], it["kb"], it["lo"], it["w"]
                                ob = OB[it["gi"] % 2]
                                po, pok = PS[ob], ("ps", ob)
                                ww = wt[g_ % 3]
                                for hh in range(2):
                                    hb = slice(64 * hh, 64 * hh + 64)
                                    h = 2 * hp + hh
                                    op("pe", lambda e: e.matmul(po[hb, lo:512], lhsT=va[:, kb, h * 64:(h + 1) * 64], rhs=ww[:, hh, 0:w],
                                                                start=(i == 0), stop=(i == n - 1), skip_group_check=True),
                                       reads=[("aw", g_ % 3), ("va", kb)], writes=[pok])
                                if i == n - 1:
                                    evac(it["gi"], ya[:, hp, qc * 512:(qc + 1) * 512], po[:, :], [pok], [("ya", (hp, qc))])

                            while tstate["n"] < 3:
                                tables_step()
                            for t in range(NI + 2):
                                if t < NI:
                                    stA(t)
                                if 0 <= t - 1 < NI:
                                    stC(t - 1)
                                if 0 <= t - 2 < NI:
                                    stE(t - 2)
                        sch.barrier()

                if tstate["n"] < 3:
                    with contextlib.ExitStack() as est:
                        if "tP" not in tabs or "a" not in parts:
                            tables_alloc(est)
                        tables_finish()
                        sch.barrier()
                if "b" in parts:
                    if True:
                        yb = sb(es, "yb", [128, 2, S], BF16)
                        with contextlib.ExitStack() as es3:
                            vb = [sb(es3, "vb%d" % g, [128, 16, 256], BF16) for g in range(3)]
                            qbt = sb(es3, "bq", [128, S], BF16)
                            kbt = sb(es3, "bk", [128, S], BF16)
                            rt = [(sb(es3, "ra%d" % i, [128, 512], F32), sb(es3, "rb%d" % i, [128, 512], F32)) for i in range(2)]
                            pt = [sb(es3, "bp%d" % i, [128, 2, 256], BF16) for i in range(3)]
                            snum = sb(es3, "snum", [128, S], F32)
                            sden = sb(es3, "sden", [128, S], F32)
                            wb, wk = wload(l, "b_v")
                            for g, d in enumerate((1, 4, 16)):
                                Lg = S // d
                                nbg = Lg // 128

                                def tokfn(blk, d=d, nbg=nbg):
                                    c, b = blk // nbg, blk % nbg
                                    st = c + 128 * b * d
                                    return slice(st, st + 127 * d + 1, d) if d > 1 else slice(st, st + 128)

                                proj_v(wb, wk, 768, 256 * g, 256, vb[g], "vb%d" % g, stride=d, tokfn=tokfn)
                            wbs = [wload(l, "b_qk%d" % j) for j in range(3)]
                            SS = [(PS[0], ("ps", 0)), (PS[1], ("ps", 1))]
                            for pi in range(2):
                                for g, d in enumerate((1, 4, 16)):
                                    wb, wk = wbs[g]
                                    Lg = S // d
                                    nbg = Lg // 128
                                    for which, dstT, dkey in ((0, qbt, "bq"), (1, kbt, "bk")):
                                        for tt in range(4):
                                            pP, pPk = psnext()
                                            pS, pSk = psnext()
                                            for kc in range(8):
                                                for (pp, ppk, cidx) in ((pP, pPk, pi * 4 + which * 2), (pS, pSk, pi * 4 + which * 2 + 1)):
                                                    op("pe", lambda e, kc=kc, tt=tt, pp=pp, cidx=cidx, wb=wb: e.matmul(
                                                        pp[:], lhsT=wb[:, kc * 1024 + cidx * 128:kc * 1024 + (cidx + 1) * 128],
                                                        rhs=ht[:, kc, tt * 512:(tt + 1) * 512], start=(kc == 0), stop=(kc == 7)),
                                                       reads=[wk, ("mht", (kc, tt))], writes=[ppk])
                                            if d == 1:
                                                dst = dstT[:, tt * 512:(tt + 1) * 512]
                                                rope_evac(pP, pPk, pS, pSk, dst, (dkey, None), tt, rt)
                                            else:
                                                m = 512 // d
                                                dst = dstT[:].rearrange("p (c l) -> p c l", c=d)[:, :, tt * m:(tt + 1) * m]
                                                rope_evac(pP, pPk, pS, pSk, dst, (dkey, None), tt, rt, perm=d)
                                    jobs = []
                                    gi = 0
                                    for c in range(d):
                                        for QT in range((Lg + 511) // 512):
                                            wqt = min(512, Lg - 512 * QT)
                                            kbs = list(range(max(0, 4 * QT - 1), min(4 * QT + 3, nbg - 1) + 1))
                                            for ji, kb in enumerate(kbs):
                                                lo = max(128 * kb, 512 * QT) - 512 * QT
                                                hi = min(128 * kb + 256, 512 * QT + wqt) - 512 * QT
                                                mc0 = (512 * QT + lo) - 128 * kb
                                                jobs.append(dict(c=c, QT=QT, kb=kb, lo=lo, hi=hi, mc0=mc0, first=(ji == 0),
                                                                 last=(ji == len(kbs) - 1), gi=gi, wqt=wqt))
                                            gi += 1
                                    nj = len(jobs)

                                    def s0(i):
                                        J = jobs[i]
                                        w = J["hi"] - J["lo"]
                                        zb = 2 * (i % 2)
                                        kc0 = J["c"] * Lg + 128 * J["kb"]
                                        base = J["c"] * Lg + 512 * J["QT"]
                                        for hh in range(2):
                                            hb = slice(64 * hh, 64 * hh + 64)
                                            op("pe", lambda e: e.matmul(PS[zb + hh][:, 0:w], lhsT=kbt[hb, kc0:kc0 + 128],
                                                                        rhs=qbt[hb, base + J["lo"]:base + J["hi"]], start=True, stop=True),
                                               reads=[("bk", None), ("bq", None)], writes=[("ps", zb + hh)])
                                        p = pt[i % 3]
                                        zv = pst[:, zb * 512:(zb + 2) * 512].rearrange("p (h c) -> p h c", h=2)[:, :, 0:w]
                                        op("act", lambda e: e.activation(out=p[:, :, 0:w], in_=zv, func=AF.Exp, scale=0.125),
                                           reads=[("ps", zb), ("ps", zb + 1)], writes=[("bp", i % 3)])
                                        op("dve", lambda e: e.tensor_tensor(out=p[:, :, 0:w], in0=p[:, :, 0:w], in1=mband2[:, :, J["mc0"]:J["mc0"] + w], op=ALU.mult),
                                           reads=[("bp", i % 3), ("cb", None)], writes=[("bp", i % 3)])

                                    def s1(i):
                                        J = jobs[i]
                                        w = J["hi"] - J["lo"]
                                        lo, hi = J["lo"], J["hi"]
                                        p = pt[i % 3]
                                        blk = J["c"] * nbg + J["kb"]
                                        nb_ = 4 + (J["gi"] % 2)
                                        db_ = 6 + (J["gi"] % 2)
                                        pn, pd = PS[nb_], PS[db_]
                                        for hh in range(2):
                                            hb = slice(64 * hh, 64 * hh + 64)
                                            hloc = 2 * pi + hh
                                            op("pe", lambda e: e.matmul(pn[hb, lo:hi], lhsT=vb[g][:, blk, hloc * 64:(hloc + 1) * 64], rhs=p[:, hh, 0:w],
                                                                        start=J["first"], stop=J["last"], skip_group_check=True),
                                               reads=[("bp", i % 3), ("vb%d" % g, blk)], writes=[("ps", nb_)])
                                            op("pe", lambda e: e.matmul(pd[hb, lo:hi], lhsT=ones[:, 0:64], rhs=p[:, hh, 0:w],
                                                                        start=J["first"], stop=J["last"], skip_group_check=True),
                                               reads=[("bp", i % 3), ("cb", None)], writes=[("ps", db_)])
                                        if J["last"]:
                                            wqt = J["wqt"]
                                            n0 = 512 * J["QT"] * d + J["c"]
                                            nsl = slice(n0, n0 + wqt) if d == 1 else slice(n0, n0 + (wqt - 1) * d + 1, d)
                                            if g == 0:
                                                op("act", lambda e: e.copy(out=snum[:, nsl], in_=pn[:, 0:wqt]), reads=[("ps", nb_)], writes=[("snum", None)])
                                                op("dve", lambda e: e.tensor_copy(out=sden[:, nsl], in_=pd[:, 0:wqt]), reads=[("ps", db_)], writes=[("sden", None)])
                                            else:
                                                op("dve", lambda e: e.tensor_tensor(out=snum[:, nsl], in0=pn[:, 0:wqt], in1=snum[:, nsl], op=ALU.add),
                                                   reads=[("ps", nb_), ("snum", None)], writes=[("snum", None)])
                                                op("dve", lambda e: e.tensor_tensor(out=sden[:, nsl], in0=pd[:, 0:wqt], in1=sden[:, nsl], op=ALU.add),
                                                   reads=[("ps", db_), ("sden", None)], writes=[("sden", None)])

                                    for t in range(nj + 2):
                                        if t < nj:
                                            s0(t)
                                        if t >= 2:
                                            s1(t - 2)
                                op("act", lambda e: e.activation(out=sden[:], in_=sden[:], func=AF.Ln), reads=[("sden", None)], writes=[("sden", None)])
                                op("act", lambda e: e.activation(out=sden[:], in_=sden[:], func=AF.Exp, scale=-1.0), reads=[("sden", None)], writes=[("sden", None)])
                                op("dve", lambda e, pi=pi: e.tensor_tensor(out=yb[:, pi, :], in0=snum[:], in1=sden[:], op=ALU.mult),
                                   reads=[("sden", None), ("snum", None)], writes=[("yb", pi)])
                        sch.barrier()

                if "c" in parts:
                    if True:
                        yc = sb(es, "yc", [128, 4, S], BF16)
                        with contextlib.ExitStack() as es3:
                            vc = sb(es3, "vc", [128, 16, 512], BF16)
                            qk = [sb(es3, "cqk%d" % i, [128, S], BF16) for i in range(4)]
                            rt = [(sb(es3, "ra%d" % i, [128, 512], F32), sb(es3, "rb%d" % i, [128, 512], F32)) for i in range(2)]
                            pt = [sb(es3, "cp%d" % i, [128, 2, 512], BF16) for i in range(3)]
                            rr = [sb(es3, "crr%d" % i, [128, 512], F32) for i in range(2)]
                            oo = [sb(es3, "coo%d" % i, [128, 512], F32) for i in range(3)]
                            osq = sb(es3, "cosq", [128, 512], BF16)
                            wb, wk = wload(l, "c_v")
                            proj_v(wb, wk, 512, 0, 512, vc, "vc", tokfn=lambda blk: slice(blk * 128, (blk + 1) * 128))
                            PO = [(PS[0], ("ps", 0)), (PS[1], ("ps", 1))]
                            PD = [(PS[2], ("ps", 2)), (PS[3], ("ps", 3))]
                            wbs = {}
                            items = []
                            for hp in range(2):
                                for m4 in range(4):
                                    for tt in range(4):
                                        items.append(dict(kind="proj", hp=hp, m4=m4, tt=tt))
                                for hh in range(2):
                                    for qc in range(4):
                                        nsteps = 4 * qc + 4
                                        for kb in range(nsteps):
                                            lo = 128 * (kb - 4 * qc) if kb >= 4 * qc else 0
                                            items.append(dict(kind="step", hp=hp, hh=hh, qc=qc, kb=kb, lo=lo, w=512 - lo,
                                                              first=(kb == 0), last=(kb == nsteps - 1), diag=(kb >= 4 * qc)))
                                        items.append(dict(kind="post", hp=hp, hh=hh, qc=qc))
                            NI = len(items)
                            deferred = {}

                            def c0(g_):
                                it = items[g_]
                                zb = 4 + 2 * (g_ % 2)
                                if it["kind"] == "proj":
                                    hp, m4, tt = it["hp"], it["m4"], it["tt"]
                                    if hp not in wbs:
                                        wbs[hp] = wload(l, "c_qk%d" % hp)
                                    wb, wk = wbs[hp]
                                    pP, pPk = PS[zb], ("ps", zb)
                                    pS, pSk = PS[zb + 1], ("ps", zb + 1)
                                    for kc in range(8):
                                        for (pp, ppk, cidx) in ((pP, pPk, 2 * m4), (pS, pSk, 2 * m4 + 1)):
                                            op("pe", lambda e: e.matmul(
                                                pp[:], lhsT=wb[:, kc * 1024 + cidx * 128:kc * 1024 + (cidx + 1) * 128],
                                                rhs=ht[:, kc, tt * 512:(tt + 1) * 512], start=(kc == 0), stop=(kc == 7)),
                                               reads=[wk, ("mht", None)], writes=[ppk])
                                    rope_evac(pP, pPk, pS, pSk, qk[m4][:, tt * 512:(tt + 1) * 512], ("cqk", m4), tt, rt)
                                    return
                                if it["kind"] == "post":
                                    return
                                hb = slice(64 * it["hh"], 64 * it["hh"] + 64)
                                qc, kb, lo, w = it["qc"], it["kb"], it["lo"], it["w"]
                                p = pt[g_ % 3]
                                for m in range(2):
                                    op("pe", lambda e: e.matmul(PS[zb + m][:, 0:w], lhsT=qk[2 + m][hb, kb * 128:(kb + 1) * 128],
                                                                rhs=qk[m][hb, qc * 512 + lo:qc * 512 + 512], start=True, stop=True),
                                       reads=[("cqk", 2 + m), ("cqk", m)], writes=[("ps", zb + m)])
                                zv = pst[:, zb * 512:(zb + 2) * 512].rearrange("p (h c) -> p h c", h=2)[:, :, 0:w]
                                op("act", lambda e: e.activation(out=p[:, :, 0:w], in_=zv, func=AF.Exp, scale=0.125),
                                   reads=[("ps", zb), ("ps", zb + 1)], writes=[("cp", g_ % 3)])
                                if it["diag"]:
                                    for m in range(2):
                                        op("dve", lambda e: e.tensor_tensor(out=p[:, m, 0:128], in0=p[:, m, 0:128], in1=mincl, op=ALU.mult),
                                           reads=[("cp", g_ % 3), ("cb", None)], writes=[("cp", g_ % 3)])

                            def c1(g_):
                                it = items[g_]
                                if it["kind"] == "proj":
                                    return
                                h = 2 * it["hp"] + it["hh"]
                                qc = it["qc"]
                                if it["kind"] == "step":
                                    kb, lo, w = it["kb"], it["lo"], it["w"]
                                    p = pt[g_ % 3]
                                    for m in range(2):
                                        po, pok = PO[m]
                                        pd, pdk = PD[m]
                                        op("pe", lambda e: e.matmul(po[:, lo:512], lhsT=vc[:, kb, h * 128:(h + 1) * 128], rhs=p[:, m, 0:w],
                                                                    start=it["first"], stop=it["last"]),
                                           reads=[("cp", g_ % 3), ("vc", kb)], writes=[pok])
                                        op("pe", lambda e: e.matmul(pd[:, lo:512], lhsT=ones, rhs=p[:, m, 0:w],
                                                                    start=it["first"], stop=it["last"]),
                                           reads=[("cp", g_ % 3), ("cb", None)], writes=[pdk])
                                    return
                                for m in range(2):
                                    pd, pdk = PD[m]
                                    op("act", lambda e: e.activation(out=rr[m][:], in_=pd[:], func=AF.Ln), reads=[pdk], writes=[("crr", m)])
                                for m in range(2):
                                    po, pok = PO[m]
                                    op("dve", lambda e: e.tensor_copy(out=oo[m][:], in_=po[:]), reads=[pok], writes=[("coo", m)])
                                for m in range(2):
                                    op("act", lambda e: e.activation(out=rr[m][:], in_=rr[m][:], func=AF.Exp, scale=-1.0),
                                       reads=[("crr", m)], writes=[("crr", m)])
                                op("dve", lambda e: e.tensor_tensor(out=oo[1][:], in0=oo[1][:], in1=rr[1][:], op=ALU.mult),
                                   reads=[("coo", 1), ("crr", 1)], writes=[("coo", 1)])
                                op("dve", lambda e: e.tensor_tensor(out=oo[0][:], in0=oo[0][:], in1=rr[0][:], op=ALU.mult),
                                   reads=[("coo", 0), ("crr", 0)], writes=[("coo", 0)])
                                op("dve", lambda e: e.scalar_tensor_tensor(out=oo[2][:], in0=oo[1][:], scalar=lamv[:, 0:1], in1=oo[0][:],
                                                                            op0=ALU.mult, op1=ALU.add),
                                   reads=[("coo", 0), ("coo", 1), ("lamv", None)], writes=[("coo", 2)])
                                op("dve", lambda e: e.tensor_tensor(out=osq[:], in0=oo[2][:], in1=oo[2][:], op=ALU.mult),
                                   reads=[("coo", 2)], writes=[("cosq", None)])
                                def post2(g2, h=h, qc=qc):
                                    zb = 4 + 2 * (g2 % 2)
                                    pss, pssk = PS[zb], ("ps", zb)
                                    op("pe", lambda e: e.matmul(pss[:], lhsT=ones, rhs=osq[:], start=True, stop=True),
                                       reads=[("cosq", None), ("cb", None)], writes=[pssk])
                                    op("act", lambda e: e.activation(out=oo[0][:], in_=pss[:], func=AF.Ln, scale=1.0 / 128, bias=1e-5),
                                       reads=[pssk], writes=[("coo", 0)])
                                    op("act", lambda e: e.activation(out=oo[0][:], in_=oo[0][:], func=AF.Exp, scale=-0.5),
                                       reads=[("coo", 0)], writes=[("coo", 0)])
                                    op("dve", lambda e: e.scalar_tensor_tensor(out=yc[:, h, qc * 512:(qc + 1) * 512], in0=oo[2][:], scalar=lamv[:, 1:2],
                                                                                in1=oo[0][:], op0=ALU.mult, op1=ALU.mult),
                                       reads=[("coo", 2), ("coo", 0), ("lamv", None)], writes=[("yc", (h, qc))])

                                deferred[g_ + 3] = post2

                            for t in range(NI + 6):
                                if t < NI:
                                    c0(t)
                                if 2 <= t < NI + 2:
                                    c1(t - 2)
                                if (t - 2) in deferred:
                                    deferred.pop(t - 2)(t - 2)
                            assert not deferred
                        sch.barrier()
                branches = []
                if "a" in parts:
                    branches.append(("a", [ya[:, j, :] for j in range(4)], ("ya", None)))
                if "b" in parts:
                    branches.append(("b", [yb[:, j, :] for j in range(2)], ("yb", None)))
                if "c" in parts:
                    branches.append(("c", [yc[:, j, :] for j in range(4)], ("yc", None)))
                merge_all(branches)
            sch.barrier()

        prog = []
        for l in range(DEPTH):
            prog.append(("ffn", l, 1))
            for s in range(NSEQ):
                prog.append(("mix", l, s))
            prog.append(("ffn", l, 2))
        if stop_after is not None:
            prog = prog[:stop_after]
        src = xin
        for pi, item in enumerate(prog):
            last = pi == len(prog) - 1
            if item[0] == "ffn":
                ffn_phase(item[1], item[2], src, final=last, emit_hm=(item[2] == 1 and not last))
                src = xa
            else:
                mixer_phase(item[1], item[2], parts=MIX_PARTS)
                if last:
                    for kc in range(8):
                        op("sp", lambda e, kc=kc: e.dma_start(out=outd.ap()[kc * 128:(kc + 1) * 128, :], in_=xa.ap()[kc * 128:(kc + 1) * 128, :]),
                           reads=[("xdram", None)], writes=[("odram", None)], dma=True)
        sch.emit()
    return nc


MIX_PARTS = "abc"
_CACHE = {}


def kernel(**inputs):
    inp = {k: np.asarray(v) for k, v in inputs.items()}
    B = inp["x"].shape[0]
    nseq = B // NCORES
    depth = inp["w_in"].shape[0]
    key = (nseq, depth)
    if key not in _CACHE:
        _CACHE[key] = build_program(NSEQ=nseq, DEPTH=depth)
    nc = _CACHE[key]
    consts = _build_consts()
    wimgs = [_build_wimg(inp, l) for l in range(depth)]
    vecs = [_build_vecs(inp, l) for l in range(depth)]
    in_maps = []
    for c in range(NCORES):
        xb = inp["x"][c * nseq:(c + 1) * nseq]
        xT = np.ascontiguousarray(xb.reshape(nseq * S, D).T)
        pos = inp["positions"][c * nseq:(c + 1) * nseq].astype(np.int32)
        posb = np.ascontiguousarray(np.broadcast_to(pos[:, None, :], (nseq, 128, S)))
        m = {"xT": xT, "pos": posb, "consts": consts}
        for l in range(depth):
            m["wimg%d" % l] = wimgs[l]
            m["vecs%d" % l] = vecs[l]
        in_maps.append(m)
    res = run_bass_kernel_spmd(nc, in_maps, core_ids=list(range(NCORES)))
    outs = [np.asarray(r["outT"]).T.reshape(nseq, S, D) for r in res.results]
    return np.ascontiguousarray(np.concatenate(outs, axis=0).astype(np.float32))
```
